# Optimizing a Trainium2 kernel written in Bass

```python
import jax, jax.numpy as jnp
from jax import lax
import numpy as np

D_MODEL = 1024
BATCH = 8
SEQ = 2048
DEPTH = 1

GRID_W = 64
CTX_LEN = 256
CONV_DIM = 512
CONV_K = 31
GLA_HEADS = 4
GLA_DK = 64
GLA_DV = 128
DECAY_RANK = 16
GATE_NORM = 16.0
CHUNK = 64
D_FF = 4 * D_MODEL
N_MOD = 6
EPS = 1e-6

COL_GLU = 0
COL_Q = COL_GLU + 2 * CONV_DIM
COL_K = COL_Q + GLA_HEADS * GLA_DK
COL_V = COL_K + GLA_HEADS * GLA_DK
COL_R = COL_V + GLA_HEADS * GLA_DV
COL_DEC = COL_R + GLA_HEADS * GLA_DV
COL_GATE = COL_DEC + 2 * DECAY_RANK
COL_END = COL_GATE + 2 * D_MODEL

kernel_name = "hybrid_conformer_gla_dit_block"


def rmsnorm(x, g):
    x32 = x.astype(jnp.float32)
    y = x32 * lax.rsqrt(jnp.mean(x32 * x32, axis=-1, keepdims=True) + EPS)
    return (y * g).astype(x.dtype)


def layernorm(x, g, b):
    x32 = x.astype(jnp.float32)
    mu = jnp.mean(x32, axis=-1, keepdims=True)
    var = jnp.mean(jnp.square(x32 - mu), axis=-1, keepdims=True)
    y = (x32 - mu) * lax.rsqrt(var + EPS)
    return (y * g + b).astype(x.dtype)


def adaln(cvec, w_mod, b_mod, n):
    m = jax.nn.silu(cvec) @ w_mod[:, :n * D_MODEL] + b_mod[:n * D_MODEL]
    return jnp.split(m[:, None, :], n, axis=-1)


def modulate(xn, shift, scale):
    return xn * (1.0 + scale) + shift


def heads(a, d):
    B_, T, _ = a.shape
    return a.reshape(B_, T, -1, d).transpose(0, 2, 1, 3)


def flip_t(a):
    return jnp.flip(a, axis=2)


def dwconv1d(x, w, b):
    y = lax.conv_general_dilated(
        x, w[:, None, :], window_strides=(1,),
        padding=[(CONV_K // 2, CONV_K // 2)],
        dimension_numbers=("NWC", "WIO", "NWC"),
        feature_group_count=x.shape[-1])
    return y + b


def conv_grid(a, w, b, rows):
    B_, T, C = a.shape
    half = C // 2
    g = a.reshape(B_, rows, GRID_W, C)
    ah = g[..., :half].reshape(B_ * rows, GRID_W, half)
    yh = dwconv1d(ah, w[:, :half], b[:half]).reshape(B_, rows, GRID_W, half)
    av = g[..., half:].transpose(0, 2, 1, 3).reshape(B_ * GRID_W, rows, half)
    yv = dwconv1d(av, w[:, half:], b[half:]).reshape(B_, GRID_W, rows, half).transpose(0, 2, 1, 3)
    return jnp.concatenate([yh, yv], axis=-1).reshape(B_, T, C)


def decay_logs(z, w_decay, b_decay):
    B_, T, _ = z.shape
    z = z.reshape(B_, T, 2, DECAY_RANK)
    logits = jnp.einsum("btdr,drk->btdk", z, w_decay) + b_decay
    la = jax.nn.log_sigmoid(logits.astype(jnp.float32)) / GATE_NORM
    return heads(la[:, :, 0], GLA_DK), heads(la[:, :, 1], GLA_DK)


def gla_scan(q, k, v, la, S0):
    B_, H, T, _ = q.shape
    n = T // CHUNK

    def chunks(a):
        return a.reshape(B_, H, n, CHUNK, a.shape[-1]).transpose(2, 0, 1, 3, 4)

    lower = jnp.tril(jnp.ones((CHUNK, CHUNK), dtype=bool))[:, :, None]

    def step(S, inp):
        qc, kc, vc, lac = inp
        G = jnp.cumsum(lac, axis=2)
        diff = G[:, :, :, None, :] - G[:, :, None, :, :]
        dec = jnp.exp(jnp.where(lower, diff, -jnp.inf))
        A = jnp.einsum("bhid,bhjd,bhijd->bhij", qc, kc, dec)
        o = (jnp.einsum("bhij,bhjv->bhiv", A, vc)
             + jnp.einsum("bhid,bhdv->bhiv", qc * jnp.exp(G), S))
        G_last = G[:, :, -1:, :]
        S_new = (jnp.exp(G_last[:, :, 0, :, None]) * S
                 + jnp.einsum("bhjd,bhjv->bhdv", kc * jnp.exp(G_last - G), vc))
        return S_new, o

    S_fin, o = lax.scan(step, S0, (chunks(q), chunks(k), chunks(v), chunks(la)))
    o = o.transpose(1, 2, 0, 3, 4).reshape(B_, H, T, v.shape[-1])
    return o, S_fin


def gla_final_state(k, v, la):
    G = jnp.cumsum(la, axis=2)
    return jnp.einsum("bhtd,bhtv->bhdv", k * jnp.exp(G[:, :, -1:] - G), v)


def context_states(uc, w_in, w_decay, b_decay):
    kv = uc @ w_in[:, COL_K:COL_R]
    k_c = heads(kv[..., :GLA_HEADS * GLA_DK], GLA_DK)
    v_c = heads(kv[..., GLA_HEADS * GLA_DK:], GLA_DV)
    la_f, la_b = decay_logs(uc @ w_in[:, COL_DEC:COL_GATE], w_decay, b_decay)
    S_f = gla_final_state(k_c, v_c, la_f)
    S_b = gla_final_state(flip_t(k_c), flip_t(v_c), flip_t(la_b))
    return S_f, S_b


def mixer(u, rows, S_f0, S_b0, w_in, conv_w, conv_b, conv_ln_g, conv_ln_b,
          w_conv_out, w_decay, b_decay, gla_norm_g, w_gla_out, w_out):
    B_, T, _ = u.shape
    proj = u @ w_in
    glu_in = proj[..., COL_GLU:COL_Q]
    a = glu_in[..., :CONV_DIM] * jax.nn.sigmoid(glu_in[..., CONV_DIM:])
    a = conv_grid(a, conv_w, conv_b, rows) if rows is not None else dwconv1d(a, conv_w, conv_b)
    y_conv = jax.nn.silu(layernorm(a, conv_ln_g, conv_ln_b)) @ w_conv_out
    q = heads(proj[..., COL_Q:COL_K], GLA_DK) * (GLA_DK ** -0.5)
    k = heads(proj[..., COL_K:COL_V], GLA_DK)
    v = heads(proj[..., COL_V:COL_R], GLA_DV)
    r = proj[..., COL_R:COL_DEC]
    la_f, la_b = decay_logs(proj[..., COL_DEC:COL_GATE], w_decay, b_decay)
    o_f, _ = gla_scan(q, k, v, la_f, S_f0)
    o_b, _ = gla_scan(flip_t(q), flip_t(k), flip_t(v), flip_t(la_b), S_b0)
    o = rmsnorm(o_f + flip_t(o_b), gla_norm_g)
    o = o.transpose(0, 2, 1, 3).reshape(B_, T, GLA_HEADS * GLA_DV).astype(u.dtype)
    y_gla = (o * jax.nn.silu(r)) @ w_gla_out
    gates = jax.nn.sigmoid(proj[..., COL_GATE:COL_END])
    merged = gates[..., :D_MODEL] * y_conv + gates[..., D_MODEL:] * y_gla
    return merged @ w_out


def sq_relu_mlp(u, w_ff1, w_ff2):
    return jnp.square(jax.nn.relu(u @ w_ff1)) @ w_ff2


def setup_inputs(seed: int = 0) -> dict:
    key = jax.random.key(seed)
    ks = jax.random.split(key, 24)
    D = D_MODEL
    f32 = jnp.float32

    def nrm(k, shape, scale):
        return jax.random.normal(k, shape, f32) * scale

    return {
        "x": nrm(ks[0], (BATCH, SEQ, D), 1.0),
        "c": nrm(ks[1], (BATCH, D), 1.0),
        "ctx": nrm(ks[2], (BATCH, CTX_LEN, D), 1.0),
        "c_ctx": nrm(ks[3], (D,), 1.0),
        "w_mod": nrm(ks[4], (DEPTH, D, N_MOD * D), 0.5 * D ** -0.5),
        "b_mod": nrm(ks[5], (DEPTH, N_MOD * D), 0.01),
        "g_pre1": 1.0 + nrm(ks[6], (DEPTH, D), 0.05),
        "g_post1": 1.0 + nrm(ks[7], (DEPTH, D), 0.05),
        "g_pre2": 1.0 + nrm(ks[8], (DEPTH, D), 0.05),
        "g_post2": 1.0 + nrm(ks[9], (DEPTH, D), 0.05),
        "w_in": nrm(ks[10], (DEPTH, D, COL_END), D ** -0.5),
        "conv_w": nrm(ks[11], (DEPTH, CONV_K, CONV_DIM), CONV_K ** -0.5),
        "conv_b": nrm(ks[12], (DEPTH, CONV_DIM), 0.01),
        "conv_ln_g": 1.0 + nrm(ks[13], (DEPTH, CONV_DIM), 0.05),
        "conv_ln_b": nrm(ks[14], (DEPTH, CONV_DIM), 0.01),
        "w_conv_out": nrm(ks[15], (DEPTH, CONV_DIM, D), CONV_DIM ** -0.5),
        "w_decay": nrm(ks[16], (DEPTH, 2, DECAY_RANK, GLA_HEADS * GLA_DK), DECAY_RANK ** -0.5),
        "b_decay": nrm(ks[17], (DEPTH, 2, GLA_HEADS * GLA_DK), 0.5),
        "gla_norm_g": 1.0 + nrm(ks[18], (DEPTH, GLA_DV), 0.05),
        "w_gla_out": nrm(ks[19], (DEPTH, GLA_HEADS * GLA_DV, D), (GLA_HEADS * GLA_DV) ** -0.5),
        "w_out": nrm(ks[20], (DEPTH, D, D), D ** -0.5),
        "w_ff1": nrm(ks[21], (DEPTH, D, D_FF), D ** -0.5),
        "w_ff2": nrm(ks[22], (DEPTH, D_FF, D), D_FF ** -0.5),
    }


def reference(x, c, ctx, c_ctx, w_mod, b_mod, g_pre1, g_post1, g_pre2, g_post2,
              w_in, conv_w, conv_b, conv_ln_g, conv_ln_b, w_conv_out, w_decay,
              b_decay, gla_norm_g, w_gla_out, w_out, w_ff1, w_ff2):
    rows = x.shape[1] // GRID_W
    h = x
    hc = ctx
    cc = c_ctx[None, :]
    for l in range(DEPTH):
        mp = (w_in[l], conv_w[l], conv_b[l], conv_ln_g[l], conv_ln_b[l], w_conv_out[l],
              w_decay[l], b_decay[l], gla_norm_g[l], w_gla_out[l], w_out[l])
        sh1, sc1, gt1, sh2, sc2, gt2 = adaln(c, w_mod[l], b_mod[l], N_MOD)
        csh1, csc1 = adaln(cc, w_mod[l], b_mod[l], 2)
        uc = modulate(rmsnorm(hc, g_pre1[l]), csh1, csc1)
        S_f, S_b = context_states(uc, w_in[l], w_decay[l], b_decay[l])
        u = modulate(rmsnorm(h, g_pre1[l]), sh1, sc1)
        y = mixer(u, rows, S_f, S_b, *mp)
        h_mid = h + gt1 * rmsnorm(y, g_post1[l])
        u2 = modulate(rmsnorm(h_mid, g_pre2[l]), sh2, sc2)
        h_new = h_mid + gt2 * rmsnorm(sq_relu_mlp(u2, w_ff1[l], w_ff2[l]), g_post2[l])
        if l < DEPTH - 1:
            _, _, cgt1, csh2, csc2, cgt2 = adaln(cc, w_mod[l], b_mod[l], N_MOD)
            zeros = jnp.zeros((hc.shape[0], GLA_HEADS, GLA_DK, GLA_DV), jnp.float32)
            yc = mixer(uc, None, zeros, zeros, *mp)
            hc_mid = hc + cgt1 * rmsnorm(yc, g_post1[l])
            uc2 = modulate(rmsnorm(hc_mid, g_pre2[l]), csh2, csc2)
            hc = hc_mid + cgt2 * rmsnorm(sq_relu_mlp(uc2, w_ff1[l], w_ff2[l]), g_post2[l])
        h = h_new
    return h
```

```python
import contextlib
import math
import numpy as np
import concourse.bass as bass
import concourse.mybir as mybir
from concourse.bass_utils import run_bass_kernel_spmd

F32 = mybir.dt.float32
BF16 = mybir.dt.bfloat16
AF = mybir.ActivationFunctionType
ALU = mybir.AluOpType

COMPUTE = ("pe", "act", "dve", "pool")
QUEUES = ("pe", "act", "dve", "pool", "sp")


class Res:
    __slots__ = ("name", "w", "r")

    def __init__(self, name=""):
        self.name = name
        self.w = None
        self.r = []


class DSem:
    __slots__ = ("sem", "count", "key", "last")

    def __init__(self, key):
        self.sem = None
        self.count = 0
        self.key = key
        self.last = None


class Ev:
    __slots__ = ("key", "val", "clock", "op", "dsem")


class Op:
    __slots__ = ("eng", "fn", "idx", "sig", "sigval", "waits", "ev", "dsem")


class Prog:
    def __init__(self, nc):
        self.nc = nc
        self.ops = {e: [] for e in QUEUES}
        self.clock = {e: {} for e in QUEUES}
        self.dsems = []
        self.last = {}

    def dsem(self, name):
        d = DSem("D%d:%s" % (len(self.dsems), name))
        self.dsems.append((name, d))
        return d

    @staticmethod
    def _gather(reads, writes):
        deps = []
        for r in reads:
            if r.w is not None:
                deps.append(r.w)
        for w in writes:
            if w.w is not None:
                deps.append(w.w)
            deps.extend(w.r)
        return deps

    def _apply_waits(self, eng, op, deps):
        clk = self.clock[eng]
        need = {}
        for d in deps:
            same = d.op is not None and d.op.eng == eng
            if same and eng in ("pe", "sp"):
                continue
            if clk.get(d.key, -1) >= d.val:
                continue
            cur = need.get(d.key)
            if cur is None or d.val > cur.val:
                need[d.key] = d
        for d in need.values():
            if d.op is not None:
                d.op.sig = True
        for d in deps:
            for k, v in d.clock.items():
                if clk.get(k, -1) < v:
                    clk[k] = v
        op.waits = list(need.values())

    def _new(self, eng, fn):
        o = Op()
        o.eng = eng
        o.fn = fn
        o.idx = len(self.ops[eng])
        o.sig = False
        o.sigval = None
        o.dsem = None
        o.waits = []
        return o

    def op(self, eng, fn, reads=(), writes=()):
        o = self._new(eng, fn)
        self._apply_waits(eng, o, self._gather(reads, writes))
        ev = Ev()
        ev.key = eng
        ev.val = o.idx
        ev.op = o
        ev.dsem = None
        clk = self.clock[eng]
        if eng in ("pe", "sp"):
            clk[eng] = o.idx
        ev.clock = dict(clk)
        ev.clock[eng] = o.idx
        o.ev = ev
        for r in reads:
            r.r.append(ev)
        for w in writes:
            w.w = ev
            w.r = []
        self.ops[eng].append(o)
        self.last[eng] = ev
        return ev

    def dma(self, queue, dsem, fn, reads=(), writes=()):
        o = self._new(queue, fn)
        o.dsem = dsem
        self._apply_waits(queue, o, self._gather(reads, writes))
        dsem.count += 16
        ev = Ev()
        ev.key = dsem.key
        ev.val = dsem.count
        ev.op = None
        ev.dsem = dsem
        ev.clock = dict(self.clock[queue])
        ev.clock[dsem.key] = dsem.count
        o.ev = ev
        dsem.last = ev
        for r in reads:
            r.r.append(ev)
        for w in writes:
            w.w = ev
            w.r = []
        self.ops[queue].append(o)
        return ev

    def wait_all(self, eng, evs):
        o = self._new(eng, None)
        self._apply_waits(eng, o, list(evs))
        self.ops[eng].append(o)

    def barrier(self):
        evs = [self.last[e] for e in COMPUTE if e in self.last]
        evs += [d.last for _, d in self.dsems if d.last is not None]
        for e in QUEUES:
            self.wait_all(e, evs)

    def emit(self, stack):
        nc = self.nc
        esem = {}
        for e in COMPUTE:
            esem[e] = stack.enter_context(nc.semaphore("s_" + e))
        for name, d in self.dsems:
            d.sem = stack.enter_context(nc.semaphore("d_" + name))
        for e in QUEUES:
            c = 0
            for o in self.ops[e]:
                if o.sig:
                    c += 1
                    o.sigval = c
        self.stats = {e: (len(self.ops[e]), sum(1 for o in self.ops[e] if o.sig),
                          sum(len(o.waits) for o in self.ops[e])) for e in QUEUES}

        def run(e, engobj):
            for o in self.ops[e]:
                for d in o.waits:
                    if d.op is not None:
                        engobj.wait_ge(esem[d.op.eng], d.op.sigval)
                    else:
                        engobj.wait_ge(d.dsem.sem, d.val)
                if o.fn is None:
                    continue
                inst = o.fn(engobj)
                if o.dsem is not None:
                    inst.then_inc(o.dsem.sem, 16)
                elif o.sig:
                    inst.then_inc(esem[e], 1)

        block = stack.enter_context(nc.Block())

        @block.sync
        def _(eng):
            run("sp", eng)

        @block.gpsimd
        def _(eng):
            run("pool", eng)

        @block.scalar
        def _(eng):
            run("act", eng)

        @block.vector
        def _(eng):
            run("dve", eng)

        @block.tensor
        def _(eng):
            run("pe", eng)


D = 1024
T = 2048
TC = 256
NT = 16
EPS = 1e-6
COL_Q = 1024
COL_K = 1280
COL_V = 1536
COL_R = 2048
COL_DEC = 2560
COL_GATE = 2592
NVEC = 224
V_C = 0
V_BMOD = 16
V_GPRE1 = 64
V_GPRE2 = 72
V_CW = 80
V_CB = 204
V_LNG = 208
V_LNB = 212
V_GN = 216
LN8 = math.log(0.125)


def build(dbg=None, upto=9, simcompat=False):
    nc = bass.Bass("TRN2", target_bir_lowering=False)

    def din(name, shape):
        return nc.dram_tensor(name, shape, F32, kind="ExternalInput").ap()

    x_d = din("x", [T, D])
    ctx_d = din("ctx", [TC, D])
    vecs_d = din("vecs", [128, NVEC])
    gpost_d = din("gpost", [128, 2048])
    wda_d = din("wda", [33, 512])
    consts_d = din("consts", [128, 1408])
    w_mod = din("w_mod", [D, 6144]).rearrange("(k p) n -> p k n", p=128)
    w_in = din("w_in", [D, 4640]).rearrange("(k p) n -> p k n", p=128)
    w_co = din("w_conv_out", [512, D]).rearrange("(k p) n -> p k n", p=128)
    w_go = din("w_gla_out", [512, D]).rearrange("(k p) n -> p k n", p=128)
    w_out = din("w_out", [D, D]).rearrange("(k p) n -> p k n", p=128)
    w_ff1_raw = din("w_ff1", [D, 4096])
    w_ff2_raw = din("w_ff2", [4096, D])
    wf1b_raw = nc.dram_tensor("wf1b", [D, 4096], BF16, kind="Internal").ap()
    wf2b_raw = nc.dram_tensor("wf2b", [4096, D], BF16, kind="Internal").ap()
    w_ff1 = wf1b_raw.rearrange("(k p) n -> p k n", p=128)
    w_ff2 = wf2b_raw.rearrange("(k p) n -> p k n", p=128)
    out_d = nc.dram_tensor("out", [T, D], F32, kind="ExternalOutput").ap()
    hmid_d = nc.dram_tensor("hmid", [T, D], F32, kind="Internal").ap()
    dbg_out = {}

    with contextlib.ExitStack() as st:
        def sb(name, shape, dt):
            return st.enter_context(nc.sbuf_tensor(name, shape, dt))

        P = Prog(nc)
        UT = sb("UT", [128, 8, 2048], BF16)
        WW = sb("WW", [128, 32768], BF16)
        R1 = sb("R1", [128, 8192], F32)
        R2 = sb("R2", [128, 8192], BF16)
        R3 = sb("R3", [128, 4, 2048], BF16)
        XT = sb("XT", [128, 4096], F32)
        GG = sb("GG", [128, 2048], F32)
        SC = sb("SC", [128, 4096], F32)
        VEC = sb("VEC", [128, NVEC], F32)
        SM = sb("SM", [128, 256], F32)
        IDF = sb("IDF", [128, 128], F32)
        ONESF = sb("ONESF", [128, 128], F32)
        CB = sb("CB", [128, 1408], BF16)
        WDA = sb("WDA", [33, 512], BF16)
        SCV = sb("SCV", [128, 16], BF16)

        MOD = SM[:, 0:48]
        CMOD = SM[:, 48:64]
        GS = SM[:, 64:88]
        STAT = SM[:, 88:120]
        RSTD = SM[:, 120:152]
        EL = SM[:, 152:224].rearrange("p (c n) -> p c n", c=4)
        TMP8 = SM[:, 224:256]
        IDB = CB[:, 0:128]
        MU4 = CB[:, 128:640]
        ML4 = CB[:, 640:1152]
        TRIU = CB[:, 1152:1280]
        TRIL = CB[:, 1280:1408]

        ONES = sb("ONES", [128, 256], BF16)
        ONESB = ONES[:, 0:128]
        ONES512 = ONES[:, 128:256]

        SCB = SC[:].bitcast(BF16)
        XTB = XT[:].bitcast(BF16)
        R1B = R1[:].bitcast(BF16)

        PSF = [st.enter_context(nc.psum_tensor("ps%d" % i, [128, 512], F32)) for i in range(7)]
        PSFR = [Res("ps%d" % i) for i in range(7)]
        PSB = st.enter_context(nc.psum_tensor("psb", [128, 1024], BF16))
        PSBR = Res("psb")
        bank_i = [0]

        nbanks = [7]
        PSF8 = PSB[:].bitcast(F32)

        class _BankView:
            def __init__(self, ap):
                self.ap = ap

            def __getitem__(self, key):
                return self.ap[key]

        def bank():
            i = bank_i[0] % nbanks[0]
            bank_i[0] += 1
            if i == 7:
                assert PSBR.w is None or len(PSBR.r) > 0
                return _BankView(PSF8), PSBR
            assert PSFR[i].w is None or len(PSFR[i].r) > 0, "PSUM bank %d re-issued before its evacuation was emitted" % i
            return PSF[i], PSFR[i]

        def ACT(out, in_, func, reads, writes, **kw):
            return P.op("act", lambda e: e.activation(out=out, in_=in_, func=func, **kw), reads, writes)

        def MM(out, lhsT, rhs, start, stop, reads, writes, tile_position=None):
            if tile_position is not None:
                return P.op("pe", lambda e: e.matmul(out, lhsT=lhsT, rhs=rhs, start=start, stop=stop,
                                                     tile_position=tile_position), reads, writes)
            return P.op("pe", lambda e: e.matmul(out, lhsT=lhsT, rhs=rhs, start=start, stop=stop), reads, writes)

        def TR(out, in_, ident, reads, writes):
            return P.op("pe", lambda e: e.transpose(out=out, in_=in_, identity=ident), reads, writes)

        def TT(eng, out, in0, in1, op, reads, writes):
            return P.op(eng, lambda e: e.tensor_tensor(out=out, in0=in0, in1=in1, op=op), reads, writes)

        def TS(eng, out, in0, s1, s2, op0, op1, reads, writes):
            if s2 is None:
                return P.op(eng, lambda e: e.tensor_scalar(out=out, in0=in0, scalar1=s1, scalar2=None, op0=op0), reads, writes)
            return P.op(eng, lambda e: e.tensor_scalar(out=out, in0=in0, scalar1=s1, scalar2=s2, op0=op0, op1=op1), reads, writes)

        def STT(eng, out, in0, scalar, in1, op0, op1, reads, writes):
            return P.op(eng, lambda e: e.scalar_tensor_tensor(out=out, in0=in0, scalar=scalar, in1=in1, op0=op0, op1=op1), reads, writes)

        def CP(eng, out, in_, reads, writes):
            return P.op(eng, lambda e: e.tensor_copy(out=out, in_=in_), reads, writes)

        def MSET(eng, ap, val, writes):
            return P.op(eng, lambda e: e.memset(ap, val), (), writes)

        def DMA(q, ds, out, in_, reads, writes):
            return P.dma(q, ds, lambda e: e.dma_start(out=out, in_=in_), reads, writes)

        def RSTDOP(out, in_, tmp, rin, rtmp, rout):
            ACT(tmp, in_, AF.Ln, [rin], [rtmp], bias=EPS)
            ACT(out, tmp, AF.Exp, [rtmp], [rout], scale=-0.5)

        def dump(name, ap, shape, reads):
            if dbg is None or name not in dbg:
                return
            t = nc.dram_tensor("dbg_" + name, list(shape), ap.dtype, kind="ExternalOutput").ap()
            ds = P.dsem("dbg_" + name)
            DMA("sp", ds, t, ap, reads, [Res()])
            dbg_out[name] = ds

        rMOD2 = Res("mod2"); rVEC = Res("vec"); rCON = Res("con"); rGG = [Res("gg0"), Res("gg1")]
        rSM = Res("sm"); rMOD = Res("mod"); rSCV = Res("scv"); rWDA = Res("wda")
        rUT = [Res("ut%d" % i) for i in range(4)]
        rCUT = Res("cut")
        rWW = [Res("ww%d" % i) for i in range(4)]
        rONES = Res("ones")

        rIDF = Res("idf")
        DMA("sp", P.dsem("c_vec"), VEC[:], vecs_d, [], [rVEC])
        DMA("sp", P.dsem("c_idf"), IDF[:], consts_d[:, 0:128], [], [rIDF])
        DMA("pool", P.dsem("c_cb"), CB[:, 0:1408], consts_d[:, 0:1408], [], [rCON])
        DMA("pool", P.dsem("c_wda"), WDA[:], wda_d, [], [rWDA])
        MSET("dve", ONESF[:], 1.0, [rONES])
        MSET("dve", ONES[:, 0:128], 1.0 / 128, [rONES])
        MSET("dve", ONES[:, 128:256], 1.0 / 512, [rONES])

        OUT_EVS = []

        def body():
            ACT(SCV[:], VEC[:, V_C:V_C + 16], AF.Silu, [rVEC], [rSCV])
            d_wm = [P.dsem("wm%d" % i) for i in range(4)]
            rWM = rWW
            WMv = [WW[:, i * 8192:(i + 1) * 8192].rearrange("p (k n) -> p k n", k=8) for i in range(4)]
            psm, rpsm = bank()
            for j in (0, 1):
                DMA("pool", d_wm[2 + j], WMv[2 + j], w_mod[:, :, j * 1024:(j + 1) * 1024], [], [rWM[2 + j]])
            for j in (0, 1):
                for ccl in range(8):
                    cc = j * 8 + ccl
                    for k in range(8):
                        MM(psm[:, cc * 2:cc * 2 + 2], WMv[2 + j][:, k, ccl * 128:(ccl + 1) * 128], SCV[:, 2 * k:2 * k + 2],
                           k == 0, k == 7, [rWM[2 + j], rSCV], [rpsm])
            psm3 = psm[:, 0:96].rearrange("p (c j) -> p c j", j=2)
            TT("dve", MOD[:, 0:16], psm3[:, 0:16, 0], VEC[:, V_BMOD:V_BMOD + 16], ALU.add, [rpsm, rVEC], [rMOD])
            TT("dve", CMOD, psm3[:, 0:16, 1], VEC[:, V_BMOD:V_BMOD + 16], ALU.add, [rpsm, rVEC], [rMOD])
            STT("dve", GS[:, 0:8], MOD[:, 8:16], 1.0, VEC[:, V_GPRE1:V_GPRE1 + 8], ALU.add, ALU.mult, [rMOD, rVEC], [rSM])
            STT("dve", GS[:, 8:16], CMOD[:, 8:16], 1.0, VEC[:, V_GPRE1:V_GPRE1 + 8], ALU.add, ALU.mult, [rMOD, rVEC], [rSM])

            WQK = WW[:, 0:4096].rearrange("p (k n) -> p k n", k=8)
            WDEC = WW[:, 4096:4352].rearrange("p (k n) -> p k n", k=8)
            WV = WW[:, 4352:8448].rearrange("p (k n) -> p k n", k=8)
            WR = WW[:, 8448:12544].rearrange("p (k n) -> p k n", k=8)
            d_wa = P.dsem("wa1")
            rWA = Res("wa1")
            DMA("pool", d_wa, WDEC, w_in[:, :, COL_DEC:COL_DEC + 32], [], [rWA])
            DMA("pool", d_wa, WQK, w_in[:, :, COL_Q:COL_Q + 512], [], [rWA])
            DMA("pool", d_wa, WV, w_in[:, :, COL_V:COL_V + 512], [], [rWA])
            rWR = Res("wr")
            d_wr = P.dsem("wr")

            GGB = GG[:].bitcast(BF16).rearrange("p (k n) -> p k n", k=8)
            rGGB = Res("ggb")
            d_gb = P.dsem("ggb")

            rCV1 = [Res("cv1_%d" % i) for i in range(4)]
            rCV2 = [Res("cv2_%d" % i) for i in range(4)]

            def adaln_piece(i):
                col0 = 2048 + i * 512
                DMA("pool", d_gb, GGB, w_mod[:, :, col0:col0 + 512], [], [rGGB])
                if i >= 4:
                    j = i - 4
                    DMA("pool", P.dsem("cv1_%d" % j), wf1b_raw[j * 256:(j + 1) * 256, :], w_ff1_raw[j * 256:(j + 1) * 256, :], [], [rCV1[j]])
                pp, rpp = bank()
                for ccl in range(4):
                    for k in range(8):
                        MM(pp[:, ccl * 2:ccl * 2 + 2], GGB[:, k, ccl * 128:(ccl + 1) * 128], SCV[:, 2 * k:2 * k + 2],
                           k == 0, k == 7, [rGGB, rSCV], [rpp])
                cc0 = 16 + i * 4
                pp3 = pp[:, 0:8].rearrange("p (c j) -> p c j", j=2)
                TT("dve", MOD[:, cc0:cc0 + 4], pp3[:, :, 0], VEC[:, V_BMOD + cc0:V_BMOD + cc0 + 4], ALU.add, [rpp, rVEC], [rMOD2])

            def adaln_finish():
                STT("dve", GS[:, 16:24], MOD[:, 32:40], 1.0, VEC[:, V_GPRE2:V_GPRE2 + 8], ALU.add, ALU.mult, [rMOD2, rVEC], [rMOD2])
                DMA("sp", P.dsem("c_gg"), GG[:], gpost_d, [], rGG + [rGGB])
                DG = SC[:, 0:1024].rearrange("p (c n) -> p c n", c=8)
                rDG = Res("dg")
                for g in range(2):
                    gt0 = 16 if g == 0 else 40
                    for c in range(8):
                        TS("dve", DG[:, c, :], IDF[:], MOD[:, gt0 + c:gt0 + c + 1], None, ALU.mult, None, [rIDF, rMOD2], [rDG])
                    for half in range(2):
                        pg, rpg = bank()
                        for c4 in range(4):
                            c = half * 4 + c4
                            MM(pg[:, c4 * 128:(c4 + 1) * 128], ONESF[:], DG[:, c, :], True, True, [rONES, rDG], [rpg])
                        sl = GG[:, g * 1024 + half * 512:g * 1024 + (half + 1) * 512]
                        TT("dve", sl, pg[:], sl, ALU.mult, [rpg, rGG[g]], [rGG[g]])

            if upto == 0:
                return
            XS = R1[:].rearrange("p (s n) -> p s n", s=8)
            rXS = [Res("xs%d" % i) for i in range(8)]
            d_xs = [P.dsem("xs%d" % i) for i in range(8)]
            JUNK = SCB[:, 4096:5120]
            CUT = XTB[:, 0:2048].rearrange("p (k n) -> p k n", k=8)
            blocks = [("c", 0, 2)] + [("x", b, 4) for b in range(4)]
            slot_ctr = [0]
            blk_slots = {}
            blk_rst = {}

            def p1_stats(bi):
                kind, bidx, ntl = blocks[bi]
                slots = []
                for t in range(ntl):
                    s_ = slot_ctr[0] % 8
                    slot_ctr[0] += 1
                    slots.append(s_)
                    src = ctx_d[t * 128:(t + 1) * 128, :] if kind == "c" else x_d[(bidx * 4 + t) * 128:(bidx * 4 + t + 1) * 128, :]
                    DMA("sp", d_xs[s_], XS[:, s_, :], src, [], [rXS[s_]])
                st0 = 0 if kind == "c" else 2 + bidx * 4
                rst = Res("stat")
                for t, s_ in enumerate(slots):
                    ACT(JUNK, XS[:, s_, :], AF.Square, [rXS[s_]], [rst], scale=1.0 / 32, accum_out=STAT[:, st0 + t:st0 + t + 1])
                RSTDOP(RSTD[:, st0:st0 + ntl], STAT[:, st0:st0 + ntl], TMP8[:, 0:ntl], rst, rst, rst)
                for t, s_ in enumerate(slots):
                    TS("dve", XS[:, s_, :], XS[:, s_, :], RSTD[:, st0 + t:st0 + t + 1], None, ALU.mult, None, [rst, rXS[s_]], [rXS[s_]])
                blk_slots[bi] = slots

            def p1_transpose(bi):
                kind, bidx, ntl = blocks[bi]
                slots = blk_slots[bi]
                for c in range(8):
                    pb, rpb = bank()
                    for t, s_ in enumerate(slots):
                        TR(pb[:, t * 128:(t + 1) * 128], XS[:, s_, c * 128:(c + 1) * 128], IDF[:], [rXS[s_], rIDF], [rpb])
                    if kind == "c":
                        ACT(CUT[:, c, :], pb[:, 0:256], AF.Identity, [rpb, rSM, rMOD], [rCUT],
                            scale=GS[:, 8 + c:9 + c], bias=CMOD[:, c:c + 1])
                    else:
                        ACT(UT[:, c, bidx * 512:(bidx + 1) * 512], pb[:], AF.Identity, [rpb, rSM, rMOD], [rUT[bidx]],
                            scale=GS[:, c:c + 1], bias=MOD[:, c:c + 1])

            p1_stats(0)
            for bi in range(5):
                if bi + 1 < 5:
                    p1_stats(bi + 1)
                p1_transpose(bi)
            dump("ut", UT[:], [128, 8, 2048], rUT)

            if upto == 1:
                return
            LA = WW[:, 12544:14592].rearrange("p (t n) -> p t n", t=4)
            CK = WW[:, 14592:15616].rearrange("p (c n) -> p c n", c=4)
            QK = WW[:, 16384:32768].rearrange("p (c n) -> p c n", c=8)

            ZT = XTB[0:33, 2048:4352]
            CKT = XTB[:, 4352:5376].rearrange("p (c n d) -> p c n d", c=4, n=2)
            CV = XTB[:, 5376:6400].rearrange("p (n d) -> p n d", n=2)
            KT = R1B[:, 0:8192].rearrange("p (c n d) -> p c n d", c=4, n=16)
            SS = R1B[:, 8192:16384].rearrange("p (c n d) -> p c n d", c=4, n=16)
            V = R2[:].rearrange("p (n d) -> p n d", n=16)
            SR = R3
            E_ = [SC[:, 0:512], SC[:, 512:1024]]
            EN_ = [SC[:, 1024:1536], SC[:, 1536:2048]]
            EQ_ = [SC[:, 2048:2560], SC[:, 2560:3072]]
            RR = SC[:, 3072:4096].rearrange("p (c b d) -> p c b d", c=4, b=2)
            rE = [Res("e0"), Res("e1")]
            rEN = [Res("en0"), Res("en1")]
            rEQ = [Res("eq0"), Res("eq1")]
            rZT = Res("zt"); rLA = Res("la"); rEL = Res("el"); rCK = Res("ck"); rCKT = Res("ckt"); rCV = Res("cv")
            rQK = [Res("qk%d" % i) for i in range(4)]
            rKT = [Res("kt%d" % i) for i in range(4)]
            rV = [Res("v%d" % i) for i in range(4)]
            rSR = [Res("sr%d" % i) for i in range(4)]
            MSET("dve", XTB[32:33, 2048:4352], 1.0, [rZT])
            ei = [0]

            def blk_params(bi):
                kind, bidx, ntl = blocks[bi]
                ntok = ntl * 128
                if kind == "c":
                    return kind, bidx, ntl, ntok, CUT, rCUT, 0, 0, 0
                return (kind, bidx, ntl, ntok, UT[:, :, bidx * 512:(bidx + 1) * 512], rUT[bidx], bidx * 512,
                        256 + bidx * 512, 2 + bidx * 4)

            def a1_s1(bi):
                kind, bidx, ntl, ntok, U, rU, tok0, zoff, ch0 = blk_params(bi)
                pz, rpz = bank()
                for k in range(8):
                    MM(pz[0:32, 0:ntok], WDEC[:, k, :], U[:, k, :], k == 0, k == 7, [rWA, rU], [rpz])
                ACT(ZT[0:32, zoff:zoff + ntok], pz[0:32, 0:ntok], AF.Copy, [rpz], [rZT])
                pvs = []
                for t in range(ntl):
                    pv, rpv = bank()
                    for k in range(8):
                        MM(pv[:], U[:, k, t * 128:(t + 1) * 128], WV[:, k, :], k == 0, k == 7, [rU, rWA], [rpv])
                    if kind == "c":
                        ACT(CV[:, t, :], pv[:], AF.Copy, [rpv], [rCV])
                    else:
                        ACT(V[:, bidx * 4 + t, :], pv[:], AF.Copy, [rpv], [rV[bidx]])
                    if t >= 1:
                        tt_ = t - 1
                        pl, rpl = bank()
                        MM(pl[:], ZT[:, zoff + tt_ * 128:zoff + (tt_ + 1) * 128], WDA[:], True, True, [rZT, rWDA], [rpl])
                        i = ei[0] % 2
                        ei[0] += 1
                        ACT(E_[i], pl[:], AF.Exp, [rpl], [rE[i]], scale=-1.0)
                        ACT(LA[:, tt_, :], E_[i], AF.Ln, [rE[i]], [rLA], bias=1.0)
                tt_ = ntl - 1
                pl, rpl = bank()
                MM(pl[:], ZT[:, zoff + tt_ * 128:zoff + (tt_ + 1) * 128], WDA[:], True, True, [rZT, rWDA], [rpl])
                i = ei[0] % 2
                ei[0] += 1
                ACT(E_[i], pl[:], AF.Exp, [rpl], [rE[i]], scale=-1.0)
                ACT(LA[:, tt_, :], E_[i], AF.Ln, [rE[i]], [rLA], bias=1.0)

            tr_jobs = {}

            def a1_s2(bi):
                kind, bidx, ntl, ntok, U, rU, tok0, zoff, ch0 = blk_params(bi)
                jobs = []
                for hp in range(2):
                    pk, rpk = bank()
                    for k in range(8):
                        MM(pk[:, 0:ntok], WQK[:, k, 256 + hp * 128:256 + (hp + 1) * 128], U[:, k, :], k == 0, k == 7, [rWA, rU], [rpk])
                    if kind == "x":
                        pq, rpq = bank()
                        for k in range(8):
                            MM(pq[:, 0:ntok], WQK[:, k, hp * 128:(hp + 1) * 128], U[:, k, :], k == 0, k == 7, [rWA, rU], [rpq])
                    for d in range(2):
                        combo = d * 2 + hp
                        pgm, rpgm = bank()
                        tri = TRIU if d == 0 else TRIL
                        for t in range(ntl):
                            MM(pgm[:, t * 128:(t + 1) * 128], LA[:, t, d * 256 + hp * 128:d * 256 + (hp + 1) * 128], tri,
                               True, True, [rLA, rCON], [rpgm])
                        i = ei[0] % 2
                        ei[0] += 1
                        ACT(EN_[i][:, 0:ntok], pgm[:, 0:ntok], AF.Exp, [rpgm], [rEN[i]], scale=1.0 / 16)
                        col0 = 127 if d == 0 else 0
                        pgl = pgm[:, 0:ntok].rearrange("p (t n) -> p t n", n=128)[:, :, col0]
                        ACT(EL[:, combo, ch0:ch0 + ntl], pgl, AF.Exp, [rpgm], [rEL], scale=-1.0 / 16)
                        if kind == "x":
                            ACT(EQ_[i][:, 0:ntok], pgm[:, 0:ntok], AF.Exp, [rpgm], [rEQ[i]], scale=-1.0 / 16, bias=LN8)
                            TT("dve", QK[:, (2 + d) * 2 + hp, tok0:tok0 + ntok], pk[:, 0:ntok], EN_[i][:, 0:ntok], ALU.mult,
                               [rpk, rEN[i]], [rQK[bidx], rWW[2], rWW[3]])
                            TT("dve", QK[:, d * 2 + hp, tok0:tok0 + ntok], pq[:, 0:ntok], EQ_[i][:, 0:ntok], ALU.mult,
                               [rpq, rEQ[i]], [rQK[bidx], rWW[2], rWW[3]])
                            jobs.append((combo, QK[:, (2 + d) * 2 + hp, tok0:tok0 + ntok], rQK[bidx]))
                        else:
                            TT("dve", CK[:, combo, 0:ntok], pk[:, 0:ntok], EN_[i][:, 0:ntok], ALU.mult, [rpk, rEN[i]], [rCK])
                            jobs.append((combo, CK[:, combo, 0:ntok], rCK))
                tr_jobs[bi] = jobs

            seqs = {0: [("c", 0), ("c", 1)] + [("x", i) for i in range(16)],
                    1: [("c", 1), ("c", 0)] + [("x", i) for i in range(15, -1, -1)]}
            rRR = [[Res("rr%d_%d" % (c, b)) for b in range(2)] for c in range(4)]
            rSS = [[Res("ss%d_%d" % (c, i)) for i in range(16)] for c in range(4)]

            def chain_step(step, combos):
                pu, rpu = bank()
                info = []
                for j_c, combo in enumerate(combos):
                    uc = slice(j_c * 128, (j_c + 1) * 128)
                    d, hp = combo // 2, combo % 2
                    kind, ci = seqs[d][step]
                    if kind == "c":
                        lhs = CKT[:, combo, ci, :]; rl = rCKT
                        rhs = CV[:, ci, hp * 256:(hp + 1) * 256]; rr_ = rCV
                    else:
                        lhs = KT[:, combo, ci, :]; rl = rKT[ci // 4]
                        rhs = V[:, ci, hp * 256:(hp + 1) * 256]; rr_ = rV[ci // 4]
                    MM(pu[0:64, uc], lhs[:, 0:64], rhs[:, 0:128], True, True, [rl, rr_], [rpu])
                    MM(pu[64:128, uc], lhs[:, 64:128], rhs[:, 128:256], True, True, [rl, rr_], [rpu], tile_position=(0, 64))
                    info.append((combo, uc, d, kind, ci))
                cur = step % 2
                prv = 1 - cur
                for combo, uc, d, kind, ci in info:
                    if step == 0:
                        CP("dve", RR[:, combo, cur, :], pu[:, uc], [rpu], [rRR[combo][cur]])
                    else:
                        pk_, pci = seqs[d][step - 1]
                        pel = pci if pk_ == "c" else 2 + pci
                        if kind == "x":
                            ACT(SS[:, combo, ci, :], RR[:, combo, prv, :], AF.Copy, [rRR[combo][prv], rEL], [rSS[combo][ci]] + rXS[4:8],
                                scale=EL[:, combo, pel:pel + 1])
                        STT("dve", RR[:, combo, cur, :], RR[:, combo, prv, :], EL[:, combo, pel:pel + 1], pu[:, uc],
                            ALU.mult, ALU.add, [rRR[combo][prv], rEL, rpu], [rRR[combo][cur]])

            def a1_s3(bi):
                kind, bidx, ntl, ntok, U, rU, tok0, zoff, ch0 = blk_params(bi)
                if kind == "x":
                    for c in range(4):
                        pr, rpr = bank()
                        for k in range(8):
                            MM(pr[:], WR[:, k, c * 128:(c + 1) * 128], U[:, k, :], k == 0, k == 7, [rWR, rU], [rpr])
                        ACT(SR[:, c, tok0:tok0 + 512], pr[:], AF.Silu, [rpr], [rSR[bidx]])
                        if c == 0:
                            adaln_piece(bidx * 2)
                for combo, ksrc, rks in tr_jobs[bi]:
                    half = combo % 2
                    for t in range(ntl):
                        TR(PSB[:, half * 512 + t * 128:half * 512 + (t + 1) * 128], ksrc[:, t * 128:(t + 1) * 128], IDB,
                           [rks, rCON], [PSBR])
                    if kind == "x":
                        CP("dve", KT[:, combo, bidx * 4:bidx * 4 + 4, :], PSB[:, half * 512:half * 512 + 512].rearrange("p (t n) -> p t n", n=128),
                           [PSBR], [rKT[bidx]] + rXS[0:4])
                    else:
                        CP("dve", CKT[:, combo, :, :], PSB[:, half * 512:half * 512 + 256].rearrange("p (t n) -> p t n", n=128),
                           [PSBR], [rCKT])
                fsteps = [0, 1] if kind == "c" else [2 + bidx * 4 + t_ for t_ in range(4)]
                for st_ in fsteps:
                    chain_step(st_, (0, 1))
                if kind == "x":
                    adaln_piece(bidx * 2 + 1)

            DMA("pool", d_wr, WR, w_in[:, :, COL_R:COL_R + 512], list(rXS), [rWR])
            a1_s1(0)
            a1_s2(0)
            for bi in range(1, 5):
                a1_s1(bi)
                a1_s3(bi - 1)
                a1_s2(bi)
            a1_s3(4)
            nbanks[0] = 8
            if upto == 1.5:
                return
            dump("qk", QK, [128, 8, 2048], rQK)
            dump("el", SM[:, 152:224], [128, 72], [rEL])

            bufsets = [
                dict(T1=SC[:, 0:512], T2=SC[:, 512:1024], RS=SC[:, 1024:1536], T3=SC[:, 1536:2048],
                     MT=SCB[:, 5120:5632], SQ=SCB[:, 5632:6144]),
                dict(T1=XT[:, 0:512], T2=XT[:, 512:1024], RS=XT[:, 1024:1536], T3=XT[:, 1536:2048],
                     MT=XTB[:, 6400:6912], SQ=XTB[:, 6912:7424]),
            ]
            for bs_ in bufsets:
                for nm in ("T1", "T2", "RS", "T3", "MT", "SQ"):
                    bs_["r" + nm] = Res(nm)
            po_l = {}
            MTall = WW[:, 0:8192].rearrange("p (c n) -> p c n", c=16)
            rMTall = [Res("mt%d" % c) for c in range(16)]

            def stage1(c):
                b = c // 4
                tk = slice(c * 128, (c + 1) * 128)
                B_ = bufsets[c % 2]
                T1, T2 = B_["T1"], B_["T2"]
                rT1, rT2 = B_["rT1"], B_["rT2"]
                MT = MTall[:, c, :]
                rMT = rMTall[c]
                pxy = [bank(), bank()]
                for j_ in range(4):
                    for hf in range(2):
                        rows = slice(hf * 64, (hf + 1) * 64)
                        px, rpx = pxy[hf]
                        hp = j_ % 2
                        if j_ < 2:
                            MM(px[:, hp * 128:(hp + 1) * 128], QK[rows, 4 + hp, tk], QK[rows, 0 + hp, tk], True, True, [rQK[b]], [rpx])
                        else:
                            MM(px[:, 256 + hp * 128:256 + (hp + 1) * 128], QK[rows, 6 + hp, tk], QK[rows, 2 + hp, tk], True, True, [rQK[b]], [rpx])
                TT("dve", T1, pxy[0][0][:], MU4, ALU.mult, [pxy[0][1], rCON], [rT1])
                TT("dve", T2, pxy[1][0][:], MU4, ALU.mult, [pxy[1][1], rCON], [rT2])
                MT3 = MT.rearrange("p (hp hf n) -> p hp hf n", hp=2, hf=2)
                TT("pool", MT3[:, :, 0, :], T1[:, 0:256].rearrange("p (a n) -> p a n", a=2), T1[:, 256:512].rearrange("p (a n) -> p a n", a=2),
                   ALU.add, [rT1], [rMT, rWA])
                TT("pool", MT3[:, :, 1, :], T2[:, 0:256].rearrange("p (a n) -> p a n", a=2), T2[:, 256:512].rearrange("p (a n) -> p a n", a=2),
                   ALU.add, [rT2], [rMT])

            bufsets = [
                dict(T1=SC[:, 0:512], T2=SC[:, 512:1024], RS=SC[:, 1024:1536], T3=SC[:, 1536:2048],
                     MT=SCB[:, 5120:5632], SQ=SCB[:, 5632:6144]),
                dict(T1=XT[:, 0:512], T2=XT[:, 512:1024], RS=XT[:, 1024:1536], T3=XT[:, 1536:2048],
                     MT=XTB[:, 6400:6912], SQ=XTB[:, 6912:7424]),
            ]
            for bs_ in bufsets:
                for nm in ("T1", "T2", "RS", "T3", "MT", "SQ"):
                    bs_["r" + nm] = Res(nm)
            po_l = {}
            MTall = WW[:, 0:8192].rearrange("p (c n) -> p c n", c=16)
            rMTall = [Res("mt%d" % c) for c in range(16)]

            def stage1(c):
                b = c // 4
                tk = slice(c * 128, (c + 1) * 128)
                B_ = bufsets[c % 2]
                T1, T2 = B_["T1"], B_["T2"]
                rT1, rT2 = B_["rT1"], B_["rT2"]
                MT = MTall[:, c, :]
                rMT = rMTall[c]
                pxy = [bank(), bank()]
                for j_ in range(4):
                    for hf in range(2):
                        rows = slice(hf * 64, (hf + 1) * 64)
                        px, rpx = pxy[hf]
                        hp = j_ % 2
                        if j_ < 2:
                            MM(px[:, hp * 128:(hp + 1) * 128], QK[rows, 4 + hp, tk], QK[rows, 0 + hp, tk], True, True, [rQK[b]], [rpx])
                        else:
                            MM(px[:, 256 + hp * 128:256 + (hp + 1) * 128], QK[rows, 6 + hp, tk], QK[rows, 2 + hp, tk], True, True, [rQK[b]], [rpx])
                TT("dve", T1, pxy[0][0][:], MU4, ALU.mult, [pxy[0][1], rCON], [rT1])
                TT("dve", T2, pxy[1][0][:], MU4, ALU.mult, [pxy[1][1], rCON], [rT2])
                MT3 = MT.rearrange("p (hp hf n) -> p hp hf n", hp=2, hf=2)
                TT("pool", MT3[:, :, 0, :], T1[:, 0:256].rearrange("p (a n) -> p a n", a=2), T1[:, 256:512].rearrange("p (a n) -> p a n", a=2),
                   ALU.add, [rT1], [rMT, rWA])
                TT("pool", MT3[:, :, 1, :], T2[:, 0:256].rearrange("p (a n) -> p a n", a=2), T2[:, 256:512].rearrange("p (a n) -> p a n", a=2),
                   ALU.add, [rT2], [rMT])

            if upto == 1.7:
                return
            dump("ss", R1B[:, 8192:16384], [128, 8192], [r_ for l_ in rSS for r_ in l_])

            def stage2(c, pos):
                b = c // 4
                tk = slice(c * 128, (c + 1) * 128)
                B_ = bufsets[pos % 2]
                SQ, rSQ = B_["SQ"], B_["rSQ"]
                MT = MTall[:, c, :]
                rMT = rMTall[c]
                pA, rpA = bank()
                pB, rpB = bank()
                po_l[c] = ((pA, rpA), (pB, rpB))
                for hp in range(2):
                    cs = slice(hp * 128, (hp + 1) * 128)
                    for hf, (pp_, rpp_) in enumerate(((pA, rpA), (pB, rpB))):
                        h = hp * 2 + hf
                        MM(pp_[:, cs], V[:, c, h * 128:(h + 1) * 128], MT[:, h * 128:(h + 1) * 128], True, False, [rV[b], rMT], [rpp_])
                    for d_ in range(2):
                        for hf, (pp_, rpp_) in enumerate(((pA, rpA), (pB, rpB))):
                            rows = slice(hf * 64, (hf + 1) * 64)
                            MM(pp_[:, cs], SS[rows, 2 * d_ + hp, c, :], QK[rows, 2 * d_ + hp, tk], False, d_ == 1,
                               [rSS[2 * d_ + hp][c], rQK[b]], [rpp_])
                SQ4 = SQ.rearrange("p (hp hf n) -> p hp hf n", hp=2, hf=2)
                ACT(SQ4[:, :, 0, :], pA[:, 0:256].rearrange("p (a n) -> p a n", a=2), AF.Square, [rpA], [rSQ])
                ACT(SQ4[:, :, 1, :], pB[:, 0:256].rearrange("p (a n) -> p a n", a=2), AF.Square, [rpB], [rSQ])

            def stage3(c, pos):
                b = c // 4
                tk = slice(c * 128, (c + 1) * 128)
                B_ = bufsets[pos % 2]
                RS, T3, SQ, rRS, rT3, rSQ = B_["RS"], B_["T3"], B_["SQ"], B_["rRS"], B_["rT3"], B_["rSQ"]
                (pA, rpA), (pB, rpB) = po_l[c]
                pms, rpms = bank()
                MM(pms[:], ONESB, SQ, True, True, [rONES, rSQ], [rpms])
                RSTDOP(RS, pms[:], RS, rpms, rRS, rRS)
                T34 = T3.rearrange("p (hp hf n) -> p hp hf n", hp=2, hf=2)
                RS4 = RS.rearrange("p (hp hf n) -> p hp hf n", hp=2, hf=2)
                STT("dve", T34[:, :, 0, :], pA[:, 0:256].rearrange("p (a n) -> p a n", a=2), VEC[:, V_GN:V_GN + 1], RS4[:, :, 0, :],
                    ALU.mult, ALU.mult, [rpA, rVEC, rRS], [rT3])
                STT("dve", T34[:, :, 1, :], pB[:, 0:256].rearrange("p (a n) -> p a n", a=2), VEC[:, V_GN:V_GN + 1], RS4[:, :, 1, :],
                    ALU.mult, ALU.mult, [rpB, rVEC, rRS], [rT3])
                TT("pool", SR[:, :, tk], T3.rearrange("p (h n) -> p h n", h=4), SR[:, :, tk], ALU.mult, [rT3, rSR[b]], [rSR[b]])

            pend = []
            npos = [0]
            for step in range(18):
                chain_step(step, (2, 3))
                if pend:
                    stage3(*pend.pop(0))
                if step < 16:
                    stage1(15 - step)
                for c_ in range(16):
                    if 17 - c_ == step:
                        stage2(c_, npos[0])
                        pend.append((c_, npos[0]))
                        npos[0] += 1
            while pend:
                stage3(*pend.pop(0))
            dump("og", SR[:], [128, 4, 2048], rSR)

            if upto == 2:
                return
            P.barrier()
            nbanks[0] = 7
            adaln_finish()
            WG = WW[:, 0:8192].rearrange("p (k n) -> p k n", k=8)
            d_wg = P.dsem("wglu")
            rWG = Res("wglu")
            DMA("pool", d_wg, WG, w_in[:, :, 0:1024], [rWA], [rWG])
            WGT = WW[:, 8192:24576].rearrange("p (k n) -> p k n", k=8)
            rWGT = Res("wgt"); rWOUT = Res("wout")
            DMA("pool", P.dsem("wgt"), WGT, w_in[:, :, COL_GATE:COL_GATE + 2048], [], [rWGT])
            for i_ in range(4):
                DMA("pool", P.dsem("cv2_%d" % i_), wf2b_raw[i_ * 1024:(i_ + 1) * 1024, :], w_ff2_raw[i_ * 1024:(i_ + 1) * 1024, :], [], [rCV2[i_]])

            AT = R2[:].rearrange("p (c n) -> p c n", c=4)
            rAT = [Res("at%d" % c) for c in range(4)]
            ST = WW[:, 24576:32768].rearrange("p (c n) -> p c n", c=4)
            rST = [[Res("st%d_%d" % (c, b_)) for b_ in range(4)] for c in range(4)]
            rDIAGall = Res("diagall"); rY32all = Res("y32all")
            DIAG = R1B[:, 0:15872].rearrange("p (j n) -> p j n", j=124)
            rDIAG = [Res("dg%d" % j) for j in range(124)]
            for j in range(124):
                wcol = VEC[:, V_CW + j:V_CW + j + 1]
                e_ = ("dve", "act")[j % 2]
                if e_ == "act":
                    ACT(DIAG[:, j, :], IDB, AF.Copy, [rCON, rVEC], [rDIAG[j]], scale=wcol)
                else:
                    TS(e_, DIAG[:, j, :], IDB, wcol, None, ALU.mult, None, [rCON, rVEC], [rDIAG[j]])
            SG = [SC[:, 0:512], SC[:, 512:1024]]
            rSG = [Res("sg0"), Res("sg1")]
            gi = 0
            for b in range(4):
                tb = slice(b * 512, (b + 1) * 512)
                for c in range(4):
                    p1, rp1 = bank()
                    p2, rp2 = bank()
                    for k in range(8):
                        MM(p2[:], WG[:, k, 512 + c * 128:512 + (c + 1) * 128], UT[:, k, tb], k == 0, k == 7, [rWG, rUT[b]], [rp2])
                    for k in range(8):
                        MM(p1[:], WG[:, k, c * 128:(c + 1) * 128], UT[:, k, tb], k == 0, k == 7, [rWG, rUT[b]], [rp1])
                    i = gi % 2
                    gi += 1
                    ACT(SG[i], p2[:], AF.Sigmoid, [rp2], [rSG[i]])
                    TT("dve", AT[:, c, tb], p1[:], SG[i], ALU.mult, [rp1, rSG[i]], [rAT[c]])
            WCO = WW[:, 0:4096].rearrange("p (k n) -> p k n", k=4)
            WGO = WW[:, 4096:8192].rearrange("p (k n) -> p k n", k=4)
            d_wb2 = P.dsem("wb2")
            rWCG = Res("wcg")
            DMA("pool", d_wb2, WCO, w_co, [], [rWCG, rWG])
            DMA("pool", d_wb2, WGO, w_go, [], [rWCG])
            Y32 = [XT[:, 0:2048].rearrange("p (c n) -> p c n", c=4), XT[:, 2048:4096].rearrange("p (c n) -> p c n", c=4)]
            rY32 = [[Res("y32_%d_%d" % (i, c)) for c in range(4)] for i in range(2)]
            YB = SCB[:, 2048:4096].rearrange("p (c n) -> p c n", c=4)
            YSQ = SCB[:, 4096:6144].rearrange("p (c n) -> p c n", c=4)
            M2 = SC[:, 3072:3584]; RS2 = SC[:, 3584:4096]
            NMRb = SC[:, 0:512]
            rYB = Res("yb"); rYSQ = Res("ysq"); rM2 = Res("m2"); rRS2 = Res("rs2"); rNMR = Res("nmr")
            for b in range(4):
                tb = slice(b * 512, (b + 1) * 512)
                yb = Y32[b % 2]; ryb = rY32[b % 2]
                for c in range(4):
                    pcv, rpcv = bank()
                    mms = []
                    for kk in [15] + [k_ for k_ in range(31) if k_ != 15]:
                        sft = kk - 15
                        if c < 2:
                            lo, hi = max(0, -sft), 64 - max(0, sft)
                            o_ = pcv[:].rearrange("p (r w) -> p r w", w=64)[:, :, lo:hi]
                            r_ = AT[:, c, tb].rearrange("p (r w) -> p r w", w=64)[:, :, lo + sft:hi + sft]
                            if simcompat:
                                o_ = pcv[:, 0:8 * (hi - lo)]
                                r_ = AT[:, c, tb][:, 0:8 * (hi - lo)]
                        else:
                            r_lo, r_hi = max(8 * b, -sft), min(8 * b + 8, 32 - sft)
                            if r_lo >= r_hi:
                                continue
                            o_ = pcv[:, (r_lo - 8 * b) * 64:(r_hi - 8 * b) * 64]
                            r_ = AT[:, c, (r_lo + sft) * 64:(r_hi + sft) * 64]
                        mms.append((o_, r_, c * 31 + kk))
                    for i_, (o_, r_, j) in enumerate(mms):
                        last_ = (b == 3 and c == 3 and i_ == len(mms) - 1)
                        MM(o_, DIAG[:, j, :], r_, i_ == 0, i_ == len(mms) - 1, [rDIAG[j], rAT[c]], [rpcv] + ([rDIAGall] if last_ else []))
                    ACT(yb[:, c, :], pcv[:], AF.Identity, [rpcv, rVEC], [ryb[c]], bias=VEC[:, V_CB + c:V_CB + c + 1])
                ACT(YB, yb[:], AF.Copy, ryb, [rYB])
                ACT(YSQ, yb[:], AF.Square, ryb, [rYSQ])
                pmean, rpmean = bank()
                pmsq, rpmsq = bank()
                for c in range(4):
                    MM(pmean[:], ONES512, YB[:, c, :], c == 0, c == 3, [rONES, rYB], [rpmean])
                for c in range(4):
                    MM(pmsq[:], ONES512, YSQ[:, c, :], c == 0, c == 3, [rONES, rYSQ], [rpmsq])
                ACT(M2, pmean[:], AF.Square, [rpmean], [rM2])
                TT("dve", M2, pmsq[:], M2, ALU.subtract, [rpmsq, rM2], [rM2])
                RSTDOP(RS2, M2, RS2, rM2, rRS2, rRS2)
                STT("dve", NMRb, pmean[:], -1.0, RS2, ALU.mult, ALU.mult, [rpmean, rRS2], [rNMR])
                for c in range(4):
                    TT("dve", yb[:, c, :], yb[:, c, :], RS2, ALU.mult, [ryb[c], rRS2], [ryb[c]])
                for c in range(4):
                    TT("dve", yb[:, c, :], yb[:, c, :], NMRb, ALU.add, [ryb[c], rNMR], [ryb[c]])
                for c in range(4):
                    ACT(ST[:, c, tb], yb[:, c, :], AF.Silu, [ryb[c], rVEC], [rST[c][b]] + ([rY32all] if (b == 3 and c == 3) else []),
                        scale=VEC[:, V_LNG + c:V_LNG + c + 1], bias=VEC[:, V_LNB + c:V_LNB + c + 1])
            DMA("pool", P.dsem("wout"), R2[:].rearrange("p (k n) -> p k n", k=8), w_out, [], [rWOUT] + rAT)
            dump("st", WW[:, 24576:32768], [128, 8192], [r_ for l_ in rST for r_ in l_])

            if upto == 3:
                return
            WOUT = R2[:].rearrange("p (k n) -> p k n", k=8)
            OG = SR
            MG = [R1B[:, 0:4096].rearrange("p (k n) -> p k n", k=8), R1B[:, 4096:8192].rearrange("p (k n) -> p k n", k=8)]
            rMG = [Res("mg0"), Res("mg1")]
            S1 = [SC[:, 0:512], SC[:, 512:1024]]; S2 = [SC[:, 1024:1536], SC[:, 1536:2048]]
            M1 = [SC[:, 2048:2560], SC[:, 2560:3072]]; M2_ = [SC[:, 3072:3584], SC[:, 3584:4096]]
            rS1 = [Res(), Res()]; rS2 = [Res(), Res()]; rM1 = [Res(), Res()]; rM2b = [Res(), Res()]
            XTt = XT[:].rearrange("p (s n) -> p s n", s=4)
            rSTP = Res("stp")
            rXTt = [Res("xt%d" % i) for i in range(4)]
            d_xt = [P.dsem("xt%d" % i) for i in range(4)]
            d_hm = [P.dsem("hm%d" % i) for i in range(4)]
            rHM = [Res("hmid%d" % i) for i in range(16)]
            JUNK2 = R1B[:, 8192:9216]
            TMPY = R1[:, 6144:8192].rearrange("p (s n) -> p s n", s=2)
            rTMPY = [Res("tmpy0"), Res("tmpy1")]
            fi_ = [0]
            rst2s = [Res("stat2_%d" % b) for b in range(4)]
            rWF2 = [Res("wf2_%d" % q) for q in range(4)]
            rJ2 = Res("junk2")
            rFS = [Res("fs%d" % i) for i in range(4)]
            d_fs = [P.dsem("fs%d" % i) for i in range(4)]

            def merge(b):
                tb = slice(b * 512, (b + 1) * 512)
                mg = MG[b % 2]; rmg = rMG[b % 2]
                for fc in range(8):
                    fs = slice(fc * 128, (fc + 1) * 128)
                    pg1, rpg1 = bank(); pg2, rpg2 = bank(); pc, rpc = bank(); pgl, rpgl = bank()
                    for k in range(8):
                        MM(pg1[:], WGT[:, k, fs], UT[:, k, tb], k == 0, k == 7, [rWGT, rUT[b]], [rpg1])
                    for k in range(8):
                        MM(pg2[:], WGT[:, k, 1024 + fc * 128:1024 + (fc + 1) * 128], UT[:, k, tb], k == 0, k == 7, [rWGT, rUT[b]], [rpg2])
                    for k in range(4):
                        MM(pc[:], WCO[:, k, fs], ST[:, k, tb], k == 0, k == 3, [rWCG, rST[k][b]], [rpc])
                    for k in range(4):
                        MM(pgl[:], WGO[:, k, fs], OG[:, k, tb], k == 0, k == 3, [rWCG, rSR[b]], [rpgl])
                    i = fi_[0] % 2
                    fi_[0] += 1
                    fu = fi_[0] <= 2
                    ACT(S1[i], pg1[:], AF.Sigmoid, [rpg1], [rS1[i]] + ([rNMR, rSG[0], rSG[1]] if fu else []))
                    ACT(S2[i], pg2[:], AF.Sigmoid, [rpg2], [rS2[i]] + ([rYB] if fu else []))
                    TT("dve", M1[i], pc[:], S1[i], ALU.mult, [rpc, rS1[i]], [rM1[i]] + ([rYSQ] if fu else []))
                    TT("dve", M2_[i], pgl[:], S2[i], ALU.mult, [rpgl, rS2[i]], [rM2b[i]] + ([rM2, rRS2] if fu else []))
                    TT("pool", mg[:, fc, :], M1[i], M2_[i], ALU.add, [rM1[i], rM2b[i], rDIAGall], [rmg])

            def wout(b):
                mg = MG[b % 2]; rmg = rMG[b % 2]
                rst2 = rst2s[b]
                for t in range(4):
                    tile = b * 4 + t
                    tk = slice(t * 128, (t + 1) * 128)
                    xs = XTt[:, t, :]
                    DMA("sp", d_xt[t], xs, x_d[tile * 128:(tile + 1) * 128, :], [rY32all], [rXTt[t]])
                    py = []
                    for half in range(2):
                        p_, rp_ = bank()
                        for k in range(8):
                            MM(p_[:], mg[:, k, tk], WOUT[:, k, half * 512:(half + 1) * 512], k == 0, k == 7, [rmg, rWOUT], [rp_])
                        py.append((p_, rp_))
                    rs_ = rSTP
                    for half in range(2):
                        ACT(JUNK2[:, 0:512], py[half][0][:], AF.Square, [py[half][1], rDIAGall], [rs_, rJ2], scale=1.0 / 32,
                            accum_out=TMP8[:, 8 + half:9 + half])
                    TT("dve", TMP8[:, 10:11], TMP8[:, 8:9], TMP8[:, 9:10], ALU.add, [rs_], [rs_])
                    RSTDOP(TMP8[:, 12:13], TMP8[:, 10:11], TMP8[:, 11:12], rs_, rs_, rs_)
                    ty = TMPY[:, t % 2, :]; rty = rTMPY[t % 2]
                    for half in range(2):
                        hs = slice(half * 512, (half + 1) * 512)
                        STT("dve", ty[:, hs], py[half][0][:], TMP8[:, 12:13], GG[:, hs], ALU.mult, ALU.mult,
                            [py[half][1], rs_, rGG[0], rDIAGall], [rty])
                    TT("dve", xs, xs, ty, ALU.add, [rXTt[t], rty], [rXTt[t]])
                    DMA("sp", d_hm[t], hmid_d[tile * 128:(tile + 1) * 128, :], xs, [rXTt[t]], [rHM[tile]])
                    ACT(JUNK2, xs, AF.Square, [rXTt[t]], [rst2, rJ2], scale=1.0 / 32, accum_out=TMP8[:, 16 + t:17 + t])

            def norm2T(b):
                tb = slice(b * 512, (b + 1) * 512)
                rst2 = rst2s[b]
                RSTDOP(TMP8[:, 24:28], TMP8[:, 16:20], TMP8[:, 20:24], rst2, rst2, rst2)
                for t in range(4):
                    xs = XTt[:, t, :]
                    TS("dve", xs, xs, TMP8[:, 24 + t:25 + t], None, ALU.mult, None, [rst2, rXTt[t]], [rXTt[t]])
                for c in range(8):
                    pb, rpb = bank()
                    for t in range(4):
                        TR(pb[:, t * 128:(t + 1) * 128], XTt[:, t, c * 128:(c + 1) * 128], IDF[:], [rXTt[t], rIDF], [rpb])
                    ACT(UT[:, c, tb], pb[:], AF.Identity, [rpb, rSM, rMOD2], [rUT[b]],
                        scale=GS[:, 16 + c:17 + c], bias=MOD[:, 24 + c:25 + c])

            WF2 = WW[:].rearrange("p (k n) -> p k n", k=32)
            R3f = R3[:].rearrange("p c n -> p (c n)")
            FS = [R3f[:, 0:4096].rearrange("p (k n) -> p k n", k=8), R3f[:, 4096:8192].rearrange("p (k n) -> p k n", k=8),
                  R2[:, 0:4096].rearrange("p (k n) -> p k n", k=8), R2[:, 4096:8192].rearrange("p (k n) -> p k n", k=8)]
            fs_old = [list(rSR), list(rSR), [rWOUT], [rWOUT]]
            HID = R1B.rearrange("p (k n) -> p k n", k=32)
            rHID = Res("hid")
            RL = [SC[:, 0:512], SC[:, 512:1024]]
            rRL = [Res(), Res()]
            JUNK3 = SCB[:, 2048:2560]
            TY2 = SC[:, 2048:4096].rearrange("p (s n) -> p s n", s=2)
            rTY2 = [Res(), Res()]
            d_hl = [P.dsem("hl%d" % i) for i in range(4)]
            d_o = [P.dsem("o%d" % i) for i in range(4)]
            cnt = {"ri": 0, "si": 0}
            ff1_done = set()

            def ff1_slab(b, s_):
                tb = slice(b * 512, (b + 1) * 512)
                f = cnt["si"] % 4
                cnt["si"] += 1
                ff1_done.add((b, s_))
                if not (b == 0 and s_ < 2):
                    DMA("pool", d_fs[f], FS[f], w_ff1[:, :, s_ * 512:(s_ + 1) * 512], rCV1, [rFS[f]] + fs_old[f])
                for oc in range(4):
                    ph, rph = bank()
                    for k in range(8):
                        MM(ph[:], FS[f][:, k, oc * 128:(oc + 1) * 128], UT[:, k, tb], k == 0, k == 7, [rFS[f], rUT[b]], [rph])
                    i = cnt["ri"] % 2
                    cnt["ri"] += 1
                    ACT(RL[i], ph[:], AF.Relu, [rph], [rRL[i]] + rS1)
                    STT("dve", HID[:, s_ * 4 + oc, :], ph[:], 0.0, RL[i], ALU.max, ALU.mult, [rph, rRL[i]],
                        [rHID] + (rMG + rTMPY + [rJ2] if b == 0 else []))

            def ff2_block(b):
                for t in range(4):
                    tile = b * 4 + t
                    tk = slice(t * 128, (t + 1) * 128)
                    hs_ = XTt[:, t, :]
                    DMA("sp", d_hl[t], hs_, hmid_d[tile * 128:(tile + 1) * 128, :], [rHM[tile]], [rXTt[t]])
                    py = []
                    for half in range(2):
                        p_, rp_ = bank()
                        for k in range(32):
                            MM(p_[:], HID[:, k, tk], WF2[:, k, half * 512:(half + 1) * 512], k == 0, k == 31,
                               [rHID, rWF2[k // 8]], [rp_])
                        py.append((p_, rp_))
                    rs_ = rSTP
                    for half in range(2):
                        ACT(JUNK3, py[half][0][:], AF.Square, [py[half][1]], [rs_], scale=1.0 / 32,
                            accum_out=TMP8[:, 8 + half:9 + half])
                    TT("dve", TMP8[:, 10:11], TMP8[:, 8:9], TMP8[:, 9:10], ALU.add, [rs_], [rs_])
                    RSTDOP(TMP8[:, 12:13], TMP8[:, 10:11], TMP8[:, 11:12], rs_, rs_, rs_)
                    ty = TY2[:, t % 2, :]; rty = rTY2[t % 2]
                    for half in range(2):
                        hs = slice(half * 512, (half + 1) * 512)
                        STT("dve", ty[:, hs], py[half][0][:], TMP8[:, 12:13], GG[:, 1024 + half * 512:1024 + (half + 1) * 512],
                            ALU.mult, ALU.mult, [py[half][1], rs_, rGG[1]], [rty] + (rM1 + rM2b if b == 0 else []))
                    TT("dve", hs_, hs_, ty, ALU.add, [rXTt[t], rty], [rXTt[t]])
                    OUT_EVS.append(DMA("sp", d_o[t], out_d[tile * 128:(tile + 1) * 128, :], hs_, [rXTt[t]], [Res()]))

            merge(0)
            wout(0)
            for b in range(1, 4):
                merge(b)
                norm2T(b - 1)
                if b == 3:
                    WF2 = WW[:].rearrange("p (k n) -> p k n", k=32)
                    old = [[rWCG], [rWGT], [rWGT], [r_ for l_ in rST for r_ in l_]]
                    R3f_ = R3[:].rearrange("p c n -> p (c n)")
                    for f_ in range(2):
                        DMA("pool", d_fs[f_], R3f_[:, f_ * 4096:(f_ + 1) * 4096].rearrange("p (k n) -> p k n", k=8),
                            w_ff1[:, :, f_ * 512:(f_ + 1) * 512], rCV1, [rFS[f_]] + list(rSR))
                    for q in range(4):
                        DMA("pool", P.dsem("wf2_%d" % q), WF2[:, q * 8:(q + 1) * 8, :], w_ff2[:, q * 8:(q + 1) * 8, :], [rCV2[q]],
                            [rWF2[q]] + old[q])
                wout(b)
            ff1_slab(0, 0)
            ff1_slab(0, 1)
            norm2T(3)
            dump("u2", UT[:], [128, 8, 2048], rUT)

            if upto == 4:
                return
            for b in range(4):
                for s_ in range(8):
                    if (b, s_) not in ff1_done:
                        ff1_slab(b, s_)
                ff2_block(b)

        body()
        allev = list(OUT_EVS) + [ds.last for ds in dbg_out.values()]
        P.wait_all("sp", allev)
        P.emit(st)
        build.stats = P.stats
    return nc


def _consts():
    ident = np.eye(128, dtype=np.float32)
    j = np.arange(128)[:, None]
    i = np.arange(128)[None, :]
    mu = (j <= i).astype(np.float32)
    ml = (j >= i).astype(np.float32)
    return np.ascontiguousarray(np.concatenate([ident, mu, mu, ml, ml, np.tile(ml, (1, 4)), mu, ml], axis=1))


def _pack(inputs, b):
    f = np.float32
    c = np.asarray(inputs["c"], f)[b]
    cc = np.asarray(inputs["c_ctx"], f)
    vecs = np.zeros((128, NVEC), f)
    cv = np.stack([c.reshape(8, 128), cc.reshape(8, 128)], axis=-1)
    vecs[:, V_C:V_C + 16] = cv.transpose(1, 0, 2).reshape(128, 16)
    vecs[:, V_BMOD:V_BMOD + 48] = np.asarray(inputs["b_mod"], f)[0].reshape(48, 128).T
    vecs[:, V_GPRE1:V_GPRE1 + 8] = np.asarray(inputs["g_pre1"], f)[0].reshape(8, 128).T
    vecs[:, V_GPRE2:V_GPRE2 + 8] = np.asarray(inputs["g_pre2"], f)[0].reshape(8, 128).T
    cw = np.asarray(inputs["conv_w"], f)[0]
    vecs[:, V_CW:V_CW + 124] = cw.reshape(31, 4, 128).transpose(2, 1, 0).reshape(128, 124)
    vecs[:, V_CB:V_CB + 4] = np.asarray(inputs["conv_b"], f)[0].reshape(4, 128).T
    vecs[:, V_LNG:V_LNG + 4] = np.asarray(inputs["conv_ln_g"], f)[0].reshape(4, 128).T
    vecs[:, V_LNB:V_LNB + 4] = np.asarray(inputs["conv_ln_b"], f)[0].reshape(4, 128).T
    vecs[:, V_GN] = np.asarray(inputs["gla_norm_g"], f)[0]
    return vecs


def _shared(inputs):
    f = np.float32
    gpost = np.concatenate([np.asarray(inputs["g_post1"], f)[0], np.asarray(inputs["g_post2"], f)[0]])
    gpost = np.ascontiguousarray(np.broadcast_to(gpost[None, :], (128, 2048)))
    wd = np.asarray(inputs["w_decay"], f)[0]
    bd = np.asarray(inputs["b_decay"], f)[0]
    wda = np.zeros((33, 512), f)
    wda[0:16, 0:256] = wd[0]
    wda[16:32, 256:512] = wd[1]
    wda[32, 0:256] = bd[0]
    wda[32, 256:512] = bd[1]
    sh = {"gpost": gpost, "wda": wda, "consts": _consts()}
    for k in ("w_mod", "w_in", "w_conv_out", "w_gla_out", "w_out", "w_ff1", "w_ff2"):
        sh[k] = np.ascontiguousarray(np.asarray(inputs[k], f)[0])
    return sh


_NC_CACHE = {}


def kernel(**inputs):
    if "nc" not in _NC_CACHE:
        _NC_CACHE["nc"] = build()
    nc = _NC_CACHE["nc"]
    sh = _shared(inputs)
    x = np.asarray(inputs["x"], np.float32)
    ctx = np.asarray(inputs["ctx"], np.float32)
    in_maps = []
    for b in range(8):
        m = dict(sh)
        m["x"] = np.ascontiguousarray(x[b])
        m["ctx"] = np.ascontiguousarray(ctx[b])
        m["vecs"] = _pack(inputs, b)
        in_maps.append(m)
    res = run_bass_kernel_spmd(nc, in_maps, core_ids=list(range(8)))
    return np.stack([np.asarray(r["out"], np.float32) for r in res.results], axis=0)
```

```python
import contextlib
import math
import numpy as np
import concourse.bass as bass
import concourse.mybir as mybir
from concourse.bass_utils import run_bass_kernel_spmd

F32 = mybir.dt.float32
BF16 = mybir.dt.bfloat16
AF = mybir.ActivationFunctionType
ALU = mybir.AluOpType

COMPUTE = ("pe", "act", "dve", "pool")
QUEUES = ("pe", "act", "dve", "pool", "sp")


class Res:
    __slots__ = ("name", "w", "r")

    def __init__(self, name=""):
        self.name = name
        self.w = None
        self.r = []


class DSem:
    __slots__ = ("sem", "count", "key", "last")

    def __init__(self, key):
        self.sem = None
        self.count = 0
        self.key = key
        self.last = None


class Ev:
    __slots__ = ("key", "val", "clock", "op", "dsem")


class Op:
    __slots__ = ("eng", "fn", "idx", "sig", "sigval", "waits", "ev", "dsem")


class Prog:
    def __init__(self, nc):
        self.nc = nc
        self.ops = {e: [] for e in QUEUES}
        self.clock = {e: {} for e in QUEUES}
        self.dsems = []
        self.last = {}

    def dsem(self, name):
        d = DSem("D%d:%s" % (len(self.dsems), name))
        self.dsems.append((name, d))
        return d

    @staticmethod
    def _gather(reads, writes):
        deps = []
        for r in reads:
            if r.w is not None:
                deps.append(r.w)
        for w in writes:
            if w.w is not None:
                deps.append(w.w)
            deps.extend(w.r)
        return deps

    def _apply_waits(self, eng, op, deps):
        clk = self.clock[eng]
        need = {}
        for d in deps:
            same = d.op is not None and d.op.eng == eng
            if same and eng in ("pe", "sp"):
                continue
            if clk.get(d.key, -1) >= d.val:
                continue
            cur = need.get(d.key)
            if cur is None or d.val > cur.val:
                need[d.key] = d
        for d in need.values():
            if d.op is not None:
                d.op.sig = True
        for d in deps:
            for k, v in d.clock.items():
                if clk.get(k, -1) < v:
                    clk[k] = v
        op.waits = list(need.values())

    def _new(self, eng, fn):
        o = Op()
        o.eng = eng
        o.fn = fn
        o.idx = len(self.ops[eng])
        o.sig = False
        o.sigval = None
        o.dsem = None
        o.waits = []
        return o

    def op(self, eng, fn, reads=(), writes=()):
        o = self._new(eng, fn)
        self._apply_waits(eng, o, self._gather(reads, writes))
        ev = Ev()
        ev.key = eng
        ev.val = o.idx
        ev.op = o
        ev.dsem = None
        clk = self.clock[eng]
        if eng in ("pe", "sp"):
            clk[eng] = o.idx
        ev.clock = dict(clk)
        ev.clock[eng] = o.idx
        o.ev = ev
        for r in reads:
            r.r.append(ev)
        for w in writes:
            w.w = ev
            w.r = []
        self.ops[eng].append(o)
        self.last[eng] = ev
        return ev

    def dma(self, queue, dsem, fn, reads=(), writes=()):
        o = self._new(queue, fn)
        o.dsem = dsem
        self._apply_waits(queue, o, self._gather(reads, writes))
        dsem.count += 16
        ev = Ev()
        ev.key = dsem.key
        ev.val = dsem.count
        ev.op = None
        ev.dsem = dsem
        ev.clock = dict(self.clock[queue])
        ev.clock[dsem.key] = dsem.count
        o.ev = ev
        dsem.last = ev
        for r in reads:
            r.r.append(ev)
        for w in writes:
            w.w = ev
            w.r = []
        self.ops[queue].append(o)
        return ev

    def wait_all(self, eng, evs):
        o = self._new(eng, None)
        self._apply_waits(eng, o, list(evs))
        self.ops[eng].append(o)

    def barrier(self):
        evs = [self.last[e] for e in COMPUTE if e in self.last]
        evs += [d.last for _, d in self.dsems if d.last is not None]
        for e in QUEUES:
            self.wait_all(e, evs)

    def emit(self, stack):
        nc = self.nc
        esem = {}
        for e in COMPUTE:
            esem[e] = stack.enter_context(nc.semaphore("s_" + e))
        for name, d in self.dsems:
            d.sem = stack.enter_context(nc.semaphore("d_" + name))
        for e in QUEUES:
            c = 0
            for o in self.ops[e]:
                if o.sig:
                    c += 1
                    o.sigval = c
        self.stats = {e: (len(self.ops[e]), sum(1 for o in self.ops[e] if o.sig),
                          sum(len(o.waits) for o in self.ops[e])) for e in QUEUES}

        def run(e, engobj):
            for o in self.ops[e]:
                for d in o.waits:
                    if d.op is not None:
                        engobj.wait_ge(esem[d.op.eng], d.op.sigval)
                    else:
                        engobj.wait_ge(d.dsem.sem, d.val)
                if o.fn is None:
                    continue
                inst = o.fn(engobj)
                if o.dsem is not None:
                    inst.then_inc(o.dsem.sem, 16)
                elif o.sig:
                    inst.then_inc(esem[e], 1)

        block = stack.enter_context(nc.Block())

        @block.sync
        def _(eng):
            run("sp", eng)

        @block.gpsimd
        def _(eng):
            run("pool", eng)

        @block.scalar
        def _(eng):
            run("act", eng)

        @block.vector
        def _(eng):
            run("dve", eng)

        @block.tensor
        def _(eng):
            run("pe", eng)


D = 1024
T = 2048
TC = 256
NT = 16
EPS = 1e-6
COL_Q = 1024
COL_K = 1280
COL_V = 1536
COL_R = 2048
COL_DEC = 2560
COL_GATE = 2592
NVEC = 224
V_C = 0
V_BMOD = 16
V_GPRE1 = 64
V_GPRE2 = 72
V_CW = 80
V_CB = 204
V_LNG = 208
V_LNB = 212
V_GN = 216
LN8 = math.log(0.125)


def build(dbg=None, upto=9, simcompat=False):
    nc = bass.Bass("TRN2", target_bir_lowering=False)

    def din(name, shape):
        return nc.dram_tensor(name, shape, F32, kind="ExternalInput").ap()

    x_d = din("x", [T, D])
    ctx_d = din("ctx", [TC, D])
    vecs_d = din("vecs", [128, NVEC])
    gpost_d = din("gpost", [128, 2048])
    wda_d = din("wda", [33, 512])
    consts_d = din("consts", [128, 1408])
    w_mod = din("w_mod", [D, 6144]).rearrange("(k p) n -> p k n", p=128)
    w_in = din("w_in", [D, 4640]).rearrange("(k p) n -> p k n", p=128)
    w_co = din("w_conv_out", [512, D]).rearrange("(k p) n -> p k n", p=128)
    w_go = din("w_gla_out", [512, D]).rearrange("(k p) n -> p k n", p=128)
    w_out = din("w_out", [D, D]).rearrange("(k p) n -> p k n", p=128)
    w_ff1_raw = din("w_ff1", [D, 4096])
    w_ff2_raw = din("w_ff2", [4096, D])
    wf1b_raw = nc.dram_tensor("wf1b", [D, 4096], BF16, kind="Internal").ap()
    wf2b_raw = nc.dram_tensor("wf2b", [4096, D], BF16, kind="Internal").ap()
    w_ff1 = wf1b_raw.rearrange("(k p) n -> p k n", p=128)
    w_ff2 = wf2b_raw.rearrange("(k p) n -> p k n", p=128)
    out_d = nc.dram_tensor("out", [T, D], F32, kind="ExternalOutput").ap()
    hmid_d = nc.dram_tensor("hmid", [T, D], F32, kind="Internal").ap()
    dbg_out = {}

    with contextlib.ExitStack() as st:
        def sb(name, shape, dt):
            return st.enter_context(nc.sbuf_tensor(name, shape, dt))

        P = Prog(nc)
        UT = sb("UT", [128, 8, 2048], BF16)
        WW = sb("WW", [128, 32768], BF16)
        R1 = sb("R1", [128, 8192], F32)
        R2 = sb("R2", [128, 8192], BF16)
        R3 = sb("R3", [128, 4, 2048], BF16)
        XT = sb("XT", [128, 4096], F32)
        GG = sb("GG", [128, 2048], F32)
        SC = sb("SC", [128, 4096], F32)
        VEC = sb("VEC", [128, NVEC], F32)
        SM = sb("SM", [128, 256], F32)
        IDF = sb("IDF", [128, 128], F32)
        ONESF = sb("ONESF", [128, 128], F32)
        CB = sb("CB", [128, 1408], BF16)
        WDA = sb("WDA", [33, 512], BF16)
        SCV = sb("SCV", [128, 16], BF16)

        MOD = SM[:, 0:48]
        CMOD = SM[:, 48:64]
        GS = SM[:, 64:88]
        STAT = SM[:, 88:120]
        RSTD = SM[:, 120:152]
        EL = SM[:, 152:224].rearrange("p (c n) -> p c n", c=4)
        TMP8 = SM[:, 224:256]
        IDB = CB[:, 0:128]
        MU4 = CB[:, 128:640]
        ML4 = CB[:, 640:1152]
        TRIU = CB[:, 1152:1280]
        TRIL = CB[:, 1280:1408]

        ONES = sb("ONES", [128, 256], BF16)
        ONESB = ONES[:, 0:128]
        ONES512 = ONES[:, 128:256]

        SCB = SC[:].bitcast(BF16)
        XTB = XT[:].bitcast(BF16)
        R1B = R1[:].bitcast(BF16)

        PSF = [st.enter_context(nc.psum_tensor("ps%d" % i, [128, 512], F32)) for i in range(7)]
        PSFR = [Res("ps%d" % i) for i in range(7)]
        PSB = st.enter_context(nc.psum_tensor("psb", [128, 1024], BF16))
        PSBR = Res("psb")
        bank_i = [0]

        nbanks = [7]
        PSF8 = PSB[:].bitcast(F32)

        class _BankView:
            def __init__(self, ap):
                self.ap = ap

            def __getitem__(self, key):
                return self.ap[key]

        def bank():
            i = bank_i[0] % nbanks[0]
            bank_i[0] += 1
            if i == 7:
                assert PSBR.w is None or len(PSBR.r) > 0
                return _BankView(PSF8), PSBR
            assert PSFR[i].w is None or len(PSFR[i].r) > 0, "PSUM bank %d re-issued before its evacuation was emitted" % i
            return PSF[i], PSFR[i]

        def ACT(out, in_, func, reads, writes, **kw):
            return P.op("act", lambda e: e.activation(out=out, in_=in_, func=func, **kw), reads, writes)

        def MM(out, lhsT, rhs, start, stop, reads, writes, tile_position=None):
            if tile_position is not None:
                return P.op("pe", lambda e: e.matmul(out, lhsT=lhsT, rhs=rhs, start=start, stop=stop,
                                                     tile_position=tile_position), reads, writes)
            return P.op("pe", lambda e: e.matmul(out, lhsT=lhsT, rhs=rhs, start=start, stop=stop), reads, writes)

        def TR(out, in_, ident, reads, writes):
            return P.op("pe", lambda e: e.transpose(out=out, in_=in_, identity=ident), reads, writes)

        def TT(eng, out, in0, in1, op, reads, writes):
            return P.op(eng, lambda e: e.tensor_tensor(out=out, in0=in0, in1=in1, op=op), reads, writes)

        def TS(eng, out, in0, s1, s2, op0, op1, reads, writes):
            if s2 is None:
                return P.op(eng, lambda e: e.tensor_scalar(out=out, in0=in0, scalar1=s1, scalar2=None, op0=op0), reads, writes)
            return P.op(eng, lambda e: e.tensor_scalar(out=out, in0=in0, scalar1=s1, scalar2=s2, op0=op0, op1=op1), reads, writes)

        def STT(eng, out, in0, scalar, in1, op0, op1, reads, writes):
            return P.op(eng, lambda e: e.scalar_tensor_tensor(out=out, in0=in0, scalar=scalar, in1=in1, op0=op0, op1=op1), reads, writes)

        def CP(eng, out, in_, reads, writes):
            return P.op(eng, lambda e: e.tensor_copy(out=out, in_=in_), reads, writes)

        def MSET(eng, ap, val, writes):
            return P.op(eng, lambda e: e.memset(ap, val), (), writes)

        def DMA(q, ds, out, in_, reads, writes):
            return P.dma(q, ds, lambda e: e.dma_start(out=out, in_=in_), reads, writes)

        def RSTDOP(out, in_, tmp, rin, rtmp, rout):
            ACT(tmp, in_, AF.Ln, [rin], [rtmp], bias=EPS)
            ACT(out, tmp, AF.Exp, [rtmp], [rout], scale=-0.5)

        def dump(name, ap, shape, reads):
            if dbg is None or name not in dbg:
                return
            t = nc.dram_tensor("dbg_" + name, list(shape), ap.dtype, kind="ExternalOutput").ap()
            ds = P.dsem("dbg_" + name)
            DMA("sp", ds, t, ap, reads, [Res()])
            dbg_out[name] = ds

        rMOD2 = Res("mod2"); rVEC = Res("vec"); rCON = Res("con"); rGG = [Res("gg0"), Res("gg1")]
        rSM = Res("sm"); rMOD = Res("mod"); rSCV = Res("scv"); rWDA = Res("wda")
        rUT = [Res("ut%d" % i) for i in range(4)]
        rCUT = Res("cut")
        rWW = [Res("ww%d" % i) for i in range(4)]
        rONES = Res("ones")

        rIDF = Res("idf")
        DMA("sp", P.dsem("c_vec"), VEC[:], vecs_d, [], [rVEC])
        DMA("sp", P.dsem("c_idf"), IDF[:], consts_d[:, 0:128], [], [rIDF])
        DMA("pool", P.dsem("c_cb"), CB[:, 0:1408], consts_d[:, 0:1408], [], [rCON])
        DMA("pool", P.dsem("c_wda"), WDA[:], wda_d, [], [rWDA])
        MSET("dve", ONESF[:], 1.0, [rONES])
        MSET("dve", ONES[:, 0:128], 1.0 / 128, [rONES])
        MSET("dve", ONES[:, 128:256], 1.0 / 512, [rONES])

        OUT_EVS = []

        def body():
            ACT(SCV[:], VEC[:, V_C:V_C + 16], AF.Silu, [rVEC], [rSCV])
            d_wm = [P.dsem("wm%d" % i) for i in range(4)]
            rWM = rWW
            WMv = [WW[:, i * 8192:(i + 1) * 8192].rearrange("p (k n) -> p k n", k=8) for i in range(4)]
            psm, rpsm = bank()
            for j in (0, 1):
                DMA("pool", d_wm[2 + j], WMv[2 + j], w_mod[:, :, j * 1024:(j + 1) * 1024], [], [rWM[2 + j]])
            for j in (0, 1):
                for ccl in range(8):
                    cc = j * 8 + ccl
                    for k in range(8):
                        MM(psm[:, cc * 2:cc * 2 + 2], WMv[2 + j][:, k, ccl * 128:(ccl + 1) * 128], SCV[:, 2 * k:2 * k + 2],
                           k == 0, k == 7, [rWM[2 + j], rSCV], [rpsm])
            psm3 = psm[:, 0:96].rearrange("p (c j) -> p c j", j=2)
            TT("dve", MOD[:, 0:16], psm3[:, 0:16, 0], VEC[:, V_BMOD:V_BMOD + 16], ALU.add, [rpsm, rVEC], [rMOD])
            TT("dve", CMOD, psm3[:, 0:16, 1], VEC[:, V_BMOD:V_BMOD + 16], ALU.add, [rpsm, rVEC], [rMOD])
            STT("dve", GS[:, 0:8], MOD[:, 8:16], 1.0, VEC[:, V_GPRE1:V_GPRE1 + 8], ALU.add, ALU.mult, [rMOD, rVEC], [rSM])
            STT("dve", GS[:, 8:16], CMOD[:, 8:16], 1.0, VEC[:, V_GPRE1:V_GPRE1 + 8], ALU.add, ALU.mult, [rMOD, rVEC], [rSM])

            WQK = WW[:, 0:4096].rearrange("p (k n) -> p k n", k=8)
            WDEC = WW[:, 4096:4352].rearrange("p (k n) -> p k n", k=8)
            WV = WW[:, 4352:8448].rearrange("p (k n) -> p k n", k=8)
            WR = WW[:, 8448:12544].rearrange("p (k n) -> p k n", k=8)
            d_wa = P.dsem("wa1")
            rWA = Res("wa1")
            DMA("pool", d_wa, WDEC, w_in[:, :, COL_DEC:COL_DEC + 32], [], [rWA])
            DMA("pool", d_wa, WQK, w_in[:, :, COL_Q:COL_Q + 512], [], [rWA])
            DMA("pool", d_wa, WV, w_in[:, :, COL_V:COL_V + 512], [], [rWA])
            rWR = Res("wr")
            d_wr = P.dsem("wr")

            GGB = GG[:].bitcast(BF16).rearrange("p (k n) -> p k n", k=8)
            rGGB = Res("ggb")
            d_gb = P.dsem("ggb")

            rCV1 = [Res("cv1_%d" % i) for i in range(4)]
            rCV2 = [Res("cv2_%d" % i) for i in range(4)]

            def adaln_piece(i):
                col0 = 2048 + i * 512
                DMA("pool", d_gb, GGB, w_mod[:, :, col0:col0 + 512], [], [rGGB])
                if i >= 4:
                    j = i - 4
                    DMA("pool", P.dsem("cv1_%d" % j), wf1b_raw[j * 256:(j + 1) * 256, :], w_ff1_raw[j * 256:(j + 1) * 256, :], [], [rCV1[j]])
                pp, rpp = bank()
                for ccl in range(4):
                    for k in range(8):
                        MM(pp[:, ccl * 2:ccl * 2 + 2], GGB[:, k, ccl * 128:(ccl + 1) * 128], SCV[:, 2 * k:2 * k + 2],
                           k == 0, k == 7, [rGGB, rSCV], [rpp])
                cc0 = 16 + i * 4
                pp3 = pp[:, 0:8].rearrange("p (c j) -> p c j", j=2)
                TT("dve", MOD[:, cc0:cc0 + 4], pp3[:, :, 0], VEC[:, V_BMOD + cc0:V_BMOD + cc0 + 4], ALU.add, [rpp, rVEC], [rMOD2])

            def adaln_finish():
                STT("dve", GS[:, 16:24], MOD[:, 32:40], 1.0, VEC[:, V_GPRE2:V_GPRE2 + 8], ALU.add, ALU.mult, [rMOD2, rVEC], [rMOD2])
                DMA("sp", P.dsem("c_gg"), GG[:], gpost_d, [], rGG + [rGGB])
                DG = SC[:, 0:1024].rearrange("p (c n) -> p c n", c=8)
                rDG = Res("dg")
                for g in range(2):
                    gt0 = 16 if g == 0 else 40
                    for c in range(8):
                        TS("dve", DG[:, c, :], IDF[:], MOD[:, gt0 + c:gt0 + c + 1], None, ALU.mult, None, [rIDF, rMOD2], [rDG])
                    for half in range(2):
                        pg, rpg = bank()
                        for c4 in range(4):
                            c = half * 4 + c4
                            MM(pg[:, c4 * 128:(c4 + 1) * 128], ONESF[:], DG[:, c, :], True, True, [rONES, rDG], [rpg])
                        sl = GG[:, g * 1024 + half * 512:g * 1024 + (half + 1) * 512]
                        TT("dve", sl, pg[:], sl, ALU.mult, [rpg, rGG[g]], [rGG[g]])

            if upto == 0:
                return
            XS = R1[:].rearrange("p (s n) -> p s n", s=8)
            rXS = [Res("xs%d" % i) for i in range(8)]
            d_xs = [P.dsem("xs%d" % i) for i in range(8)]
            JUNK = SCB[:, 4096:5120]
            JUNKD = XTB[:, 6400:7424]
            CUT = XTB[:, 0:2048].rearrange("p (k n) -> p k n", k=8)
            blocks = [("c", 0, 2)] + [("x", b, 4) for b in range(4)]
            slot_ctr = [0]
            blk_slots = {}
            blk_rst = {}

            def p1_stats(bi):
                kind, bidx, ntl = blocks[bi]
                slots = []
                for t in range(ntl):
                    s_ = slot_ctr[0] % 8
                    slot_ctr[0] += 1
                    slots.append(s_)
                    src = ctx_d[t * 128:(t + 1) * 128, :] if kind == "c" else x_d[(bidx * 4 + t) * 128:(bidx * 4 + t + 1) * 128, :]
                    DMA("sp", d_xs[s_], XS[:, s_, :], src, [], [rXS[s_]])
                st0 = 0 if kind == "c" else 2 + bidx * 4
                rst = Res("stat")
                for t, s_ in enumerate(slots):
                    if kind == "c":
                        ACT(JUNK, XS[:, s_, :], AF.Square, [rXS[s_]], [rst], scale=1.0 / 32, accum_out=STAT[:, st0 + t:st0 + t + 1])
                    else:
                        P.op("dve", lambda e, s_=s_, cc_=st0 + t: e.scalar_tensor_tensor(
                            out=JUNKD, in0=XS[:, s_, :], scalar=1.0 / 1024, in1=XS[:, s_, :], op0=ALU.mult, op1=ALU.mult,
                            accum_out=STAT[:, cc_:cc_ + 1]), [rXS[s_]], [rst])
                RSTDOP(RSTD[:, st0:st0 + ntl], STAT[:, st0:st0 + ntl], TMP8[:, 0:ntl], rst, rst, rst)
                for t, s_ in enumerate(slots):
                    TS("dve", XS[:, s_, :], XS[:, s_, :], RSTD[:, st0 + t:st0 + t + 1], None, ALU.mult, None, [rst, rXS[s_]], [rXS[s_]])
                blk_slots[bi] = slots

            def p1_transpose(bi):
                kind, bidx, ntl = blocks[bi]
                slots = blk_slots[bi]
                for c in range(8):
                    pb, rpb = bank()
                    for t, s_ in enumerate(slots):
                        TR(pb[:, t * 128:(t + 1) * 128], XS[:, s_, c * 128:(c + 1) * 128], IDF[:], [rXS[s_], rIDF], [rpb])
                    if kind == "c":
                        ACT(CUT[:, c, :], pb[:, 0:256], AF.Identity, [rpb, rSM, rMOD], [rCUT],
                            scale=GS[:, 8 + c:9 + c], bias=CMOD[:, c:c + 1])
                    else:
                        ACT(UT[:, c, bidx * 512:(bidx + 1) * 512], pb[:], AF.Identity, [rpb, rSM, rMOD], [rUT[bidx]],
                            scale=GS[:, c:c + 1], bias=MOD[:, c:c + 1])

            p1_stats(0)
            for bi in range(5):
                if bi + 1 < 5:
                    p1_stats(bi + 1)
                p1_transpose(bi)
            dump("ut", UT[:], [128, 8, 2048], rUT)

            if upto == 1:
                return
            LA = WW[:, 12544:14592].rearrange("p (t n) -> p t n", t=4)
            CK = WW[:, 14592:15616].rearrange("p (c n) -> p c n", c=4)
            QK = WW[:, 16384:32768].rearrange("p (c n) -> p c n", c=8)

            ZT = XTB[0:33, 2048:4352]
            CKT = XTB[:, 4352:5376].rearrange("p (c n d) -> p c n d", c=4, n=2)
            CV = XTB[:, 5376:6400].rearrange("p (n d) -> p n d", n=2)
            KT = R1B[:, 0:8192].rearrange("p (c n d) -> p c n d", c=4, n=16)
            SS = R1B[:, 8192:16384].rearrange("p (c n d) -> p c n d", c=4, n=16)
            V = R2[:].rearrange("p (n d) -> p n d", n=16)
            SR = R3
            E_ = [SC[:, 0:512], SC[:, 512:1024]]
            EN_ = [SC[:, 1024:1536], SC[:, 1536:2048]]
            EQ_ = [SC[:, 2048:2560], SC[:, 2560:3072]]
            RR = SC[:, 3072:4096].rearrange("p (c b d) -> p c b d", c=4, b=2)
            rE = [Res("e0"), Res("e1")]
            rEN = [Res("en0"), Res("en1")]
            rEQ = [Res("eq0"), Res("eq1")]
            rZT = Res("zt"); rLA = Res("la"); rEL = Res("el"); rCK = Res("ck"); rCKT = Res("ckt"); rCV = Res("cv")
            rQK = [Res("qk%d" % i) for i in range(4)]
            rKT = [Res("kt%d" % i) for i in range(4)]
            rV = [Res("v%d" % i) for i in range(4)]
            rSR = [Res("sr%d" % i) for i in range(4)]
            MSET("dve", XTB[32:33, 2048:4352], 1.0, [rZT])
            ei = [0]

            def blk_params(bi):
                kind, bidx, ntl = blocks[bi]
                ntok = ntl * 128
                if kind == "c":
                    return kind, bidx, ntl, ntok, CUT, rCUT, 0, 0, 0
                return (kind, bidx, ntl, ntok, UT[:, :, bidx * 512:(bidx + 1) * 512], rUT[bidx], bidx * 512,
                        256 + bidx * 512, 2 + bidx * 4)

            def a1_s1(bi):
                kind, bidx, ntl, ntok, U, rU, tok0, zoff, ch0 = blk_params(bi)
                pz, rpz = bank()
                for k in range(8):
                    MM(pz[0:32, 0:ntok], WDEC[:, k, :], U[:, k, :], k == 0, k == 7, [rWA, rU], [rpz])
                ACT(ZT[0:32, zoff:zoff + ntok], pz[0:32, 0:ntok], AF.Copy, [rpz], [rZT])
                pvs = []
                for t in range(ntl):
                    pv, rpv = bank()
                    for k in range(8):
                        MM(pv[:], U[:, k, t * 128:(t + 1) * 128], WV[:, k, :], k == 0, k == 7, [rU, rWA], [rpv])
                    if kind == "c":
                        ACT(CV[:, t, :], pv[:], AF.Copy, [rpv], [rCV])
                    else:
                        ACT(V[:, bidx * 4 + t, :], pv[:], AF.Copy, [rpv], [rV[bidx]])
                    if t >= 1:
                        tt_ = t - 1
                        pl, rpl = bank()
                        MM(pl[:], ZT[:, zoff + tt_ * 128:zoff + (tt_ + 1) * 128], WDA[:], True, True, [rZT, rWDA], [rpl])
                        i = ei[0] % 2
                        ei[0] += 1
                        ACT(E_[i], pl[:], AF.Exp, [rpl], [rE[i]], scale=-1.0)
                        ACT(LA[:, tt_, :], E_[i], AF.Ln, [rE[i]], [rLA], bias=1.0)
                tt_ = ntl - 1
                pl, rpl = bank()
                MM(pl[:], ZT[:, zoff + tt_ * 128:zoff + (tt_ + 1) * 128], WDA[:], True, True, [rZT, rWDA], [rpl])
                i = ei[0] % 2
                ei[0] += 1
                ACT(E_[i], pl[:], AF.Exp, [rpl], [rE[i]], scale=-1.0)
                ACT(LA[:, tt_, :], E_[i], AF.Ln, [rE[i]], [rLA], bias=1.0)

            tr_jobs = {}

            def a1_s2(bi):
                kind, bidx, ntl, ntok, U, rU, tok0, zoff, ch0 = blk_params(bi)
                jobs = []
                for hp in range(2):
                    pk, rpk = bank()
                    for k in range(8):
                        MM(pk[:, 0:ntok], WQK[:, k, 256 + hp * 128:256 + (hp + 1) * 128], U[:, k, :], k == 0, k == 7, [rWA, rU], [rpk])
                    if kind == "x":
                        pq, rpq = bank()
                        for k in range(8):
                            MM(pq[:, 0:ntok], WQK[:, k, hp * 128:(hp + 1) * 128], U[:, k, :], k == 0, k == 7, [rWA, rU], [rpq])
                    for d in range(2):
                        combo = d * 2 + hp
                        pgm, rpgm = bank()
                        tri = TRIU if d == 0 else TRIL
                        for t in range(ntl):
                            MM(pgm[:, t * 128:(t + 1) * 128], LA[:, t, d * 256 + hp * 128:d * 256 + (hp + 1) * 128], tri,
                               True, True, [rLA, rCON], [rpgm])
                        i = ei[0] % 2
                        ei[0] += 1
                        ACT(EN_[i][:, 0:ntok], pgm[:, 0:ntok], AF.Exp, [rpgm], [rEN[i]], scale=1.0 / 16)
                        col0 = 127 if d == 0 else 0
                        pgl = pgm[:, 0:ntok].rearrange("p (t n) -> p t n", n=128)[:, :, col0]
                        ACT(EL[:, combo, ch0:ch0 + ntl], pgl, AF.Exp, [rpgm], [rEL], scale=-1.0 / 16)
                        if kind == "x":
                            ACT(EQ_[i][:, 0:ntok], pgm[:, 0:ntok], AF.Exp, [rpgm], [rEQ[i]], scale=-1.0 / 16, bias=LN8)
                            TT("dve", QK[:, (2 + d) * 2 + hp, tok0:tok0 + ntok], pk[:, 0:ntok], EN_[i][:, 0:ntok], ALU.mult,
                               [rpk, rEN[i]], [rQK[bidx], rWW[2], rWW[3]])
                            TT("dve", QK[:, d * 2 + hp, tok0:tok0 + ntok], pq[:, 0:ntok], EQ_[i][:, 0:ntok], ALU.mult,
                               [rpq, rEQ[i]], [rQK[bidx], rWW[2], rWW[3]])
                            jobs.append((combo, QK[:, (2 + d) * 2 + hp, tok0:tok0 + ntok], rQK[bidx]))
                        else:
                            TT("dve", CK[:, combo, 0:ntok], pk[:, 0:ntok], EN_[i][:, 0:ntok], ALU.mult, [rpk, rEN[i]], [rCK])
                            jobs.append((combo, CK[:, combo, 0:ntok], rCK))
                tr_jobs[bi] = jobs

            seqs = {0: [("c", 0), ("c", 1)] + [("x", i) for i in range(16)],
                    1: [("c", 1), ("c", 0)] + [("x", i) for i in range(15, -1, -1)]}
            rRR = [[Res("rr%d_%d" % (c, b)) for b in range(2)] for c in range(4)]
            rSS = [[Res("ss%d_%d" % (c, i)) for i in range(16)] for c in range(4)]

            def chain_step(step, combos):
                pu, rpu = bank()
                info = []
                for j_c, combo in enumerate(combos):
                    uc = slice(j_c * 128, (j_c + 1) * 128)
                    d, hp = combo // 2, combo % 2
                    kind, ci = seqs[d][step]
                    if kind == "c":
                        lhs = CKT[:, combo, ci, :]; rl = rCKT
                        rhs = CV[:, ci, hp * 256:(hp + 1) * 256]; rr_ = rCV
                    else:
                        lhs = KT[:, combo, ci, :]; rl = rKT[ci // 4]
                        rhs = V[:, ci, hp * 256:(hp + 1) * 256]; rr_ = rV[ci // 4]
                    MM(pu[0:64, uc], lhs[:, 0:64], rhs[:, 0:128], True, True, [rl, rr_], [rpu])
                    MM(pu[64:128, uc], lhs[:, 64:128], rhs[:, 128:256], True, True, [rl, rr_], [rpu], tile_position=(0, 64))
                    info.append((combo, uc, d, kind, ci))
                cur = step % 2
                prv = 1 - cur
                for combo, uc, d, kind, ci in info:
                    if step == 0:
                        CP("dve", RR[:, combo, cur, :], pu[:, uc], [rpu], [rRR[combo][cur]])
                    else:
                        pk_, pci = seqs[d][step - 1]
                        pel = pci if pk_ == "c" else 2 + pci
                        if kind == "x":
                            ACT(SS[:, combo, ci, :], RR[:, combo, prv, :], AF.Copy, [rRR[combo][prv], rEL], [rSS[combo][ci]] + rXS[4:8],
                                scale=EL[:, combo, pel:pel + 1])
                        STT("dve", RR[:, combo, cur, :], RR[:, combo, prv, :], EL[:, combo, pel:pel + 1], pu[:, uc],
                            ALU.mult, ALU.add, [rRR[combo][prv], rEL, rpu], [rRR[combo][cur]])

            def a1_s3(bi):
                kind, bidx, ntl, ntok, U, rU, tok0, zoff, ch0 = blk_params(bi)
                if kind == "x":
                    for c in range(4):
                        pr, rpr = bank()
                        for k in range(8):
                            MM(pr[:], WR[:, k, c * 128:(c + 1) * 128], U[:, k, :], k == 0, k == 7, [rWR, rU], [rpr])
                        ACT(SR[:, c, tok0:tok0 + 512], pr[:], AF.Silu, [rpr], [rSR[bidx]])
                        if c == 0:
                            adaln_piece(bidx * 2)
                for combo, ksrc, rks in tr_jobs[bi]:
                    half = combo % 2
                    for t in range(ntl):
                        TR(PSB[:, half * 512 + t * 128:half * 512 + (t + 1) * 128], ksrc[:, t * 128:(t + 1) * 128], IDB,
                           [rks, rCON], [PSBR])
                    if kind == "x":
                        CP("dve", KT[:, combo, bidx * 4:bidx * 4 + 4, :], PSB[:, half * 512:half * 512 + 512].rearrange("p (t n) -> p t n", n=128),
                           [PSBR], [rKT[bidx]] + rXS[0:4])
                    else:
                        CP("dve", CKT[:, combo, :, :], PSB[:, half * 512:half * 512 + 256].rearrange("p (t n) -> p t n", n=128),
                           [PSBR], [rCKT])
                fsteps = [0, 1] if kind == "c" else [2 + bidx * 4 + t_ for t_ in range(4)]
                for st_ in fsteps:
                    chain_step(st_, (0, 1))
                if kind == "x":
                    adaln_piece(bidx * 2 + 1)

            DMA("pool", d_wr, WR, w_in[:, :, COL_R:COL_R + 512], list(rXS), [rWR])
            a1_s1(0)
            a1_s2(0)
            for bi in range(1, 5):
                a1_s1(bi)
                a1_s3(bi - 1)
                a1_s2(bi)
            a1_s3(4)
            nbanks[0] = 8
            if upto == 1.5:
                return
            dump("qk", QK, [128, 8, 2048], rQK)
            dump("el", SM[:, 152:224], [128, 72], [rEL])

            bufsets = [
                dict(T1=SC[:, 0:512], T2=SC[:, 512:1024], RS=SC[:, 1024:1536], T3=SC[:, 1536:2048],
                     MT=SCB[:, 5120:5632], SQ=SCB[:, 5632:6144]),
                dict(T1=XT[:, 0:512], T2=XT[:, 512:1024], RS=XT[:, 1024:1536], T3=XT[:, 1536:2048],
                     MT=XTB[:, 6400:6912], SQ=XTB[:, 6912:7424]),
            ]
            for bs_ in bufsets:
                for nm in ("T1", "T2", "RS", "T3", "MT", "SQ"):
                    bs_["r" + nm] = Res(nm)
            po_l = {}
            MTall = WW[:, 0:8192].rearrange("p (c n) -> p c n", c=16)
            rMTall = [Res("mt%d" % c) for c in range(16)]

            def stage1(c):
                b = c // 4
                tk = slice(c * 128, (c + 1) * 128)
                B_ = bufsets[c % 2]
                T1, T2 = B_["T1"], B_["T2"]
                rT1, rT2 = B_["rT1"], B_["rT2"]
                MT = MTall[:, c, :]
                rMT = rMTall[c]
                pxy = [bank(), bank()]
                for j_ in range(4):
                    for hf in range(2):
                        rows = slice(hf * 64, (hf + 1) * 64)
                        px, rpx = pxy[hf]
                        hp = j_ % 2
                        if j_ < 2:
                            MM(px[:, hp * 128:(hp + 1) * 128], QK[rows, 4 + hp, tk], QK[rows, 0 + hp, tk], True, True, [rQK[b]], [rpx])
                        else:
                            MM(px[:, 256 + hp * 128:256 + (hp + 1) * 128], QK[rows, 6 + hp, tk], QK[rows, 2 + hp, tk], True, True, [rQK[b]], [rpx])
                TT("dve", T1, pxy[0][0][:], MU4, ALU.mult, [pxy[0][1], rCON], [rT1])
                TT("dve", T2, pxy[1][0][:], MU4, ALU.mult, [pxy[1][1], rCON], [rT2])
                MT3 = MT.rearrange("p (hp hf n) -> p hp hf n", hp=2, hf=2)
                TT("pool", MT3[:, :, 0, :], T1[:, 0:256].rearrange("p (a n) -> p a n", a=2), T1[:, 256:512].rearrange("p (a n) -> p a n", a=2),
                   ALU.add, [rT1], [rMT, rWA])
                TT("pool", MT3[:, :, 1, :], T2[:, 0:256].rearrange("p (a n) -> p a n", a=2), T2[:, 256:512].rearrange("p (a n) -> p a n", a=2),
                   ALU.add, [rT2], [rMT])

            bufsets = [
                dict(T1=SC[:, 0:512], T2=SC[:, 512:1024], RS=SC[:, 1024:1536], T3=SC[:, 1536:2048],
                     MT=SCB[:, 5120:5632], SQ=SCB[:, 5632:6144]),
                dict(T1=XT[:, 0:512], T2=XT[:, 512:1024], RS=XT[:, 1024:1536], T3=XT[:, 1536:2048],
                     MT=XTB[:, 6400:6912], SQ=XTB[:, 6912:7424]),
            ]
            for bs_ in bufsets:
                for nm in ("T1", "T2", "RS", "T3", "MT", "SQ"):
                    bs_["r" + nm] = Res(nm)
            po_l = {}
            MTall = WW[:, 0:8192].rearrange("p (c n) -> p c n", c=16)
            rMTall = [Res("mt%d" % c) for c in range(16)]

            def stage1(c):
                b = c // 4
                tk = slice(c * 128, (c + 1) * 128)
                B_ = bufsets[c % 2]
                T1, T2 = B_["T1"], B_["T2"]
                rT1, rT2 = B_["rT1"], B_["rT2"]
                MT = MTall[:, c, :]
                rMT = rMTall[c]
                pxy = [bank(), bank()]
                for j_ in range(4):
                    for hf in range(2):
                        rows = slice(hf * 64, (hf + 1) * 64)
                        px, rpx = pxy[hf]
                        hp = j_ % 2
                        if j_ < 2:
                            MM(px[:, hp * 128:(hp + 1) * 128], QK[rows, 4 + hp, tk], QK[rows, 0 + hp, tk], True, True, [rQK[b]], [rpx])
                        else:
                            MM(px[:, 256 + hp * 128:256 + (hp + 1) * 128], QK[rows, 6 + hp, tk], QK[rows, 2 + hp, tk], True, True, [rQK[b]], [rpx])
                TT("dve", T1, pxy[0][0][:], MU4, ALU.mult, [pxy[0][1], rCON], [rT1])
                TT("dve", T2, pxy[1][0][:], MU4, ALU.mult, [pxy[1][1], rCON], [rT2])
                MT3 = MT.rearrange("p (hp hf n) -> p hp hf n", hp=2, hf=2)
                TT("pool", MT3[:, :, 0, :], T1[:, 0:256].rearrange("p (a n) -> p a n", a=2), T1[:, 256:512].rearrange("p (a n) -> p a n", a=2),
                   ALU.add, [rT1], [rMT, rWA])
                TT("pool", MT3[:, :, 1, :], T2[:, 0:256].rearrange("p (a n) -> p a n", a=2), T2[:, 256:512].rearrange("p (a n) -> p a n", a=2),
                   ALU.add, [rT2], [rMT])

            if upto == 1.7:
                return
            dump("ss", R1B[:, 8192:16384], [128, 8192], [r_ for l_ in rSS for r_ in l_])

            def stage2(c, pos):
                b = c // 4
                tk = slice(c * 128, (c + 1) * 128)
                B_ = bufsets[pos % 2]
                SQ, rSQ = B_["SQ"], B_["rSQ"]
                MT = MTall[:, c, :]
                rMT = rMTall[c]
                pA, rpA = bank()
                pB, rpB = bank()
                po_l[c] = ((pA, rpA), (pB, rpB))
                for hp in range(2):
                    cs = slice(hp * 128, (hp + 1) * 128)
                    for hf, (pp_, rpp_) in enumerate(((pA, rpA), (pB, rpB))):
                        h = hp * 2 + hf
                        MM(pp_[:, cs], V[:, c, h * 128:(h + 1) * 128], MT[:, h * 128:(h + 1) * 128], True, False, [rV[b], rMT], [rpp_])
                    for d_ in range(2):
                        for hf, (pp_, rpp_) in enumerate(((pA, rpA), (pB, rpB))):
                            rows = slice(hf * 64, (hf + 1) * 64)
                            MM(pp_[:, cs], SS[rows, 2 * d_ + hp, c, :], QK[rows, 2 * d_ + hp, tk], False, d_ == 1,
                               [rSS[2 * d_ + hp][c], rQK[b]], [rpp_])
                SQ4 = SQ.rearrange("p (hp hf n) -> p hp hf n", hp=2, hf=2)
                ACT(SQ4[:, :, 0, :], pA[:, 0:256].rearrange("p (a n) -> p a n", a=2), AF.Square, [rpA], [rSQ])
                ACT(SQ4[:, :, 1, :], pB[:, 0:256].rearrange("p (a n) -> p a n", a=2), AF.Square, [rpB], [rSQ])

            def stage3(c, pos):
                b = c // 4
                tk = slice(c * 128, (c + 1) * 128)
                B_ = bufsets[pos % 2]
                RS, T3, SQ, rRS, rT3, rSQ = B_["RS"], B_["T3"], B_["SQ"], B_["rRS"], B_["rT3"], B_["rSQ"]
                (pA, rpA), (pB, rpB) = po_l[c]
                pms, rpms = bank()
                MM(pms[:], ONESB, SQ, True, True, [rONES, rSQ], [rpms])
                RSTDOP(RS, pms[:], RS, rpms, rRS, rRS)
                T34 = T3.rearrange("p (hp hf n) -> p hp hf n", hp=2, hf=2)
                RS4 = RS.rearrange("p (hp hf n) -> p hp hf n", hp=2, hf=2)
                STT("dve", T34[:, :, 0, :], pA[:, 0:256].rearrange("p (a n) -> p a n", a=2), VEC[:, V_GN:V_GN + 1], RS4[:, :, 0, :],
                    ALU.mult, ALU.mult, [rpA, rVEC, rRS], [rT3])
                STT("dve", T34[:, :, 1, :], pB[:, 0:256].rearrange("p (a n) -> p a n", a=2), VEC[:, V_GN:V_GN + 1], RS4[:, :, 1, :],
                    ALU.mult, ALU.mult, [rpB, rVEC, rRS], [rT3])
                TT("pool", SR[:, :, tk], T3.rearrange("p (h n) -> p h n", h=4), SR[:, :, tk], ALU.mult, [rT3, rSR[b]], [rSR[b]])

            pend = []
            npos = [0]
            for step in range(18):
                chain_step(step, (2, 3))
                if pend:
                    stage3(*pend.pop(0))
                if step < 16:
                    stage1(15 - step)
                for c_ in range(16):
                    if 17 - c_ == step:
                        stage2(c_, npos[0])
                        pend.append((c_, npos[0]))
                        npos[0] += 1
            while pend:
                stage3(*pend.pop(0))
            dump("og", SR[:], [128, 4, 2048], rSR)

            if upto == 2:
                return
            P.barrier()
            nbanks[0] = 7
            adaln_finish()
            WG = WW[:, 0:8192].rearrange("p (k n) -> p k n", k=8)
            d_wg = P.dsem("wglu")
            rWG = Res("wglu")
            DMA("pool", d_wg, WG, w_in[:, :, 0:1024], [rWA], [rWG])
            WGT = WW[:, 8192:24576].rearrange("p (k n) -> p k n", k=8)
            rWGT = Res("wgt"); rWOUT = Res("wout")
            DMA("pool", P.dsem("wgt"), WGT, w_in[:, :, COL_GATE:COL_GATE + 2048], [], [rWGT])
            for i_ in range(4):
                DMA("pool", P.dsem("cv2_%d" % i_), wf2b_raw[i_ * 1024:(i_ + 1) * 1024, :], w_ff2_raw[i_ * 1024:(i_ + 1) * 1024, :], [], [rCV2[i_]])

            AT = R2[:].rearrange("p (c n) -> p c n", c=4)
            rAT = [Res("at%d" % c) for c in range(4)]
            ST = WW[:, 24576:32768].rearrange("p (c n) -> p c n", c=4)
            rST = [[Res("st%d_%d" % (c, b_)) for b_ in range(4)] for c in range(4)]
            rDIAGall = Res("diagall"); rY32all = Res("y32all")
            DIAG = R1B[:, 0:15872].rearrange("p (j n) -> p j n", j=124)
            rDIAG = [Res("dg%d" % j) for j in range(124)]
            for j in range(124):
                wcol = VEC[:, V_CW + j:V_CW + j + 1]
                e_ = ("dve", "act")[j % 2]
                if e_ == "act":
                    ACT(DIAG[:, j, :], IDB, AF.Copy, [rCON, rVEC], [rDIAG[j]], scale=wcol)
                else:
                    TS(e_, DIAG[:, j, :], IDB, wcol, None, ALU.mult, None, [rCON, rVEC], [rDIAG[j]])
            SG = [SC[:, 0:512], SC[:, 512:1024]]
            rSG = [Res("sg0"), Res("sg1")]
            gi = 0
            for b in range(4):
                tb = slice(b * 512, (b + 1) * 512)
                for c in range(4):
                    p1, rp1 = bank()
                    p2, rp2 = bank()
                    for k in range(8):
                        MM(p2[:], WG[:, k, 512 + c * 128:512 + (c + 1) * 128], UT[:, k, tb], k == 0, k == 7, [rWG, rUT[b]], [rp2])
                    for k in range(8):
                        MM(p1[:], WG[:, k, c * 128:(c + 1) * 128], UT[:, k, tb], k == 0, k == 7, [rWG, rUT[b]], [rp1])
                    i = gi % 2
                    gi += 1
                    ACT(SG[i], p2[:], AF.Sigmoid, [rp2], [rSG[i]])
                    TT("dve", AT[:, c, tb], p1[:], SG[i], ALU.mult, [rp1, rSG[i]], [rAT[c]])
            WCO = WW[:, 0:4096].rearrange("p (k n) -> p k n", k=4)
            WGO = WW[:, 4096:8192].rearrange("p (k n) -> p k n", k=4)
            d_wb2 = P.dsem("wb2")
            rWCG = Res("wcg")
            DMA("pool", d_wb2, WCO, w_co, [], [rWCG, rWG])
            DMA("pool", d_wb2, WGO, w_go, [], [rWCG])
            Y32 = [XT[:, 0:2048].rearrange("p (c n) -> p c n", c=4), XT[:, 2048:4096].rearrange("p (c n) -> p c n", c=4)]
            rY32 = [[Res("y32_%d_%d" % (i, c)) for c in range(4)] for i in range(2)]
            YB = SCB[:, 2048:4096].rearrange("p (c n) -> p c n", c=4)
            YSQ = SCB[:, 4096:6144].rearrange("p (c n) -> p c n", c=4)
            M2 = SC[:, 3072:3584]; RS2 = SC[:, 3584:4096]
            NMRb = SC[:, 0:512]
            rYB = Res("yb"); rYSQ = Res("ysq"); rM2 = Res("m2"); rRS2 = Res("rs2"); rNMR = Res("nmr")
            for b in range(4):
                tb = slice(b * 512, (b + 1) * 512)
                yb = Y32[b % 2]; ryb = rY32[b % 2]
                for c in range(4):
                    pcv, rpcv = bank()
                    mms = []
                    for kk in [15] + [k_ for k_ in range(31) if k_ != 15]:
                        sft = kk - 15
                        if c < 2:
                            lo, hi = max(0, -sft), 64 - max(0, sft)
                            o_ = pcv[:].rearrange("p (r w) -> p r w", w=64)[:, :, lo:hi]
                            r_ = AT[:, c, tb].rearrange("p (r w) -> p r w", w=64)[:, :, lo + sft:hi + sft]
                            if simcompat:
                                o_ = pcv[:, 0:8 * (hi - lo)]
                                r_ = AT[:, c, tb][:, 0:8 * (hi - lo)]
                        else:
                            r_lo, r_hi = max(8 * b, -sft), min(8 * b + 8, 32 - sft)
                            if r_lo >= r_hi:
                                continue
                            o_ = pcv[:, (r_lo - 8 * b) * 64:(r_hi - 8 * b) * 64]
                            r_ = AT[:, c, (r_lo + sft) * 64:(r_hi + sft) * 64]
                        mms.append((o_, r_, c * 31 + kk))
                    for i_, (o_, r_, j) in enumerate(mms):
                        last_ = (b == 3 and c == 3 and i_ == len(mms) - 1)
                        MM(o_, DIAG[:, j, :], r_, i_ == 0, i_ == len(mms) - 1, [rDIAG[j], rAT[c]], [rpcv] + ([rDIAGall] if last_ else []))
                    ACT(yb[:, c, :], pcv[:], AF.Identity, [rpcv, rVEC], [ryb[c]], bias=VEC[:, V_CB + c:V_CB + c + 1])
                ACT(YB, yb[:], AF.Copy, ryb, [rYB])
                ACT(YSQ, yb[:], AF.Square, ryb, [rYSQ])
                pmean, rpmean = bank()
                pmsq, rpmsq = bank()
                for c in range(4):
                    MM(pmean[:], ONES512, YB[:, c, :], c == 0, c == 3, [rONES, rYB], [rpmean])
                for c in range(4):
                    MM(pmsq[:], ONES512, YSQ[:, c, :], c == 0, c == 3, [rONES, rYSQ], [rpmsq])
                ACT(M2, pmean[:], AF.Square, [rpmean], [rM2])
                TT("dve", M2, pmsq[:], M2, ALU.subtract, [rpmsq, rM2], [rM2])
                RSTDOP(RS2, M2, RS2, rM2, rRS2, rRS2)
                STT("dve", NMRb, pmean[:], -1.0, RS2, ALU.mult, ALU.mult, [rpmean, rRS2], [rNMR])
                for c in range(4):
                    TT("dve", yb[:, c, :], yb[:, c, :], RS2, ALU.mult, [ryb[c], rRS2], [ryb[c]])
                for c in range(4):
                    TT("dve", yb[:, c, :], yb[:, c, :], NMRb, ALU.add, [ryb[c], rNMR], [ryb[c]])
                for c in range(4):
                    ACT(ST[:, c, tb], yb[:, c, :], AF.Silu, [ryb[c], rVEC], [rST[c][b]] + ([rY32all] if (b == 3 and c == 3) else []),
                        scale=VEC[:, V_LNG + c:V_LNG + c + 1], bias=VEC[:, V_LNB + c:V_LNB + c + 1])
            DMA("pool", P.dsem("wout"), R2[:].rearrange("p (k n) -> p k n", k=8), w_out, [], [rWOUT] + rAT)
            dump("st", WW[:, 24576:32768], [128, 8192], [r_ for l_ in rST for r_ in l_])

            if upto == 3:
                return
            WOUT = R2[:].rearrange("p (k n) -> p k n", k=8)
            OG = SR
            MG = [R1B[:, 0:4096].rearrange("p (k n) -> p k n", k=8), R1B[:, 4096:8192].rearrange("p (k n) -> p k n", k=8)]
            rMG = [Res("mg0"), Res("mg1")]
            S1 = [SC[:, 0:512], SC[:, 512:1024]]; S2 = [SC[:, 1024:1536], SC[:, 1536:2048]]
            M1 = [SC[:, 2048:2560], SC[:, 2560:3072]]; M2_ = [SC[:, 3072:3584], SC[:, 3584:4096]]
            rS1 = [Res(), Res()]; rS2 = [Res(), Res()]; rM1 = [Res(), Res()]; rM2b = [Res(), Res()]
            XTt = XT[:].rearrange("p (s n) -> p s n", s=4)
            rSTP = Res("stp")
            rXTt = [Res("xt%d" % i) for i in range(4)]
            d_xt = [P.dsem("xt%d" % i) for i in range(4)]
            d_hm = [P.dsem("hm%d" % i) for i in range(4)]
            rHM = [Res("hmid%d" % i) for i in range(16)]
            JUNK2 = R1B[:, 8192:9216]
            TMPY = R1[:, 6144:8192].rearrange("p (s n) -> p s n", s=2)
            rTMPY = [Res("tmpy0"), Res("tmpy1")]
            fi_ = [0]
            rst2s = [Res("stat2_%d" % b) for b in range(4)]
            rWF2 = [Res("wf2_%d" % q) for q in range(4)]
            rJ2 = Res("junk2")
            rFS = [Res("fs%d" % i) for i in range(4)]
            d_fs = [P.dsem("fs%d" % i) for i in range(4)]

            def merge(b):
                tb = slice(b * 512, (b + 1) * 512)
                mg = MG[b % 2]; rmg = rMG[b % 2]
                for fc in range(8):
                    fs = slice(fc * 128, (fc + 1) * 128)
                    pg1, rpg1 = bank(); pg2, rpg2 = bank(); pc, rpc = bank(); pgl, rpgl = bank()
                    for k in range(8):
                        MM(pg1[:], WGT[:, k, fs], UT[:, k, tb], k == 0, k == 7, [rWGT, rUT[b]], [rpg1])
                    for k in range(8):
                        MM(pg2[:], WGT[:, k, 1024 + fc * 128:1024 + (fc + 1) * 128], UT[:, k, tb], k == 0, k == 7, [rWGT, rUT[b]], [rpg2])
                    for k in range(4):
                        MM(pc[:], WCO[:, k, fs], ST[:, k, tb], k == 0, k == 3, [rWCG, rST[k][b]], [rpc])
                    for k in range(4):
                        MM(pgl[:], WGO[:, k, fs], OG[:, k, tb], k == 0, k == 3, [rWCG, rSR[b]], [rpgl])
                    i = fi_[0] % 2
                    fi_[0] += 1
                    fu = fi_[0] <= 2
                    ACT(S1[i], pg1[:], AF.Sigmoid, [rpg1], [rS1[i]] + ([rNMR, rSG[0], rSG[1]] if fu else []))
                    ACT(S2[i], pg2[:], AF.Sigmoid, [rpg2], [rS2[i]] + ([rYB] if fu else []))
                    TT("dve", M1[i], pc[:], S1[i], ALU.mult, [rpc, rS1[i]], [rM1[i]] + ([rYSQ] if fu else []))
                    TT("dve", M2_[i], pgl[:], S2[i], ALU.mult, [rpgl, rS2[i]], [rM2b[i]] + ([rM2, rRS2] if fu else []))
                    TT("pool", mg[:, fc, :], M1[i], M2_[i], ALU.add, [rM1[i], rM2b[i], rDIAGall], [rmg])

            def wout(b):
                mg = MG[b % 2]; rmg = rMG[b % 2]
                rst2 = rst2s[b]
                for t in range(4):
                    tile = b * 4 + t
                    tk = slice(t * 128, (t + 1) * 128)
                    xs = XTt[:, t, :]
                    DMA("sp", d_xt[t], xs, x_d[tile * 128:(tile + 1) * 128, :], [rY32all], [rXTt[t]])
                    py = []
                    for half in range(2):
                        p_, rp_ = bank()
                        for k in range(8):
                            MM(p_[:], mg[:, k, tk], WOUT[:, k, half * 512:(half + 1) * 512], k == 0, k == 7, [rmg, rWOUT], [rp_])
                        py.append((p_, rp_))
                    rs_ = rSTP
                    for half in range(2):
                        ACT(JUNK2[:, 0:512], py[half][0][:], AF.Square, [py[half][1], rDIAGall], [rs_, rJ2], scale=1.0 / 32,
                            accum_out=TMP8[:, 8 + half:9 + half])
                    TT("dve", TMP8[:, 10:11], TMP8[:, 8:9], TMP8[:, 9:10], ALU.add, [rs_], [rs_])
                    RSTDOP(TMP8[:, 12:13], TMP8[:, 10:11], TMP8[:, 11:12], rs_, rs_, rs_)
                    ty = TMPY[:, t % 2, :]; rty = rTMPY[t % 2]
                    for half in range(2):
                        hs = slice(half * 512, (half + 1) * 512)
                        STT("dve", ty[:, hs], py[half][0][:], TMP8[:, 12:13], GG[:, hs], ALU.mult, ALU.mult,
                            [py[half][1], rs_, rGG[0], rDIAGall], [rty])
                    TT("dve", xs, xs, ty, ALU.add, [rXTt[t], rty], [rXTt[t]])
                    DMA("sp", d_hm[t], hmid_d[tile * 128:(tile + 1) * 128, :], xs, [rXTt[t]], [rHM[tile]])
                    ACT(JUNK2, xs, AF.Square, [rXTt[t]], [rst2, rJ2], scale=1.0 / 32, accum_out=TMP8[:, 16 + t:17 + t])

            def norm2T(b):
                tb = slice(b * 512, (b + 1) * 512)
                rst2 = rst2s[b]
                RSTDOP(TMP8[:, 24:28], TMP8[:, 16:20], TMP8[:, 20:24], rst2, rst2, rst2)
                for t in range(4):
                    xs = XTt[:, t, :]
                    TS("dve", xs, xs, TMP8[:, 24 + t:25 + t], None, ALU.mult, None, [rst2, rXTt[t]], [rXTt[t]])
                for c in range(8):
                    pb, rpb = bank()
                    for t in range(4):
                        TR(pb[:, t * 128:(t + 1) * 128], XTt[:, t, c * 128:(c + 1) * 128], IDF[:], [rXTt[t], rIDF], [rpb])
                    ACT(UT[:, c, tb], pb[:], AF.Identity, [rpb, rSM, rMOD2], [rUT[b]],
                        scale=GS[:, 16 + c:17 + c], bias=MOD[:, 24 + c:25 + c])

            WF2 = WW[:].rearrange("p (k n) -> p k n", k=32)
            R3f = R3[:].rearrange("p c n -> p (c n)")
            FS = [R3f[:, 0:4096].rearrange("p (k n) -> p k n", k=8), R3f[:, 4096:8192].rearrange("p (k n) -> p k n", k=8),
                  R2[:, 0:4096].rearrange("p (k n) -> p k n", k=8), R2[:, 4096:8192].rearrange("p (k n) -> p k n", k=8)]
            fs_old = [list(rSR), list(rSR), [rWOUT], [rWOUT]]
            HID = R1B.rearrange("p (k n) -> p k n", k=32)
            rHID = Res("hid")
            RL = [SC[:, 0:512], SC[:, 512:1024]]
            rRL = [Res(), Res()]
            JUNK3 = SCB[:, 2048:2560]
            TY2 = SC[:, 2048:4096].rearrange("p (s n) -> p s n", s=2)
            rTY2 = [Res(), Res()]
            d_hl = [P.dsem("hl%d" % i) for i in range(4)]
            d_o = [P.dsem("o%d" % i) for i in range(4)]
            cnt = {"ri": 0, "si": 0}
            ff1_done = set()

            def ff1_slab(b, s_):
                tb = slice(b * 512, (b + 1) * 512)
                f = cnt["si"] % 4
                cnt["si"] += 1
                ff1_done.add((b, s_))
                if not (b == 0 and s_ < 2):
                    DMA("pool", d_fs[f], FS[f], w_ff1[:, :, s_ * 512:(s_ + 1) * 512], rCV1, [rFS[f]] + fs_old[f])
                for oc in range(4):
                    ph, rph = bank()
                    for k in range(8):
                        MM(ph[:], FS[f][:, k, oc * 128:(oc + 1) * 128], UT[:, k, tb], k == 0, k == 7, [rFS[f], rUT[b]], [rph])
                    i = cnt["ri"] % 2
                    cnt["ri"] += 1
                    ACT(RL[i], ph[:], AF.Relu, [rph], [rRL[i]] + rS1)
                    STT("dve", HID[:, s_ * 4 + oc, :], ph[:], 0.0, RL[i], ALU.max, ALU.mult, [rph, rRL[i]],
                        [rHID] + (rMG + rTMPY + [rJ2] if b == 0 else []))

            def ff2_block(b):
                for t in range(4):
                    tile = b * 4 + t
                    tk = slice(t * 128, (t + 1) * 128)
                    hs_ = XTt[:, t, :]
                    DMA("sp", d_hl[t], hs_, hmid_d[tile * 128:(tile + 1) * 128, :], [rHM[tile]], [rXTt[t]])
                    py = []
                    for half in range(2):
                        p_, rp_ = bank()
                        for k in range(32):
                            MM(p_[:], HID[:, k, tk], WF2[:, k, half * 512:(half + 1) * 512], k == 0, k == 31,
                               [rHID, rWF2[k // 8]], [rp_])
                        py.append((p_, rp_))
                    rs_ = rSTP
                    for half in range(2):
                        ACT(JUNK3, py[half][0][:], AF.Square, [py[half][1]], [rs_], scale=1.0 / 32,
                            accum_out=TMP8[:, 8 + half:9 + half])
                    TT("dve", TMP8[:, 10:11], TMP8[:, 8:9], TMP8[:, 9:10], ALU.add, [rs_], [rs_])
                    RSTDOP(TMP8[:, 12:13], TMP8[:, 10:11], TMP8[:, 11:12], rs_, rs_, rs_)
                    ty = TY2[:, t % 2, :]; rty = rTY2[t % 2]
                    for half in range(2):
                        hs = slice(half * 512, (half + 1) * 512)
                        STT("dve", ty[:, hs], py[half][0][:], TMP8[:, 12:13], GG[:, 1024 + half * 512:1024 + (half + 1) * 512],
                            ALU.mult, ALU.mult, [py[half][1], rs_, rGG[1]], [rty] + (rM1 + rM2b if b == 0 else []))
                    TT("dve", hs_, hs_, ty, ALU.add, [rXTt[t], rty], [rXTt[t]])
                    OUT_EVS.append(DMA("sp", d_o[t], out_d[tile * 128:(tile + 1) * 128, :], hs_, [rXTt[t]], [Res()]))

            merge(0)
            wout(0)
            for b in range(1, 4):
                merge(b)
                norm2T(b - 1)
                if b == 3:
                    WF2 = WW[:].rearrange("p (k n) -> p k n", k=32)
                    old = [[rWCG], [rWGT], [rWGT], [r_ for l_ in rST for r_ in l_]]
                    R3f_ = R3[:].rearrange("p c n -> p (c n)")
                    for f_ in range(2):
                        DMA("pool", d_fs[f_], R3f_[:, f_ * 4096:(f_ + 1) * 4096].rearrange("p (k n) -> p k n", k=8),
                            w_ff1[:, :, f_ * 512:(f_ + 1) * 512], rCV1, [rFS[f_]] + list(rSR))
                    for q in range(4):
                        DMA("pool", P.dsem("wf2_%d" % q), WF2[:, q * 8:(q + 1) * 8, :], w_ff2[:, q * 8:(q + 1) * 8, :], [rCV2[q]],
                            [rWF2[q]] + old[q])
                wout(b)
            ff1_slab(0, 0)
            ff1_slab(0, 1)
            norm2T(3)
            dump("u2", UT[:], [128, 8, 2048], rUT)

            if upto == 4:
                return
            for b in range(4):
                for s_ in range(8):
                    if (b, s_) not in ff1_done:
                        ff1_slab(b, s_)
                ff2_block(b)

        body()
        allev = list(OUT_EVS) + [ds.last for ds in dbg_out.values()]
        P.wait_all("sp", allev)
        P.emit(st)
        build.stats = P.stats
    return nc


def _consts():
    ident = np.eye(128, dtype=np.float32)
    j = np.arange(128)[:, None]
    i = np.arange(128)[None, :]
    mu = (j <= i).astype(np.float32)
    ml = (j >= i).astype(np.float32)
    return np.ascontiguousarray(np.concatenate([ident, mu, mu, ml, ml, np.tile(ml, (1, 4)), mu, ml], axis=1))


def _pack(inputs, b):
    f = np.float32
    c = np.asarray(inputs["c"], f)[b]
    cc = np.asarray(inputs["c_ctx"], f)
    vecs = np.zeros((128, NVEC), f)
    cv = np.stack([c.reshape(8, 128), cc.reshape(8, 128)], axis=-1)
    vecs[:, V_C:V_C + 16] = cv.transpose(1, 0, 2).reshape(128, 16)
    vecs[:, V_BMOD:V_BMOD + 48] = np.asarray(inputs["b_mod"], f)[0].reshape(48, 128).T
    vecs[:, V_GPRE1:V_GPRE1 + 8] = np.asarray(inputs["g_pre1"], f)[0].reshape(8, 128).T
    vecs[:, V_GPRE2:V_GPRE2 + 8] = np.asarray(inputs["g_pre2"], f)[0].reshape(8, 128).T
    cw = np.asarray(inputs["conv_w"], f)[0]
    vecs[:, V_CW:V_CW + 124] = cw.reshape(31, 4, 128).transpose(2, 1, 0).reshape(128, 124)
    vecs[:, V_CB:V_CB + 4] = np.asarray(inputs["conv_b"], f)[0].reshape(4, 128).T
    vecs[:, V_LNG:V_LNG + 4] = np.asarray(inputs["conv_ln_g"], f)[0].reshape(4, 128).T
    vecs[:, V_LNB:V_LNB + 4] = np.asarray(inputs["conv_ln_b"], f)[0].reshape(4, 128).T
    vecs[:, V_GN] = np.asarray(inputs["gla_norm_g"], f)[0]
    return vecs


def _shared(inputs):
    f = np.float32
    gpost = np.concatenate([np.asarray(inputs["g_post1"], f)[0], np.asarray(inputs["g_post2"], f)[0]])
    gpost = np.ascontiguousarray(np.broadcast_to(gpost[None, :], (128, 2048)))
    wd = np.asarray(inputs["w_decay"], f)[0]
    bd = np.asarray(inputs["b_decay"], f)[0]
    wda = np.zeros((33, 512), f)
    wda[0:16, 0:256] = wd[0]
    wda[16:32, 256:512] = wd[1]
    wda[32, 0:256] = bd[0]
    wda[32, 256:512] = bd[1]
    sh = {"gpost": gpost, "wda": wda, "consts": _consts()}
    for k in ("w_mod", "w_in", "w_conv_out", "w_gla_out", "w_out", "w_ff1", "w_ff2"):
        sh[k] = np.ascontiguousarray(np.asarray(inputs[k], f)[0])
    return sh


_NC_CACHE = {}


def kernel(**inputs):
    if "nc" not in _NC_CACHE:
        _NC_CACHE["nc"] = build()
    nc = _NC_CACHE["nc"]
    sh = _shared(inputs)
    x = np.asarray(inputs["x"], np.float32)
    ctx = np.asarray(inputs["ctx"], np.float32)
    in_maps = []
    for b in range(8):
        m = dict(sh)
        m["x"] = np.ascontiguousarray(x[b])
        m["ctx"] = np.ascontiguousarray(ctx[b])
        m["vecs"] = _pack(inputs, b)
        in_maps.append(m)
    res = run_bass_kernel_spmd(nc, in_maps, core_ids=list(range(8)))
    return np.stack([np.asarray(r["out"], np.float32) for r in res.results], axis=0)
```

```python
import contextlib
import math
import numpy as np
import concourse.bass as bass
import concourse.mybir as mybir
from concourse.bass_utils import run_bass_kernel_spmd

F32 = mybir.dt.float32
BF16 = mybir.dt.bfloat16
AF = mybir.ActivationFunctionType
ALU = mybir.AluOpType

COMPUTE = ("pe", "act", "dve", "pool")
QUEUES = ("pe", "act", "dve", "pool", "sp")


class Res:
    __slots__ = ("name", "w", "r")

    def __init__(self, name=""):
        self.name = name
        self.w = None
        self.r = []


class DSem:
    __slots__ = ("sem", "count", "key", "last")

    def __init__(self, key):
        self.sem = None
        self.count = 0
        self.key = key
        self.last = None


class Ev:
    __slots__ = ("key", "val", "clock", "op", "dsem")


class Op:
    __slots__ = ("eng", "fn", "idx", "sig", "sigval", "waits", "ev", "dsem")


class Prog:
    def __init__(self, nc):
        self.nc = nc
        self.ops = {e: [] for e in QUEUES}
        self.clock = {e: {} for e in QUEUES}
        self.dsems = []
        self.last = {}

    def dsem(self, name):
        d = DSem("D%d:%s" % (len(self.dsems), name))
        self.dsems.append((name, d))
        return d

    @staticmethod
    def _gather(reads, writes):
        deps = []
        for r in reads:
            if r.w is not None:
                deps.append(r.w)
        for w in writes:
            if w.w is not None:
                deps.append(w.w)
            deps.extend(w.r)
        return deps

    def _apply_waits(self, eng, op, deps):
        clk = self.clock[eng]
        need = {}
        for d in deps:
            same = d.op is not None and d.op.eng == eng
            if same and eng in ("pe", "sp"):
                continue
            if clk.get(d.key, -1) >= d.val:
                continue
            cur = need.get(d.key)
            if cur is None or d.val > cur.val:
                need[d.key] = d
        for d in need.values():
            if d.op is not None:
                d.op.sig = True
        for d in deps:
            for k, v in d.clock.items():
                if clk.get(k, -1) < v:
                    clk[k] = v
        op.waits = list(need.values())

    def _new(self, eng, fn):
        o = Op()
        o.eng = eng
        o.fn = fn
        o.idx = len(self.ops[eng])
        o.sig = False
        o.sigval = None
        o.dsem = None
        o.waits = []
        return o

    def op(self, eng, fn, reads=(), writes=()):
        o = self._new(eng, fn)
        self._apply_waits(eng, o, self._gather(reads, writes))
        ev = Ev()
        ev.key = eng
        ev.val = o.idx
        ev.op = o
        ev.dsem = None
        clk = self.clock[eng]
        if eng in ("pe", "sp"):
            clk[eng] = o.idx
        ev.clock = dict(clk)
        ev.clock[eng] = o.idx
        o.ev = ev
        for r in reads:
            r.r.append(ev)
        for w in writes:
            w.w = ev
            w.r = []
        self.ops[eng].append(o)
        self.last[eng] = ev
        return ev

    def dma(self, queue, dsem, fn, reads=(), writes=()):
        o = self._new(queue, fn)
        o.dsem = dsem
        self._apply_waits(queue, o, self._gather(reads, writes))
        dsem.count += 16
        ev = Ev()
        ev.key = dsem.key
        ev.val = dsem.count
        ev.op = None
        ev.dsem = dsem
        ev.clock = dict(self.clock[queue])
        ev.clock[dsem.key] = dsem.count
        o.ev = ev
        dsem.last = ev
        for r in reads:
            r.r.append(ev)
        for w in writes:
            w.w = ev
            w.r = []
        self.ops[queue].append(o)
        return ev

    def wait_all(self, eng, evs):
        o = self._new(eng, None)
        self._apply_waits(eng, o, list(evs))
        self.ops[eng].append(o)

    def barrier(self):
        evs = [self.last[e] for e in COMPUTE if e in self.last]
        evs += [d.last for _, d in self.dsems if d.last is not None]
        for e in QUEUES:
            self.wait_all(e, evs)

    def emit(self, stack):
        nc = self.nc
        esem = {}
        for e in COMPUTE:
            esem[e] = stack.enter_context(nc.semaphore("s_" + e))
        for name, d in self.dsems:
            d.sem = stack.enter_context(nc.semaphore("d_" + name))
        for e in QUEUES:
            c = 0
            for o in self.ops[e]:
                if o.sig:
                    c += 1
                    o.sigval = c
        self.stats = {e: (len(self.ops[e]), sum(1 for o in self.ops[e] if o.sig),
                          sum(len(o.waits) for o in self.ops[e])) for e in QUEUES}

        def run(e, engobj):
            for o in self.ops[e]:
                for d in o.waits:
                    if d.op is not None:
                        engobj.wait_ge(esem[d.op.eng], d.op.sigval)
                    else:
                        engobj.wait_ge(d.dsem.sem, d.val)
                if o.fn is None:
                    continue
                inst = o.fn(engobj)
                if o.dsem is not None:
                    inst.then_inc(o.dsem.sem, 16)
                elif o.sig:
                    inst.then_inc(esem[e], 1)

        block = stack.enter_context(nc.Block())

        @block.sync
        def _(eng):
            run("sp", eng)

        @block.gpsimd
        def _(eng):
            run("pool", eng)

        @block.scalar
        def _(eng):
            run("act", eng)

        @block.vector
        def _(eng):
            run("dve", eng)

        @block.tensor
        def _(eng):
            run("pe", eng)


D = 1024
T = 2048
TC = 256
NT = 16
EPS = 1e-6
COL_Q = 1024
COL_K = 1280
COL_V = 1536
COL_R = 2048
COL_DEC = 2560
COL_GATE = 2592
NVEC = 224
V_C = 0
V_BMOD = 16
V_GPRE1 = 64
V_GPRE2 = 72
V_CW = 80
V_CB = 204
V_LNG = 208
V_LNB = 212
V_GN = 216
LN8 = math.log(0.125)


def build(dbg=None, upto=9, simcompat=False):
    nc = bass.Bass("TRN2", target_bir_lowering=False)

    def din(name, shape):
        return nc.dram_tensor(name, shape, F32, kind="ExternalInput").ap()

    x_d = din("x", [T, D])
    ctx_d = din("ctx", [TC, D])
    vecs_d = din("vecs", [128, NVEC])
    gpost_d = din("gpost", [128, 2048])
    wda_d = din("wda", [33, 512])
    consts_d = din("consts", [128, 1408])
    w_mod = din("w_mod", [D, 6144]).rearrange("(k p) n -> p k n", p=128)
    w_in = din("w_in", [D, 4640]).rearrange("(k p) n -> p k n", p=128)
    w_co = din("w_conv_out", [512, D]).rearrange("(k p) n -> p k n", p=128)
    w_go = din("w_gla_out", [512, D]).rearrange("(k p) n -> p k n", p=128)
    w_out = din("w_out", [D, D]).rearrange("(k p) n -> p k n", p=128)
    w_ff1_raw = din("w_ff1", [D, 4096])
    w_ff2_raw = din("w_ff2", [4096, D])
    wf1b_raw = nc.dram_tensor("wf1b", [D, 4096], BF16, kind="Internal").ap()
    wf2b_raw = nc.dram_tensor("wf2b", [4096, D], BF16, kind="Internal").ap()
    w_ff1 = wf1b_raw.rearrange("(k p) n -> p k n", p=128)
    w_ff2 = wf2b_raw.rearrange("(k p) n -> p k n", p=128)
    out_d = nc.dram_tensor("out", [T, D], F32, kind="ExternalOutput").ap()
    hmid_d = nc.dram_tensor("hmid", [T, D], F32, kind="Internal").ap()
    dbg_out = {}

    with contextlib.ExitStack() as st:
        def sb(name, shape, dt):
            return st.enter_context(nc.sbuf_tensor(name, shape, dt))

        P = Prog(nc)
        UT = sb("UT", [128, 8, 2048], BF16)
        WW = sb("WW", [128, 32768], BF16)
        R1 = sb("R1", [128, 8192], F32)
        R2 = sb("R2", [128, 8192], BF16)
        R3 = sb("R3", [128, 4, 2048], BF16)
        XT = sb("XT", [128, 4096], F32)
        GG = sb("GG", [128, 2048], F32)
        SC = sb("SC", [128, 4096], F32)
        VEC = sb("VEC", [128, NVEC], F32)
        SM = sb("SM", [128, 256], F32)
        IDF = sb("IDF", [128, 128], F32)
        ONESF = sb("ONESF", [128, 128], F32)
        CB = sb("CB", [128, 1408], BF16)
        WDA = sb("WDA", [33, 512], BF16)
        SCV = sb("SCV", [128, 16], BF16)

        MOD = SM[:, 0:48]
        CMOD = SM[:, 48:64]
        GS = SM[:, 64:88]
        STAT = SM[:, 88:120]
        RSTD = SM[:, 120:152]
        EL = SM[:, 152:224].rearrange("p (c n) -> p c n", c=4)
        TMP8 = SM[:, 224:256]
        IDB = CB[:, 0:128]
        MU4 = CB[:, 128:640]
        ML4 = CB[:, 640:1152]
        TRIU = CB[:, 1152:1280]
        TRIL = CB[:, 1280:1408]

        ONES = sb("ONES", [128, 256], BF16)
        ONESB = ONES[:, 0:128]
        ONES512 = ONES[:, 128:256]

        SCB = SC[:].bitcast(BF16)
        XTB = XT[:].bitcast(BF16)
        R1B = R1[:].bitcast(BF16)

        PSF = [st.enter_context(nc.psum_tensor("ps%d" % i, [128, 512], F32)) for i in range(7)]
        PSFR = [Res("ps%d" % i) for i in range(7)]
        PSB = st.enter_context(nc.psum_tensor("psb", [128, 1024], BF16))
        PSBR = Res("psb")
        bank_i = [0]

        nbanks = [7]
        PSF8 = PSB[:].bitcast(F32)

        class _BankView:
            def __init__(self, ap):
                self.ap = ap

            def __getitem__(self, key):
                return self.ap[key]

        def bank():
            i = bank_i[0] % nbanks[0]
            bank_i[0] += 1
            if i == 7:
                assert PSBR.w is None or len(PSBR.r) > 0
                return _BankView(PSF8), PSBR
            assert PSFR[i].w is None or len(PSFR[i].r) > 0, "PSUM bank %d re-issued before its evacuation was emitted" % i
            return PSF[i], PSFR[i]

        def ACT(out, in_, func, reads, writes, **kw):
            return P.op("act", lambda e: e.activation(out=out, in_=in_, func=func, **kw), reads, writes)

        def MM(out, lhsT, rhs, start, stop, reads, writes, tile_position=None):
            if tile_position is not None:
                return P.op("pe", lambda e: e.matmul(out, lhsT=lhsT, rhs=rhs, start=start, stop=stop,
                                                     tile_position=tile_position), reads, writes)
            return P.op("pe", lambda e: e.matmul(out, lhsT=lhsT, rhs=rhs, start=start, stop=stop), reads, writes)

        def TR(out, in_, ident, reads, writes):
            return P.op("pe", lambda e: e.transpose(out=out, in_=in_, identity=ident), reads, writes)

        def TT(eng, out, in0, in1, op, reads, writes):
            return P.op(eng, lambda e: e.tensor_tensor(out=out, in0=in0, in1=in1, op=op), reads, writes)

        def TS(eng, out, in0, s1, s2, op0, op1, reads, writes):
            if s2 is None:
                return P.op(eng, lambda e: e.tensor_scalar(out=out, in0=in0, scalar1=s1, scalar2=None, op0=op0), reads, writes)
            return P.op(eng, lambda e: e.tensor_scalar(out=out, in0=in0, scalar1=s1, scalar2=s2, op0=op0, op1=op1), reads, writes)

        def STT(eng, out, in0, scalar, in1, op0, op1, reads, writes):
            return P.op(eng, lambda e: e.scalar_tensor_tensor(out=out, in0=in0, scalar=scalar, in1=in1, op0=op0, op1=op1), reads, writes)

        def CP(eng, out, in_, reads, writes):
            return P.op(eng, lambda e: e.tensor_copy(out=out, in_=in_), reads, writes)

        def MSET(eng, ap, val, writes):
            return P.op(eng, lambda e: e.memset(ap, val), (), writes)

        def DMA(q, ds, out, in_, reads, writes):
            return P.dma(q, ds, lambda e: e.dma_start(out=out, in_=in_), reads, writes)

        def RSTDOP(out, in_, tmp, rin, rtmp, rout):
            ACT(tmp, in_, AF.Ln, [rin], [rtmp], bias=EPS)
            ACT(out, tmp, AF.Exp, [rtmp], [rout], scale=-0.5)

        def dump(name, ap, shape, reads):
            if dbg is None or name not in dbg:
                return
            t = nc.dram_tensor("dbg_" + name, list(shape), ap.dtype, kind="ExternalOutput").ap()
            ds = P.dsem("dbg_" + name)
            DMA("sp", ds, t, ap, reads, [Res()])
            dbg_out[name] = ds

        rMOD2 = Res("mod2"); rVEC = Res("vec"); rCON = Res("con"); rGG = [Res("gg0"), Res("gg1")]
        rSM = Res("sm"); rMOD = Res("mod"); rSCV = Res("scv"); rWDA = Res("wda")
        rUT = [Res("ut%d" % i) for i in range(4)]
        rCUT = Res("cut")
        rWW = [Res("ww%d" % i) for i in range(4)]
        rONES = Res("ones")

        rIDF = Res("idf")
        DMA("sp", P.dsem("c_vec"), VEC[:], vecs_d, [], [rVEC])
        DMA("sp", P.dsem("c_idf"), IDF[:], consts_d[:, 0:128], [], [rIDF])
        DMA("pool", P.dsem("c_cb"), CB[:, 0:1408], consts_d[:, 0:1408], [], [rCON])
        DMA("pool", P.dsem("c_wda"), WDA[:], wda_d, [], [rWDA])
        MSET("dve", ONESF[:], 1.0, [rONES])
        MSET("dve", ONES[:, 0:128], 1.0 / 128, [rONES])
        MSET("dve", ONES[:, 128:256], 1.0 / 512, [rONES])

        OUT_EVS = []

        def body():
            ACT(SCV[:], VEC[:, V_C:V_C + 16], AF.Silu, [rVEC], [rSCV])
            d_wm = [P.dsem("wm%d" % i) for i in range(4)]
            rWM = rWW
            WMv = [WW[:, i * 8192:(i + 1) * 8192].rearrange("p (k n) -> p k n", k=8) for i in range(4)]
            psm, rpsm = bank()
            for j in (0, 1):
                DMA("pool", d_wm[2 + j], WMv[2 + j], w_mod[:, :, j * 1024:(j + 1) * 1024], [], [rWM[2 + j]])
            for j in (0, 1):
                for ccl in range(8):
                    cc = j * 8 + ccl
                    for k in range(8):
                        MM(psm[:, cc * 2:cc * 2 + 2], WMv[2 + j][:, k, ccl * 128:(ccl + 1) * 128], SCV[:, 2 * k:2 * k + 2],
                           k == 0, k == 7, [rWM[2 + j], rSCV], [rpsm])
            psm3 = psm[:, 0:96].rearrange("p (c j) -> p c j", j=2)
            TT("dve", MOD[:, 0:16], psm3[:, 0:16, 0], VEC[:, V_BMOD:V_BMOD + 16], ALU.add, [rpsm, rVEC], [rMOD])
            TT("dve", CMOD, psm3[:, 0:16, 1], VEC[:, V_BMOD:V_BMOD + 16], ALU.add, [rpsm, rVEC], [rMOD])
            STT("dve", GS[:, 0:8], MOD[:, 8:16], 1.0, VEC[:, V_GPRE1:V_GPRE1 + 8], ALU.add, ALU.mult, [rMOD, rVEC], [rSM])
            STT("dve", GS[:, 8:16], CMOD[:, 8:16], 1.0, VEC[:, V_GPRE1:V_GPRE1 + 8], ALU.add, ALU.mult, [rMOD, rVEC], [rSM])

            WQK = WW[:, 0:4096].rearrange("p (k n) -> p k n", k=8)
            WDEC = WW[:, 4096:4352].rearrange("p (k n) -> p k n", k=8)
            WV = WW[:, 4352:8448].rearrange("p (k n) -> p k n", k=8)
            WR = WW[:, 8448:12544].rearrange("p (k n) -> p k n", k=8)
            d_wa = P.dsem("wa1")
            rWA = Res("wa1")
            DMA("pool", d_wa, WDEC, w_in[:, :, COL_DEC:COL_DEC + 32], [], [rWA])
            DMA("pool", d_wa, WQK, w_in[:, :, COL_Q:COL_Q + 512], [], [rWA])
            DMA("pool", d_wa, WV, w_in[:, :, COL_V:COL_V + 512], [], [rWA])
            rWR = Res("wr")
            d_wr = P.dsem("wr")

            GGB = GG[:].bitcast(BF16).rearrange("p (k n) -> p k n", k=8)
            rGGB = Res("ggb")
            d_gb = P.dsem("ggb")

            rCV1 = [Res("cv1_%d" % i) for i in range(4)]
            rCV2 = [Res("cv2_%d" % i) for i in range(4)]

            def adaln_piece(i):
                col0 = 2048 + i * 512
                DMA("pool", d_gb, GGB, w_mod[:, :, col0:col0 + 512], [], [rGGB])
                if i >= 4:
                    j = i - 4
                    DMA("pool", P.dsem("cv1_%d" % j), wf1b_raw[j * 256:(j + 1) * 256, :], w_ff1_raw[j * 256:(j + 1) * 256, :], [], [rCV1[j]])
                pp, rpp = bank()
                for ccl in range(4):
                    for k in range(8):
                        MM(pp[:, ccl * 2:ccl * 2 + 2], GGB[:, k, ccl * 128:(ccl + 1) * 128], SCV[:, 2 * k:2 * k + 2],
                           k == 0, k == 7, [rGGB, rSCV], [rpp])
                cc0 = 16 + i * 4
                pp3 = pp[:, 0:8].rearrange("p (c j) -> p c j", j=2)
                TT("dve", MOD[:, cc0:cc0 + 4], pp3[:, :, 0], VEC[:, V_BMOD + cc0:V_BMOD + cc0 + 4], ALU.add, [rpp, rVEC], [rMOD2])

            def adaln_finish():
                STT("dve", GS[:, 16:24], MOD[:, 32:40], 1.0, VEC[:, V_GPRE2:V_GPRE2 + 8], ALU.add, ALU.mult, [rMOD2, rVEC], [rMOD2])
                DMA("sp", P.dsem("c_gg"), GG[:], gpost_d, [], rGG + [rGGB])
                DG = SC[:, 0:1024].rearrange("p (c n) -> p c n", c=8)
                rDG = Res("dg")
                for g in range(2):
                    gt0 = 16 if g == 0 else 40
                    for c in range(8):
                        TS("dve", DG[:, c, :], IDF[:], MOD[:, gt0 + c:gt0 + c + 1], None, ALU.mult, None, [rIDF, rMOD2], [rDG])
                    for half in range(2):
                        pg, rpg = bank()
                        for c4 in range(4):
                            c = half * 4 + c4
                            MM(pg[:, c4 * 128:(c4 + 1) * 128], ONESF[:], DG[:, c, :], True, True, [rONES, rDG], [rpg])
                        sl = GG[:, g * 1024 + half * 512:g * 1024 + (half + 1) * 512]
                        TT("dve", sl, pg[:], sl, ALU.mult, [rpg, rGG[g]], [rGG[g]])

            if upto == 0:
                return
            XS = R1[:].rearrange("p (s n) -> p s n", s=8)
            rXS = [Res("xs%d" % i) for i in range(8)]
            d_xs = [P.dsem("xs%d" % i) for i in range(8)]
            JUNK = SCB[:, 4096:5120]
            JUNKD = XTB[:, 6400:7424]
            CUT = XTB[:, 0:2048].rearrange("p (k n) -> p k n", k=8)
            blocks = [("c", 0, 2)] + [("x", b, 4) for b in range(4)]
            slot_ctr = [0]
            blk_slots = {}
            blk_rst = {}

            def p1_stats(bi):
                kind, bidx, ntl = blocks[bi]
                slots = []
                for t in range(ntl):
                    s_ = slot_ctr[0] % 8
                    slot_ctr[0] += 1
                    slots.append(s_)
                    src = ctx_d[t * 128:(t + 1) * 128, :] if kind == "c" else x_d[(bidx * 4 + t) * 128:(bidx * 4 + t + 1) * 128, :]
                    DMA("sp", d_xs[s_], XS[:, s_, :], src, [], [rXS[s_]])
                st0 = 0 if kind == "c" else 2 + bidx * 4
                rst = Res("stat")
                for t, s_ in enumerate(slots):
                    if kind == "c":
                        ACT(JUNK, XS[:, s_, :], AF.Square, [rXS[s_]], [rst], scale=1.0 / 32, accum_out=STAT[:, st0 + t:st0 + t + 1])
                    else:
                        P.op("dve", lambda e, s_=s_, cc_=st0 + t: e.scalar_tensor_tensor(
                            out=JUNKD, in0=XS[:, s_, :], scalar=1.0 / 1024, in1=XS[:, s_, :], op0=ALU.mult, op1=ALU.mult,
                            accum_out=STAT[:, cc_:cc_ + 1]), [rXS[s_]], [rst])
                RSTDOP(RSTD[:, st0:st0 + ntl], STAT[:, st0:st0 + ntl], TMP8[:, 0:ntl], rst, rst, rst)
                for t, s_ in enumerate(slots):
                    TS("dve", XS[:, s_, :], XS[:, s_, :], RSTD[:, st0 + t:st0 + t + 1], None, ALU.mult, None, [rst, rXS[s_]], [rXS[s_]])
                blk_slots[bi] = slots

            def p1_transpose(bi):
                kind, bidx, ntl = blocks[bi]
                slots = blk_slots[bi]
                for c in range(8):
                    pb, rpb = bank()
                    for t, s_ in enumerate(slots):
                        TR(pb[:, t * 128:(t + 1) * 128], XS[:, s_, c * 128:(c + 1) * 128], IDF[:], [rXS[s_], rIDF], [rpb])
                    if kind == "c":
                        ACT(CUT[:, c, :], pb[:, 0:256], AF.Identity, [rpb, rSM, rMOD], [rCUT],
                            scale=GS[:, 8 + c:9 + c], bias=CMOD[:, c:c + 1])
                    else:
                        ACT(UT[:, c, bidx * 512:(bidx + 1) * 512], pb[:], AF.Identity, [rpb, rSM, rMOD], [rUT[bidx]],
                            scale=GS[:, c:c + 1], bias=MOD[:, c:c + 1])

            p1_stats(0)
            for bi in range(5):
                if bi + 1 < 5:
                    p1_stats(bi + 1)
                p1_transpose(bi)
            dump("ut", UT[:], [128, 8, 2048], rUT)

            if upto == 1:
                return
            LA = WW[:, 12544:14592].rearrange("p (t n) -> p t n", t=4)
            CK = WW[:, 14592:15616].rearrange("p (c n) -> p c n", c=4)
            QK = WW[:, 16384:32768].rearrange("p (c n) -> p c n", c=8)

            ZT = XTB[0:33, 2048:4352]
            CKT = XTB[:, 4352:5376].rearrange("p (c n d) -> p c n d", c=4, n=2)
            CV = XTB[:, 5376:6400].rearrange("p (n d) -> p n d", n=2)
            KT = R1B[:, 0:8192].rearrange("p (c n d) -> p c n d", c=4, n=16)
            SS = R1B[:, 8192:16384].rearrange("p (c n d) -> p c n d", c=4, n=16)
            V = R2[:].rearrange("p (n d) -> p n d", n=16)
            SR = R3
            E_ = [SC[:, 0:512], SC[:, 512:1024]]
            EN_ = [SC[:, 1024:1536], SC[:, 1536:2048]]
            EQ_ = [SC[:, 2048:2560], SC[:, 2560:3072]]
            RR = SC[:, 3072:4096].rearrange("p (c b d) -> p c b d", c=4, b=2)
            rE = [Res("e0"), Res("e1")]
            rEN = [Res("en0"), Res("en1")]
            rEQ = [Res("eq0"), Res("eq1")]
            rZT = Res("zt"); rLA = Res("la"); rEL = Res("el"); rCK = Res("ck"); rCKT = Res("ckt"); rCV = Res("cv")
            rQK = [Res("qk%d" % i) for i in range(4)]
            rKT = [Res("kt%d" % i) for i in range(4)]
            rV = [Res("v%d" % i) for i in range(4)]
            rSR = [Res("sr%d" % i) for i in range(4)]
            MSET("dve", XTB[32:33, 2048:4352], 1.0, [rZT])
            ei = [0]

            def blk_params(bi):
                kind, bidx, ntl = blocks[bi]
                ntok = ntl * 128
                if kind == "c":
                    return kind, bidx, ntl, ntok, CUT, rCUT, 0, 0, 0
                return (kind, bidx, ntl, ntok, UT[:, :, bidx * 512:(bidx + 1) * 512], rUT[bidx], bidx * 512,
                        256 + bidx * 512, 2 + bidx * 4)

            def a1_s1(bi):
                kind, bidx, ntl, ntok, U, rU, tok0, zoff, ch0 = blk_params(bi)
                pz, rpz = bank()
                for k in range(8):
                    MM(pz[0:32, 0:ntok], WDEC[:, k, :], U[:, k, :], k == 0, k == 7, [rWA, rU], [rpz])
                ACT(ZT[0:32, zoff:zoff + ntok], pz[0:32, 0:ntok], AF.Copy, [rpz], [rZT])
                pvs = []
                for t in range(ntl):
                    pv, rpv = bank()
                    for k in range(8):
                        MM(pv[:], U[:, k, t * 128:(t + 1) * 128], WV[:, k, :], k == 0, k == 7, [rU, rWA], [rpv])
                    if kind == "c":
                        ACT(CV[:, t, :], pv[:], AF.Copy, [rpv], [rCV])
                    else:
                        ACT(V[:, bidx * 4 + t, :], pv[:], AF.Copy, [rpv], [rV[bidx]])
                    if t >= 1:
                        tt_ = t - 1
                        pl, rpl = bank()
                        MM(pl[:], ZT[:, zoff + tt_ * 128:zoff + (tt_ + 1) * 128], WDA[:], True, True, [rZT, rWDA], [rpl])
                        i = ei[0] % 2
                        ei[0] += 1
                        ACT(E_[i], pl[:], AF.Exp, [rpl], [rE[i]], scale=-1.0)
                        ACT(LA[:, tt_, :], E_[i], AF.Ln, [rE[i]], [rLA], bias=1.0)
                tt_ = ntl - 1
                pl, rpl = bank()
                MM(pl[:], ZT[:, zoff + tt_ * 128:zoff + (tt_ + 1) * 128], WDA[:], True, True, [rZT, rWDA], [rpl])
                i = ei[0] % 2
                ei[0] += 1
                ACT(E_[i], pl[:], AF.Exp, [rpl], [rE[i]], scale=-1.0)
                ACT(LA[:, tt_, :], E_[i], AF.Ln, [rE[i]], [rLA], bias=1.0)

            tr_jobs = {}

            def a1_s2(bi):
                kind, bidx, ntl, ntok, U, rU, tok0, zoff, ch0 = blk_params(bi)
                jobs = []
                for hp in range(2):
                    pk, rpk = bank()
                    for k in range(8):
                        MM(pk[:, 0:ntok], WQK[:, k, 256 + hp * 128:256 + (hp + 1) * 128], U[:, k, :], k == 0, k == 7, [rWA, rU], [rpk])
                    if kind == "x":
                        pq, rpq = bank()
                        for k in range(8):
                            MM(pq[:, 0:ntok], WQK[:, k, hp * 128:(hp + 1) * 128], U[:, k, :], k == 0, k == 7, [rWA, rU], [rpq])
                    for d in range(2):
                        combo = d * 2 + hp
                        pgm, rpgm = bank()
                        tri = TRIU if d == 0 else TRIL
                        for t in range(ntl):
                            MM(pgm[:, t * 128:(t + 1) * 128], LA[:, t, d * 256 + hp * 128:d * 256 + (hp + 1) * 128], tri,
                               True, True, [rLA, rCON], [rpgm])
                        i = ei[0] % 2
                        ei[0] += 1
                        ACT(EN_[i][:, 0:ntok], pgm[:, 0:ntok], AF.Exp, [rpgm], [rEN[i]], scale=1.0 / 16)
                        col0 = 127 if d == 0 else 0
                        pgl = pgm[:, 0:ntok].rearrange("p (t n) -> p t n", n=128)[:, :, col0]
                        ACT(EL[:, combo, ch0:ch0 + ntl], pgl, AF.Exp, [rpgm], [rEL], scale=-1.0 / 16)
                        if kind == "x":
                            ACT(EQ_[i][:, 0:ntok], pgm[:, 0:ntok], AF.Exp, [rpgm], [rEQ[i]], scale=-1.0 / 16, bias=LN8)
                            TT("dve", QK[:, (2 + d) * 2 + hp, tok0:tok0 + ntok], pk[:, 0:ntok], EN_[i][:, 0:ntok], ALU.mult,
                               [rpk, rEN[i]], [rQK[bidx], rWW[2], rWW[3]])
                            TT("dve", QK[:, d * 2 + hp, tok0:tok0 + ntok], pq[:, 0:ntok], EQ_[i][:, 0:ntok], ALU.mult,
                               [rpq, rEQ[i]], [rQK[bidx], rWW[2], rWW[3]])
                            jobs.append((combo, QK[:, (2 + d) * 2 + hp, tok0:tok0 + ntok], rQK[bidx]))
                        else:
                            TT("dve", CK[:, combo, 0:ntok], pk[:, 0:ntok], EN_[i][:, 0:ntok], ALU.mult, [rpk, rEN[i]], [rCK])
                            jobs.append((combo, CK[:, combo, 0:ntok], rCK))
                tr_jobs[bi] = jobs

            seqs = {0: [("c", 0), ("c", 1)] + [("x", i) for i in range(16)],
                    1: [("c", 1), ("c", 0)] + [("x", i) for i in range(15, -1, -1)]}
            rRR = [[Res("rr%d_%d" % (c, b)) for b in range(2)] for c in range(4)]
            rSS = [[Res("ss%d_%d" % (c, i)) for i in range(16)] for c in range(4)]

            def chain_step(step, combos):
                pu, rpu = bank()
                info = []
                for j_c, combo in enumerate(combos):
                    uc = slice(j_c * 128, (j_c + 1) * 128)
                    d, hp = combo // 2, combo % 2
                    kind, ci = seqs[d][step]
                    if kind == "c":
                        lhs = CKT[:, combo, ci, :]; rl = rCKT
                        rhs = CV[:, ci, hp * 256:(hp + 1) * 256]; rr_ = rCV
                    else:
                        lhs = KT[:, combo, ci, :]; rl = rKT[ci // 4]
                        rhs = V[:, ci, hp * 256:(hp + 1) * 256]; rr_ = rV[ci // 4]
                    MM(pu[0:64, uc], lhs[:, 0:64], rhs[:, 0:128], True, True, [rl, rr_], [rpu])
                    MM(pu[64:128, uc], lhs[:, 64:128], rhs[:, 128:256], True, True, [rl, rr_], [rpu], tile_position=(0, 64))
                    info.append((combo, uc, d, kind, ci))
                cur = step % 2
                prv = 1 - cur
                for combo, uc, d, kind, ci in info:
                    if step == 0:
                        CP("dve", RR[:, combo, cur, :], pu[:, uc], [rpu], [rRR[combo][cur]])
                    else:
                        pk_, pci = seqs[d][step - 1]
                        pel = pci if pk_ == "c" else 2 + pci
                        if kind == "x":
                            ACT(SS[:, combo, ci, :], RR[:, combo, prv, :], AF.Copy, [rRR[combo][prv], rEL], [rSS[combo][ci]] + rXS[4:8],
                                scale=EL[:, combo, pel:pel + 1])
                        STT("dve", RR[:, combo, cur, :], RR[:, combo, prv, :], EL[:, combo, pel:pel + 1], pu[:, uc],
                            ALU.mult, ALU.add, [rRR[combo][prv], rEL, rpu], [rRR[combo][cur]])

            def a1_s3(bi):
                kind, bidx, ntl, ntok, U, rU, tok0, zoff, ch0 = blk_params(bi)
                if kind == "x":
                    for c in range(4):
                        pr, rpr = bank()
                        for k in range(8):
                            MM(pr[:], WR[:, k, c * 128:(c + 1) * 128], U[:, k, :], k == 0, k == 7, [rWR, rU], [rpr])
                        ACT(SR[:, c, tok0:tok0 + 512], pr[:], AF.Silu, [rpr], [rSR[bidx]])
                        if c == 0:
                            adaln_piece(bidx * 2)
                for combo, ksrc, rks in tr_jobs[bi]:
                    half = combo % 2
                    for t in range(ntl):
                        TR(PSB[:, half * 512 + t * 128:half * 512 + (t + 1) * 128], ksrc[:, t * 128:(t + 1) * 128], IDB,
                           [rks, rCON], [PSBR])
                    if kind == "x":
                        CP("dve", KT[:, combo, bidx * 4:bidx * 4 + 4, :], PSB[:, half * 512:half * 512 + 512].rearrange("p (t n) -> p t n", n=128),
                           [PSBR], [rKT[bidx]] + rXS[0:4])
                    else:
                        CP("dve", CKT[:, combo, :, :], PSB[:, half * 512:half * 512 + 256].rearrange("p (t n) -> p t n", n=128),
                           [PSBR], [rCKT])
                fsteps = [0, 1] if kind == "c" else [2 + bidx * 4 + t_ for t_ in range(4)]
                for st_ in fsteps:
                    chain_step(st_, (0, 1))
                if kind == "x":
                    adaln_piece(bidx * 2 + 1)

            DMA("pool", d_wr, WR, w_in[:, :, COL_R:COL_R + 512], list(rXS), [rWR])
            a1_s1(0)
            a1_s2(0)
            for bi in range(1, 5):
                a1_s1(bi)
                a1_s3(bi - 1)
                a1_s2(bi)
            a1_s3(4)
            nbanks[0] = 8
            if upto == 1.5:
                return
            dump("qk", QK, [128, 8, 2048], rQK)
            dump("el", SM[:, 152:224], [128, 72], [rEL])

            bufsets = [
                dict(T1=SC[:, 0:512], T2=SC[:, 512:1024], RS=SC[:, 1024:1536], T3=SC[:, 1536:2048],
                     MT=SCB[:, 5120:5632], SQ=SCB[:, 5632:6144]),
                dict(T1=XT[:, 0:512], T2=XT[:, 512:1024], RS=XT[:, 1024:1536], T3=XT[:, 1536:2048],
                     MT=XTB[:, 6400:6912], SQ=XTB[:, 6912:7424]),
            ]
            for bs_ in bufsets:
                for nm in ("T1", "T2", "RS", "T3", "MT", "SQ"):
                    bs_["r" + nm] = Res(nm)
            po_l = {}
            MTall = WW[:, 0:8192].rearrange("p (c n) -> p c n", c=16)
            rMTall = [Res("mt%d" % c) for c in range(16)]

            def stage1(c):
                b = c // 4
                tk = slice(c * 128, (c + 1) * 128)
                B_ = bufsets[c % 2]
                T1, T2 = B_["T1"], B_["T2"]
                rT1, rT2 = B_["rT1"], B_["rT2"]
                MT = MTall[:, c, :]
                rMT = rMTall[c]
                pxy = [bank(), bank()]
                for j_ in range(4):
                    for hf in range(2):
                        rows = slice(hf * 64, (hf + 1) * 64)
                        px, rpx = pxy[hf]
                        hp = j_ % 2
                        if j_ < 2:
                            MM(px[:, hp * 128:(hp + 1) * 128], QK[rows, 4 + hp, tk], QK[rows, 0 + hp, tk], True, True, [rQK[b]], [rpx])
                        else:
                            MM(px[:, 256 + hp * 128:256 + (hp + 1) * 128], QK[rows, 6 + hp, tk], QK[rows, 2 + hp, tk], True, True, [rQK[b]], [rpx])
                TT("dve", T1, pxy[0][0][:], MU4, ALU.mult, [pxy[0][1], rCON], [rT1])
                TT("dve", T2, pxy[1][0][:], MU4, ALU.mult, [pxy[1][1], rCON], [rT2])
                MT3 = MT.rearrange("p (hp hf n) -> p hp hf n", hp=2, hf=2)
                TT("pool", MT3[:, :, 0, :], T1[:, 0:256].rearrange("p (a n) -> p a n", a=2), T1[:, 256:512].rearrange("p (a n) -> p a n", a=2),
                   ALU.add, [rT1], [rMT, rWA])
                TT("pool", MT3[:, :, 1, :], T2[:, 0:256].rearrange("p (a n) -> p a n", a=2), T2[:, 256:512].rearrange("p (a n) -> p a n", a=2),
                   ALU.add, [rT2], [rMT])

            bufsets = [
                dict(T1=SC[:, 0:512], T2=SC[:, 512:1024], RS=SC[:, 1024:1536], T3=SC[:, 1536:2048],
                     MT=SCB[:, 5120:5632], SQ=SCB[:, 5632:6144]),
                dict(T1=XT[:, 0:512], T2=XT[:, 512:1024], RS=XT[:, 1024:1536], T3=XT[:, 1536:2048],
                     MT=XTB[:, 6400:6912], SQ=XTB[:, 6912:7424]),
            ]
            for bs_ in bufsets:
                for nm in ("T1", "T2", "RS", "T3", "MT", "SQ"):
                    bs_["r" + nm] = Res(nm)
            po_l = {}
            MTall = WW[:, 0:8192].rearrange("p (c n) -> p c n", c=16)
            rMTall = [Res("mt%d" % c) for c in range(16)]

            def stage1(c):
                b = c // 4
                tk = slice(c * 128, (c + 1) * 128)
                B_ = bufsets[c % 2]
                T1, T2 = B_["T1"], B_["T2"]
                rT1, rT2 = B_["rT1"], B_["rT2"]
                MT = MTall[:, c, :]
                rMT = rMTall[c]
                pxy = [bank(), bank()]
                for j_ in range(4):
                    for hf in range(2):
                        rows = slice(hf * 64, (hf + 1) * 64)
                        px, rpx = pxy[hf]
                        hp = j_ % 2
                        if j_ < 2:
                            MM(px[:, hp * 128:(hp + 1) * 128], QK[rows, 4 + hp, tk], QK[rows, 0 + hp, tk], True, True, [rQK[b]], [rpx])
                        else:
                            MM(px[:, 256 + hp * 128:256 + (hp + 1) * 128], QK[rows, 6 + hp, tk], QK[rows, 2 + hp, tk], True, True, [rQK[b]], [rpx])
                TT("dve", T1, pxy[0][0][:], MU4, ALU.mult, [pxy[0][1], rCON], [rT1])
                TT("dve", T2, pxy[1][0][:], MU4, ALU.mult, [pxy[1][1], rCON], [rT2])
                MT3 = MT.rearrange("p (hp hf n) -> p hp hf n", hp=2, hf=2)
                TT("pool", MT3[:, :, 0, :], T1[:, 0:256].rearrange("p (a n) -> p a n", a=2), T1[:, 256:512].rearrange("p (a n) -> p a n", a=2),
                   ALU.add, [rT1], [rMT, rWA])
                TT("pool", MT3[:, :, 1, :], T2[:, 0:256].rearrange("p (a n) -> p a n", a=2), T2[:, 256:512].rearrange("p (a n) -> p a n", a=2),
                   ALU.add, [rT2], [rMT])

            if upto == 1.7:
                return
            dump("ss", R1B[:, 8192:16384], [128, 8192], [r_ for l_ in rSS for r_ in l_])

            def stage2(c, pos):
                b = c // 4
                tk = slice(c * 128, (c + 1) * 128)
                B_ = bufsets[pos % 2]
                SQ, rSQ = B_["SQ"], B_["rSQ"]
                MT = MTall[:, c, :]
                rMT = rMTall[c]
                pA, rpA = bank()
                pB, rpB = bank()
                po_l[c] = ((pA, rpA), (pB, rpB))
                for hp in range(2):
                    cs = slice(hp * 128, (hp + 1) * 128)
                    for hf, (pp_, rpp_) in enumerate(((pA, rpA), (pB, rpB))):
                        h = hp * 2 + hf
                        MM(pp_[:, cs], V[:, c, h * 128:(h + 1) * 128], MT[:, h * 128:(h + 1) * 128], True, False, [rV[b], rMT], [rpp_])
                    for d_ in range(2):
                        for hf, (pp_, rpp_) in enumerate(((pA, rpA), (pB, rpB))):
                            rows = slice(hf * 64, (hf + 1) * 64)
                            MM(pp_[:, cs], SS[rows, 2 * d_ + hp, c, :], QK[rows, 2 * d_ + hp, tk], False, d_ == 1,
                               [rSS[2 * d_ + hp][c], rQK[b]], [rpp_])
                SQ4 = SQ.rearrange("p (hp hf n) -> p hp hf n", hp=2, hf=2)
                ACT(SQ4[:, :, 0, :], pA[:, 0:256].rearrange("p (a n) -> p a n", a=2), AF.Square, [rpA], [rSQ])
                ACT(SQ4[:, :, 1, :], pB[:, 0:256].rearrange("p (a n) -> p a n", a=2), AF.Square, [rpB], [rSQ])

            def stage3(c, pos):
                b = c // 4
                tk = slice(c * 128, (c + 1) * 128)
                B_ = bufsets[pos % 2]
                RS, T3, SQ, rRS, rT3, rSQ = B_["RS"], B_["T3"], B_["SQ"], B_["rRS"], B_["rT3"], B_["rSQ"]
                (pA, rpA), (pB, rpB) = po_l[c]
                pms, rpms = bank()
                MM(pms[:], ONESB, SQ, True, True, [rONES, rSQ], [rpms])
                RSTDOP(RS, pms[:], RS, rpms, rRS, rRS)
                T34 = T3.rearrange("p (hp hf n) -> p hp hf n", hp=2, hf=2)
                RS4 = RS.rearrange("p (hp hf n) -> p hp hf n", hp=2, hf=2)
                STT("dve", T34[:, :, 0, :], pA[:, 0:256].rearrange("p (a n) -> p a n", a=2), VEC[:, V_GN:V_GN + 1], RS4[:, :, 0, :],
                    ALU.mult, ALU.mult, [rpA, rVEC, rRS], [rT3])
                STT("dve", T34[:, :, 1, :], pB[:, 0:256].rearrange("p (a n) -> p a n", a=2), VEC[:, V_GN:V_GN + 1], RS4[:, :, 1, :],
                    ALU.mult, ALU.mult, [rpB, rVEC, rRS], [rT3])
                TT("pool", SR[:, :, tk], T3.rearrange("p (h n) -> p h n", h=4), SR[:, :, tk], ALU.mult, [rT3, rSR[b]], [rSR[b]])

            pend = []
            npos = [0]
            for step in range(18):
                chain_step(step, (2, 3))
                if pend:
                    stage3(*pend.pop(0))
                if step < 16:
                    stage1(15 - step)
                for c_ in range(16):
                    if 17 - c_ == step:
                        stage2(c_, npos[0])
                        pend.append((c_, npos[0]))
                        npos[0] += 1
            while pend:
                stage3(*pend.pop(0))
            dump("og", SR[:], [128, 4, 2048], rSR)

            if upto == 2:
                return
            P.barrier()
            nbanks[0] = 7
            adaln_finish()
            WG = WW[:, 0:8192].rearrange("p (k n) -> p k n", k=8)
            d_wg = P.dsem("wglu")
            rWG = Res("wglu")
            DMA("pool", d_wg, WG, w_in[:, :, 0:1024], [rWA], [rWG])
            WGT = WW[:, 8192:24576].rearrange("p (k n) -> p k n", k=8)
            rWGT = Res("wgt"); rWOUT = Res("wout")
            DMA("pool", P.dsem("wgt"), WGT, w_in[:, :, COL_GATE:COL_GATE + 2048], [], [rWGT])
            for i_ in range(4):
                DMA("pool", P.dsem("cv2_%d" % i_), wf2b_raw[i_ * 1024:(i_ + 1) * 1024, :], w_ff2_raw[i_ * 1024:(i_ + 1) * 1024, :], [], [rCV2[i_]])

            AT = R2[:].rearrange("p (c n) -> p c n", c=4)
            rAT = [Res("at%d" % c) for c in range(4)]
            ST = WW[:, 24576:32768].rearrange("p (c n) -> p c n", c=4)
            rST = [[Res("st%d_%d" % (c, b_)) for b_ in range(4)] for c in range(4)]
            rDIAGall = Res("diagall"); rY32all = Res("y32all")
            DIAG = R1B[:, 0:15872].rearrange("p (j n) -> p j n", j=124)
            rDIAG = [Res("dg%d" % j) for j in range(124)]
            def diag_op(j):
                wcol = VEC[:, V_CW + j:V_CW + j + 1]
                e_ = ("dve", "act")[j % 2]
                if e_ == "act":
                    ACT(DIAG[:, j, :], IDB, AF.Copy, [rCON, rVEC], [rDIAG[j]], scale=wcol)
                else:
                    TS(e_, DIAG[:, j, :], IDB, wcol, None, ALU.mult, None, [rCON, rVEC], [rDIAG[j]])
            diag_next = [0]
            SG = [SC[:, 0:512], SC[:, 512:1024]]
            rSG = [Res("sg0"), Res("sg1")]
            gi = 0
            for b in range(4):
                tb = slice(b * 512, (b + 1) * 512)
                for c in range(4):
                    p1, rp1 = bank()
                    p2, rp2 = bank()
                    for k in range(8):
                        MM(p2[:], WG[:, k, 512 + c * 128:512 + (c + 1) * 128], UT[:, k, tb], k == 0, k == 7, [rWG, rUT[b]], [rp2])
                    for k in range(8):
                        MM(p1[:], WG[:, k, c * 128:(c + 1) * 128], UT[:, k, tb], k == 0, k == 7, [rWG, rUT[b]], [rp1])
                    i = gi % 2
                    gi += 1
                    ACT(SG[i], p2[:], AF.Sigmoid, [rp2], [rSG[i]])
                    TT("dve", AT[:, c, tb], p1[:], SG[i], ALU.mult, [rp1, rSG[i]], [rAT[c]])
                    for _ in range(8):
                        if diag_next[0] < 124:
                            diag_op(diag_next[0])
                            diag_next[0] += 1
            WCO = WW[:, 0:4096].rearrange("p (k n) -> p k n", k=4)
            WGO = WW[:, 4096:8192].rearrange("p (k n) -> p k n", k=4)
            d_wb2 = P.dsem("wb2")
            rWCG = Res("wcg")
            DMA("pool", d_wb2, WCO, w_co, [], [rWCG, rWG])
            DMA("pool", d_wb2, WGO, w_go, [], [rWCG])
            Y32 = [XT[:, 0:2048].rearrange("p (c n) -> p c n", c=4), XT[:, 2048:4096].rearrange("p (c n) -> p c n", c=4)]
            rY32 = [[Res("y32_%d_%d" % (i, c)) for c in range(4)] for i in range(2)]
            YB = SCB[:, 2048:4096].rearrange("p (c n) -> p c n", c=4)
            YSQ = SCB[:, 4096:6144].rearrange("p (c n) -> p c n", c=4)
            M2 = SC[:, 3072:3584]; RS2 = SC[:, 3584:4096]
            NMRb = SC[:, 0:512]
            rYB = Res("yb"); rYSQ = Res("ysq"); rM2 = Res("m2"); rRS2 = Res("rs2"); rNMR = Res("nmr")
            for b in range(4):
                tb = slice(b * 512, (b + 1) * 512)
                yb = Y32[b % 2]; ryb = rY32[b % 2]
                for c in range(4):
                    pcv, rpcv = bank()
                    mms = []
                    for kk in [15] + [k_ for k_ in range(31) if k_ != 15]:
                        sft = kk - 15
                        if c < 2:
                            lo, hi = max(0, -sft), 64 - max(0, sft)
                            o_ = pcv[:].rearrange("p (r w) -> p r w", w=64)[:, :, lo:hi]
                            r_ = AT[:, c, tb].rearrange("p (r w) -> p r w", w=64)[:, :, lo + sft:hi + sft]
                            if simcompat:
                                o_ = pcv[:, 0:8 * (hi - lo)]
                                r_ = AT[:, c, tb][:, 0:8 * (hi - lo)]
                        else:
                            r_lo, r_hi = max(8 * b, -sft), min(8 * b + 8, 32 - sft)
                            if r_lo >= r_hi:
                                continue
                            o_ = pcv[:, (r_lo - 8 * b) * 64:(r_hi - 8 * b) * 64]
                            r_ = AT[:, c, (r_lo + sft) * 64:(r_hi + sft) * 64]
                        mms.append((o_, r_, c * 31 + kk))
                    for i_, (o_, r_, j) in enumerate(mms):
                        last_ = (b == 3 and c == 3 and i_ == len(mms) - 1)
                        MM(o_, DIAG[:, j, :], r_, i_ == 0, i_ == len(mms) - 1, [rDIAG[j], rAT[c]], [rpcv] + ([rDIAGall] if last_ else []))
                    ACT(yb[:, c, :], pcv[:], AF.Identity, [rpcv, rVEC], [ryb[c]], bias=VEC[:, V_CB + c:V_CB + c + 1])
                ACT(YB, yb[:], AF.Copy, ryb, [rYB])
                ACT(YSQ, yb[:], AF.Square, ryb, [rYSQ])
                pmean, rpmean = bank()
                pmsq, rpmsq = bank()
                for c in range(4):
                    MM(pmean[:], ONES512, YB[:, c, :], c == 0, c == 3, [rONES, rYB], [rpmean])
                for c in range(4):
                    MM(pmsq[:], ONES512, YSQ[:, c, :], c == 0, c == 3, [rONES, rYSQ], [rpmsq])
                ACT(M2, pmean[:], AF.Square, [rpmean], [rM2])
                TT("dve", M2, pmsq[:], M2, ALU.subtract, [rpmsq, rM2], [rM2])
                RSTDOP(RS2, M2, RS2, rM2, rRS2, rRS2)
                STT("dve", NMRb, pmean[:], -1.0, RS2, ALU.mult, ALU.mult, [rpmean, rRS2], [rNMR])
                for c in range(4):
                    TT("dve", yb[:, c, :], yb[:, c, :], RS2, ALU.mult, [ryb[c], rRS2], [ryb[c]])
                for c in range(4):
                    TT("dve", yb[:, c, :], yb[:, c, :], NMRb, ALU.add, [ryb[c], rNMR], [ryb[c]])
                for c in range(4):
                    ACT(ST[:, c, tb], yb[:, c, :], AF.Silu, [ryb[c], rVEC], [rST[c][b]] + ([rY32all] if (b == 3 and c == 3) else []),
                        scale=VEC[:, V_LNG + c:V_LNG + c + 1], bias=VEC[:, V_LNB + c:V_LNB + c + 1])
            DMA("pool", P.dsem("wout"), R2[:].rearrange("p (k n) -> p k n", k=8), w_out, [], [rWOUT] + rAT)
            dump("st", WW[:, 24576:32768], [128, 8192], [r_ for l_ in rST for r_ in l_])

            if upto == 3:
                return
            WOUT = R2[:].rearrange("p (k n) -> p k n", k=8)
            OG = SR
            MG = [R1B[:, 0:4096].rearrange("p (k n) -> p k n", k=8), R1B[:, 4096:8192].rearrange("p (k n) -> p k n", k=8)]
            rMG = [Res("mg0"), Res("mg1")]
            S1 = [SC[:, 0:512], SC[:, 512:1024]]; S2 = [SC[:, 1024:1536], SC[:, 1536:2048]]
            M1 = [SC[:, 2048:2560], SC[:, 2560:3072]]; M2_ = [SC[:, 3072:3584], SC[:, 3584:4096]]
            rS1 = [Res(), Res()]; rS2 = [Res(), Res()]; rM1 = [Res(), Res()]; rM2b = [Res(), Res()]
            XTt = XT[:].rearrange("p (s n) -> p s n", s=4)
            rSTP = Res("stp")
            rXTt = [Res("xt%d" % i) for i in range(4)]
            d_xt = [P.dsem("xt%d" % i) for i in range(4)]
            d_hm = [P.dsem("hm%d" % i) for i in range(4)]
            rHM = [Res("hmid%d" % i) for i in range(16)]
            JUNK2 = R1B[:, 8192:9216]
            TMPY = R1[:, 6144:8192].rearrange("p (s n) -> p s n", s=2)
            rTMPY = [Res("tmpy0"), Res("tmpy1")]
            fi_ = [0]
            rst2s = [Res("stat2_%d" % b) for b in range(4)]
            rWF2 = [Res("wf2_%d" % q) for q in range(4)]
            rJ2 = Res("junk2")
            rFS = [Res("fs%d" % i) for i in range(4)]
            d_fs = [P.dsem("fs%d" % i) for i in range(4)]

            def merge(b):
                tb = slice(b * 512, (b + 1) * 512)
                mg = MG[b % 2]; rmg = rMG[b % 2]
                for fc in range(8):
                    fs = slice(fc * 128, (fc + 1) * 128)
                    pg1, rpg1 = bank(); pg2, rpg2 = bank(); pc, rpc = bank(); pgl, rpgl = bank()
                    for k in range(8):
                        MM(pg1[:], WGT[:, k, fs], UT[:, k, tb], k == 0, k == 7, [rWGT, rUT[b]], [rpg1])
                    for k in range(8):
                        MM(pg2[:], WGT[:, k, 1024 + fc * 128:1024 + (fc + 1) * 128], UT[:, k, tb], k == 0, k == 7, [rWGT, rUT[b]], [rpg2])
                    for k in range(4):
                        MM(pc[:], WCO[:, k, fs], ST[:, k, tb], k == 0, k == 3, [rWCG, rST[k][b]], [rpc])
                    for k in range(4):
                        MM(pgl[:], WGO[:, k, fs], OG[:, k, tb], k == 0, k == 3, [rWCG, rSR[b]], [rpgl])
                    i = fi_[0] % 2
                    fi_[0] += 1
                    fu = fi_[0] <= 2
                    ACT(S1[i], pg1[:], AF.Sigmoid, [rpg1], [rS1[i]] + ([rNMR, rSG[0], rSG[1]] if fu else []))
                    ACT(S2[i], pg2[:], AF.Sigmoid, [rpg2], [rS2[i]] + ([rYB] if fu else []))
                    TT("dve", M1[i], pc[:], S1[i], ALU.mult, [rpc, rS1[i]], [rM1[i]] + ([rYSQ] if fu else []))
                    TT("dve", M2_[i], pgl[:], S2[i], ALU.mult, [rpgl, rS2[i]], [rM2b[i]] + ([rM2, rRS2] if fu else []))
                    TT("pool", mg[:, fc, :], M1[i], M2_[i], ALU.add, [rM1[i], rM2b[i], rDIAGall], [rmg])

            def wout(b):
                mg = MG[b % 2]; rmg = rMG[b % 2]
                rst2 = rst2s[b]
                for t in range(4):
                    tile = b * 4 + t
                    tk = slice(t * 128, (t + 1) * 128)
                    xs = XTt[:, t, :]
                    DMA("sp", d_xt[t], xs, x_d[tile * 128:(tile + 1) * 128, :], [rY32all], [rXTt[t]])
                    py = []
                    for half in range(2):
                        p_, rp_ = bank()
                        for k in range(8):
                            MM(p_[:], mg[:, k, tk], WOUT[:, k, half * 512:(half + 1) * 512], k == 0, k == 7, [rmg, rWOUT], [rp_])
                        py.append((p_, rp_))
                    rs_ = rSTP
                    for half in range(2):
                        ACT(JUNK2[:, 0:512], py[half][0][:], AF.Square, [py[half][1], rDIAGall], [rs_, rJ2], scale=1.0 / 32,
                            accum_out=TMP8[:, 8 + half:9 + half])
                    TT("dve", TMP8[:, 10:11], TMP8[:, 8:9], TMP8[:, 9:10], ALU.add, [rs_], [rs_])
                    RSTDOP(TMP8[:, 12:13], TMP8[:, 10:11], TMP8[:, 11:12], rs_, rs_, rs_)
                    ty = TMPY[:, t % 2, :]; rty = rTMPY[t % 2]
                    for half in range(2):
                        hs = slice(half * 512, (half + 1) * 512)
                        STT("dve", ty[:, hs], py[half][0][:], TMP8[:, 12:13], GG[:, hs], ALU.mult, ALU.mult,
                            [py[half][1], rs_, rGG[0], rDIAGall], [rty])
                    TT("dve", xs, xs, ty, ALU.add, [rXTt[t], rty], [rXTt[t]])
                    DMA("sp", d_hm[t], hmid_d[tile * 128:(tile + 1) * 128, :], xs, [rXTt[t]], [rHM[tile]])
                    ACT(JUNK2, xs, AF.Square, [rXTt[t]], [rst2, rJ2], scale=1.0 / 32, accum_out=TMP8[:, 16 + t:17 + t])

            def norm2T(b):
                tb = slice(b * 512, (b + 1) * 512)
                rst2 = rst2s[b]
                RSTDOP(TMP8[:, 24:28], TMP8[:, 16:20], TMP8[:, 20:24], rst2, rst2, rst2)
                for t in range(4):
                    xs = XTt[:, t, :]
                    TS("dve", xs, xs, TMP8[:, 24 + t:25 + t], None, ALU.mult, None, [rst2, rXTt[t]], [rXTt[t]])
                for c in range(8):
                    pb, rpb = bank()
                    for t in range(4):
                        TR(pb[:, t * 128:(t + 1) * 128], XTt[:, t, c * 128:(c + 1) * 128], IDF[:], [rXTt[t], rIDF], [rpb])
                    ACT(UT[:, c, tb], pb[:], AF.Identity, [rpb, rSM, rMOD2], [rUT[b]],
                        scale=GS[:, 16 + c:17 + c], bias=MOD[:, 24 + c:25 + c])

            WF2 = WW[:].rearrange("p (k n) -> p k n", k=32)
            R3f = R3[:].rearrange("p c n -> p (c n)")
            FS = [R3f[:, 0:4096].rearrange("p (k n) -> p k n", k=8), R3f[:, 4096:8192].rearrange("p (k n) -> p k n", k=8),
                  R2[:, 0:4096].rearrange("p (k n) -> p k n", k=8), R2[:, 4096:8192].rearrange("p (k n) -> p k n", k=8)]
            fs_old = [list(rSR), list(rSR), [rWOUT], [rWOUT]]
            HID = R1B.rearrange("p (k n) -> p k n", k=32)
            rHID = Res("hid")
            RL = [SC[:, 0:512], SC[:, 512:1024]]
            rRL = [Res(), Res()]
            JUNK3 = SCB[:, 2048:2560]
            TY2 = SC[:, 2048:4096].rearrange("p (s n) -> p s n", s=2)
            rTY2 = [Res(), Res()]
            d_hl = [P.dsem("hl%d" % i) for i in range(4)]
            d_o = [P.dsem("o%d" % i) for i in range(4)]
            cnt = {"ri": 0, "si": 0}
            ff1_done = set()

            def ff1_slab(b, s_):
                tb = slice(b * 512, (b + 1) * 512)
                f = cnt["si"] % 4
                cnt["si"] += 1
                ff1_done.add((b, s_))
                if not (b == 0 and s_ < 2):
                    DMA("pool", d_fs[f], FS[f], w_ff1[:, :, s_ * 512:(s_ + 1) * 512], rCV1, [rFS[f]] + fs_old[f])
                for oc in range(4):
                    ph, rph = bank()
                    for k in range(8):
                        MM(ph[:], FS[f][:, k, oc * 128:(oc + 1) * 128], UT[:, k, tb], k == 0, k == 7, [rFS[f], rUT[b]], [rph])
                    i = cnt["ri"] % 2
                    cnt["ri"] += 1
                    ACT(RL[i], ph[:], AF.Relu, [rph], [rRL[i]] + rS1)
                    STT("dve", HID[:, s_ * 4 + oc, :], ph[:], 0.0, RL[i], ALU.max, ALU.mult, [rph, rRL[i]],
                        [rHID] + (rMG + rTMPY + [rJ2] if b == 0 else []))

            def ff2_block(b):
                for t in range(4):
                    tile = b * 4 + t
                    tk = slice(t * 128, (t + 1) * 128)
                    hs_ = XTt[:, t, :]
                    DMA("sp", d_hl[t], hs_, hmid_d[tile * 128:(tile + 1) * 128, :], [rHM[tile]], [rXTt[t]])
                    py = []
                    for half in range(2):
                        p_, rp_ = bank()
                        for k in range(32):
                            MM(p_[:], HID[:, k, tk], WF2[:, k, half * 512:(half + 1) * 512], k == 0, k == 31,
                               [rHID, rWF2[k // 8]], [rp_])
                        py.append((p_, rp_))
                    rs_ = rSTP
                    for half in range(2):
                        ACT(JUNK3, py[half][0][:], AF.Square, [py[half][1]], [rs_], scale=1.0 / 32,
                            accum_out=TMP8[:, 8 + half:9 + half])
                    TT("dve", TMP8[:, 10:11], TMP8[:, 8:9], TMP8[:, 9:10], ALU.add, [rs_], [rs_])
                    RSTDOP(TMP8[:, 12:13], TMP8[:, 10:11], TMP8[:, 11:12], rs_, rs_, rs_)
                    ty = TY2[:, t % 2, :]; rty = rTY2[t % 2]
                    for half in range(2):
                        hs = slice(half * 512, (half + 1) * 512)
                        STT("dve", ty[:, hs], py[half][0][:], TMP8[:, 12:13], GG[:, 1024 + half * 512:1024 + (half + 1) * 512],
                            ALU.mult, ALU.mult, [py[half][1], rs_, rGG[1]], [rty] + (rM1 + rM2b if b == 0 else []))
                    TT("dve", hs_, hs_, ty, ALU.add, [rXTt[t], rty], [rXTt[t]])
                    OUT_EVS.append(DMA("sp", d_o[t], out_d[tile * 128:(tile + 1) * 128, :], hs_, [rXTt[t]], [Res()]))

            merge(0)
            wout(0)
            for b in range(1, 4):
                merge(b)
                norm2T(b - 1)
                if b == 3:
                    WF2 = WW[:].rearrange("p (k n) -> p k n", k=32)
                    old = [[rWCG], [rWGT], [rWGT], [r_ for l_ in rST for r_ in l_]]
                    R3f_ = R3[:].rearrange("p c n -> p (c n)")
                    for f_ in range(2):
                        DMA("pool", d_fs[f_], R3f_[:, f_ * 4096:(f_ + 1) * 4096].rearrange("p (k n) -> p k n", k=8),
                            w_ff1[:, :, f_ * 512:(f_ + 1) * 512], rCV1, [rFS[f_]] + list(rSR))
                    for q in range(4):
                        DMA("pool", P.dsem("wf2_%d" % q), WF2[:, q * 8:(q + 1) * 8, :], w_ff2[:, q * 8:(q + 1) * 8, :], [rCV2[q]],
                            [rWF2[q]] + old[q])
                wout(b)
            ff1_slab(0, 0)
            ff1_slab(0, 1)
            norm2T(3)
            dump("u2", UT[:], [128, 8, 2048], rUT)

            if upto == 4:
                return
            for b in range(4):
                for s_ in range(8):
                    if (b, s_) not in ff1_done:
                        ff1_slab(b, s_)
                ff2_block(b)

        body()
        allev = list(OUT_EVS) + [ds.last for ds in dbg_out.values()]
        P.wait_all("sp", allev)
        P.emit(st)
        build.stats = P.stats
    return nc


def _consts():
    ident = np.eye(128, dtype=np.float32)
    j = np.arange(128)[:, None]
    i = np.arange(128)[None, :]
    mu = (j <= i).astype(np.float32)
    ml = (j >= i).astype(np.float32)
    return np.ascontiguousarray(np.concatenate([ident, mu, mu, ml, ml, np.tile(ml, (1, 4)), mu, ml], axis=1))


def _pack(inputs, b):
    f = np.float32
    c = np.asarray(inputs["c"], f)[b]
    cc = np.asarray(inputs["c_ctx"], f)
    vecs = np.zeros((128, NVEC), f)
    cv = np.stack([c.reshape(8, 128), cc.reshape(8, 128)], axis=-1)
    vecs[:, V_C:V_C + 16] = cv.transpose(1, 0, 2).reshape(128, 16)
    vecs[:, V_BMOD:V_BMOD + 48] = np.asarray(inputs["b_mod"], f)[0].reshape(48, 128).T
    vecs[:, V_GPRE1:V_GPRE1 + 8] = np.asarray(inputs["g_pre1"], f)[0].reshape(8, 128).T
    vecs[:, V_GPRE2:V_GPRE2 + 8] = np.asarray(inputs["g_pre2"], f)[0].reshape(8, 128).T
    cw = np.asarray(inputs["conv_w"], f)[0]
    vecs[:, V_CW:V_CW + 124] = cw.reshape(31, 4, 128).transpose(2, 1, 0).reshape(128, 124)
    vecs[:, V_CB:V_CB + 4] = np.asarray(inputs["conv_b"], f)[0].reshape(4, 128).T
    vecs[:, V_LNG:V_LNG + 4] = np.asarray(inputs["conv_ln_g"], f)[0].reshape(4, 128).T
    vecs[:, V_LNB:V_LNB + 4] = np.asarray(inputs["conv_ln_b"], f)[0].reshape(4, 128).T
    vecs[:, V_GN] = np.asarray(inputs["gla_norm_g"], f)[0]
    return vecs


def _shared(inputs):
    f = np.float32
    gpost = np.concatenate([np.asarray(inputs["g_post1"], f)[0], np.asarray(inputs["g_post2"], f)[0]])
    gpost = np.ascontiguousarray(np.broadcast_to(gpost[None, :], (128, 2048)))
    wd = np.asarray(inputs["w_decay"], f)[0]
    bd = np.asarray(inputs["b_decay"], f)[0]
    wda = np.zeros((33, 512), f)
    wda[0:16, 0:256] = wd[0]
    wda[16:32, 256:512] = wd[1]
    wda[32, 0:256] = bd[0]
    wda[32, 256:512] = bd[1]
    sh = {"gpost": gpost, "wda": wda, "consts": _consts()}
    for k in ("w_mod", "w_in", "w_conv_out", "w_gla_out", "w_out", "w_ff1", "w_ff2"):
        sh[k] = np.ascontiguousarray(np.asarray(inputs[k], f)[0])
    return sh


_NC_CACHE = {}


def kernel(**inputs):
    if "nc" not in _NC_CACHE:
        _NC_CACHE["nc"] = build()
    nc = _NC_CACHE["nc"]
    sh = _shared(inputs)
    x = np.asarray(inputs["x"], np.float32)
    ctx = np.asarray(inputs["ctx"], np.float32)
    in_maps = []
    for b in range(8):
        m = dict(sh)
        m["x"] = np.ascontiguousarray(x[b])
        m["ctx"] = np.ascontiguousarray(ctx[b])
        m["vecs"] = _pack(inputs, b)
        in_maps.append(m)
    res = run_bass_kernel_spmd(nc, in_maps, core_ids=list(range(8)))
    return np.stack([np.asarray(r["out"], np.float32) for r in res.results], axis=0)
```
